# Optimizing a Trainium2 kernel written in Bass

```python
import math
import jax, jax.numpy as jnp
from jax import lax
import numpy as np

D_MODEL = 2048
BATCH = 4
SEQ = 2048
DEPTH = 1
DEC_BATCH = 128
DEC_SEQ = 4
PAST_LEN = 16384
PAGE_SIZE = 128

MIX_WIDTH = D_MODEL
S5_WIDTH = MIX_WIDTH // 2
S5_GROUP = 16
S5_GROUPS = S5_WIDTH // S5_GROUP
S5_STATE = 64
GDN_WIDTH = MIX_WIDTH - S5_WIDTH
GDN_HEAD_DIM = 128
GDN_HEADS = GDN_WIDTH // GDN_HEAD_DIM
GDN_CONV = 4
GDN_CHUNK = 64
D_FF = ((8 * D_MODEL // 3 + 255) // 256) * 256
OFF_QKV = S5_WIDTH
OFF_GATE = OFF_QKV + 3 * GDN_WIDTH
OFF_BETA = OFF_GATE + GDN_WIDTH
OFF_DECAY = OFF_BETA + GDN_HEADS
MIX_IN_COLS = OFF_DECAY + GDN_HEADS
DEEP_ALPHA = (2.0 * DEPTH) ** 0.25
DEEP_BETA = (8.0 * DEPTH) ** -0.25
LN_EPS = 1e-5
NORM_EPS = 1e-6
F32 = jnp.float32

kernel_name = 'hymba_s5_gdn_macaron_deepnorm_step'


def layer_norm(x, g, b):
    x = x.astype(F32)
    mu = jnp.mean(x, -1, keepdims=True)
    var = jnp.mean(jnp.square(x - mu), -1, keepdims=True)
    return (x - mu) * lax.rsqrt(var + LN_EPS) * g.astype(F32) + b.astype(F32)


def swiglu(x, w_in, w_out):
    gate, up = jnp.split(x @ w_in, 2, axis=-1)
    return (jax.nn.silu(gate) * up) @ w_out


def s5_discretize(lam_re, lam_im, log_dt, b_re, b_im):
    dt = jnp.exp(log_dt.astype(F32))[:, None]
    lr, li = lam_re.astype(F32), lam_im.astype(F32)
    mag = jnp.exp(lr * dt)
    ab_re = mag * jnp.cos(li * dt)
    ab_im = mag * jnp.sin(li * dt)
    nr, ni = ab_re - 1.0, ab_im
    den = lr * lr + li * li
    c_re = (nr * lr + ni * li) / den
    c_im = (ni * lr - nr * li) / den
    b_re, b_im = b_re.astype(F32), b_im.astype(F32)
    bb_re = c_re[..., None] * b_re - c_im[..., None] * b_im
    bb_im = c_re[..., None] * b_im + c_im[..., None] * b_re
    return ab_re, ab_im, bb_re, bb_im


def s5_combine(e1, e2):
    a1r, a1i, b1r, b1i = e1
    a2r, a2i, b2r, b2i = e2
    return (a2r * a1r - a2i * a1i, a2r * a1i + a2i * a1r,
            a2r * b1r - a2i * b1i + b2r, a2r * b1i + a2i * b1r + b2i)


def s5_mixer(u, h0_re, h0_im, w):
    bn, t, _ = u.shape
    ug = u.reshape(bn, t, S5_GROUPS, S5_GROUP)
    ab_re, ab_im, bb_re, bb_im = s5_discretize(w['s5_lambda_re'], w['s5_lambda_im'], w['s5_log_dt'], w['s5_b_re'], w['s5_b_im'])
    bu_re = jnp.einsum('btgh,gph->btgp', ug, bb_re)
    bu_im = jnp.einsum('btgh,gph->btgp', ug, bb_im)
    h0_re, h0_im = h0_re.astype(F32), h0_im.astype(F32)
    bu_re = bu_re.at[:, 0].add(ab_re * h0_re - ab_im * h0_im)
    bu_im = bu_im.at[:, 0].add(ab_re * h0_im + ab_im * h0_re)
    a_re = jnp.broadcast_to(ab_re, bu_re.shape)
    a_im = jnp.broadcast_to(ab_im, bu_im.shape)
    _, _, h_re, h_im = lax.associative_scan(s5_combine, (a_re, a_im, bu_re, bu_im), axis=1)
    d = w['s5_d'].astype(F32).reshape(S5_GROUPS, S5_GROUP)
    y = (jnp.einsum('btgp,ghp->btgh', h_re, w['s5_c_re'].astype(F32))
         - jnp.einsum('btgp,ghp->btgh', h_im, w['s5_c_im'].astype(F32)) + d * ug)
    z = jax.nn.gelu(y.reshape(bn, t, S5_WIDTH))
    out = z * jax.nn.sigmoid(z @ w['s5_glu_w'] + w['s5_glu_b'])
    return out, h_re[:, -1], h_im[:, -1]


def causal_short_conv(x, buf, cw):
    t = x.shape[1]
    xp = jnp.concatenate([buf.astype(F32), x], axis=1)
    out = xp[:, 0:t] * cw[0]
    for j in range(1, GDN_CONV):
        out = out + xp[:, j:j + t] * cw[j]
    return out, xp[:, -(GDN_CONV - 1):]


def l2norm(x):
    return x * lax.rsqrt(jnp.sum(x * x, -1, keepdims=True) + NORM_EPS)


def gated_delta_chunked(q, k, v, g, beta, s0):
    bn, t, nh, dk = q.shape
    c = min(GDN_CHUNK, t)
    n = -(-t // c)
    pad = n * c - t

    def blocks(a):
        a = jnp.moveaxis(a, 2, 1)
        if pad:
            a = jnp.pad(a, [(0, 0), (0, 0), (0, pad)] + [(0, 0)] * (a.ndim - 3))
        return a.reshape(a.shape[:2] + (n, c) + a.shape[3:])

    q, k, v, g, beta = (blocks(a) for a in (q, k, v, g, beta))
    gc = jnp.cumsum(g, axis=-1)
    pos = jnp.arange(c)
    causal = pos[:, None] >= pos[None, :]
    strict = (pos[:, None] > pos[None, :]).astype(F32)
    decay = jnp.exp(jnp.where(causal, gc[..., :, None] - gc[..., None, :], -jnp.inf))
    kb = k * beta[..., None]
    m = jnp.einsum('bhnid,bhnjd->bhnij', kb, k) * decay * strict
    eye = jnp.eye(c, dtype=F32)
    tinv = lax.linalg.triangular_solve(eye + m, jnp.broadcast_to(eye, m.shape), left_side=True, lower=True)
    u = tinv @ (v * beta[..., None])
    wk = tinv @ (kb * jnp.exp(gc)[..., None])
    attn = jnp.einsum('bhnid,bhnjd->bhnij', q, k) * decay
    qg = q * jnp.exp(gc)[..., None]
    kd = k * jnp.exp(gc[..., -1:] - gc)[..., None]
    glast = jnp.exp(gc[..., -1])

    def step(s, xs):
        u_i, w_i, qg_i, kd_i, a_i, gl_i = xs
        v_new = u_i - w_i @ s
        o_i = qg_i @ s + a_i @ v_new
        s = s * gl_i[..., None, None] + jnp.einsum('bhcd,bhce->bhde', kd_i, v_new)
        return s, o_i

    xs = tuple(jnp.moveaxis(a, 2, 0) for a in (u, wk, qg, kd, attn, glast))
    s_fin, o = lax.scan(step, s0.astype(F32), xs)
    o = jnp.moveaxis(o, 0, 2).reshape(bn, nh, n * c, -1)[:, :, :t]
    return jnp.moveaxis(o, 1, 2), s_fin


def gdn_mixer(qkv, gate, b_raw, a_raw, s0, conv_buf, w):
    bn, t, _ = qkv.shape
    qkv_c, new_buf = causal_short_conv(qkv, conv_buf, w['gdn_conv_w'].astype(F32))
    q, k, v = jnp.split(jax.nn.silu(qkv_c), 3, axis=-1)
    shp = (bn, t, GDN_HEADS, GDN_HEAD_DIM)
    q = l2norm(q.reshape(shp)) * (GDN_HEAD_DIM ** -0.5)
    k = l2norm(k.reshape(shp))
    v = v.reshape(shp)
    beta = jax.nn.sigmoid(b_raw)
    g = -jnp.exp(w['gdn_a_log'].astype(F32)) * jax.nn.softplus(a_raw + w['gdn_dt_bias'].astype(F32))
    o, s_fin = gated_delta_chunked(q, k, v, g, beta, s0)
    o = o * lax.rsqrt(jnp.mean(o * o, -1, keepdims=True) + NORM_EPS) * w['gdn_norm_w'].astype(F32)
    o = o * jax.nn.silu(gate.reshape(shp))
    return o.reshape(bn, t, GDN_WIDTH), s_fin, new_buf


def decoder_layer(x, s5_re, s5_im, gdn_s, conv_buf, w):
    x = layer_norm(DEEP_ALPHA * x + 0.5 * swiglu(x, w['ffn1_w_in'], w['ffn1_w_out']), w['ln1_g'], w['ln1_b'])
    proj = x @ w['w_mix_in']
    y_s5, n_re, n_im = s5_mixer(proj[..., :S5_WIDTH], s5_re, s5_im, w)
    y_gdn, n_s, n_buf = gdn_mixer(proj[..., OFF_QKV:OFF_GATE], proj[..., OFF_GATE:OFF_BETA],
                                  proj[..., OFF_BETA:OFF_DECAY], proj[..., OFF_DECAY:MIX_IN_COLS],
                                  gdn_s, conv_buf, w)
    mix = jnp.concatenate([y_s5, y_gdn], axis=-1) @ w['w_mix_out']
    x = layer_norm(DEEP_ALPHA * x + mix, w['ln2_g'], w['ln2_b'])
    x = layer_norm(DEEP_ALPHA * x + 0.5 * swiglu(x, w['ffn2_w_in'], w['ffn2_w_out']), w['ln3_g'], w['ln3_b'])
    return x, n_re, n_im, n_s, n_buf


def setup_inputs(seed: int = 0) -> dict:
    key = jax.random.key(seed)
    ks = iter(jax.random.split(key, 40))
    L = DEPTH
    nrm = lambda shape, s: jax.random.normal(next(ks), shape, F32) * s
    gain = lambda shape: 1.0 + nrm(shape, 0.02)
    lam_im = jnp.pi * jnp.arange(S5_STATE, dtype=F32)
    dt = jnp.exp(jax.random.uniform(next(ks), (L, GDN_HEADS), F32, math.log(1e-3), math.log(1e-1)))
    return {
        'x_prompt': nrm((BATCH, SEQ, D_MODEL), 1.0),
        'x_sample': nrm((DEC_BATCH, DEC_SEQ, D_MODEL), 1.0),
        'state_s5_re': nrm((L, DEC_BATCH, S5_GROUPS, S5_STATE), 0.1),
        'state_s5_im': nrm((L, DEC_BATCH, S5_GROUPS, S5_STATE), 0.1),
        'state_gdn': nrm((L, DEC_BATCH, GDN_HEADS, GDN_HEAD_DIM, GDN_HEAD_DIM), 0.3),
        'state_conv': nrm((L, DEC_BATCH, GDN_CONV - 1, 3 * GDN_WIDTH), 1.0),
        'ln1_g': gain((L, D_MODEL)),
        'ln1_b': nrm((L, D_MODEL), 0.02),
        'ffn1_w_in': nrm((L, D_MODEL, 2 * D_FF), D_MODEL ** -0.5),
        'ffn1_w_out': nrm((L, D_FF, D_MODEL), D_FF ** -0.5 * DEEP_BETA),
        'w_mix_in': nrm((L, D_MODEL, MIX_IN_COLS), D_MODEL ** -0.5),
        's5_lambda_re': -0.5 + nrm((L, S5_GROUPS, S5_STATE), 0.01),
        's5_lambda_im': lam_im + nrm((L, S5_GROUPS, S5_STATE), 0.01),
        's5_log_dt': jax.random.uniform(next(ks), (L, S5_GROUPS), F32, math.log(1e-3), math.log(1e-1)),
        's5_b_re': nrm((L, S5_GROUPS, S5_STATE, S5_GROUP), (2 * S5_GROUP) ** -0.5),
        's5_b_im': nrm((L, S5_GROUPS, S5_STATE, S5_GROUP), (2 * S5_GROUP) ** -0.5),
        's5_c_re': nrm((L, S5_GROUPS, S5_GROUP, S5_STATE), (2 * S5_STATE) ** -0.5),
        's5_c_im': nrm((L, S5_GROUPS, S5_GROUP, S5_STATE), (2 * S5_STATE) ** -0.5),
        's5_d': nrm((L, S5_WIDTH), 1.0),
        's5_glu_w': nrm((L, S5_WIDTH, S5_WIDTH), S5_WIDTH ** -0.5),
        's5_glu_b': nrm((L, S5_WIDTH), 0.02),
        'gdn_conv_w': nrm((L, GDN_CONV, 3 * GDN_WIDTH), GDN_CONV ** -0.5),
        'gdn_a_log': jnp.log(jax.random.uniform(next(ks), (L, GDN_HEADS), F32, 1.0, 16.0)),
        'gdn_dt_bias': dt + jnp.log(-jnp.expm1(-dt)),
        'gdn_norm_w': gain((L, GDN_HEAD_DIM)),
        'w_mix_out': nrm((L, MIX_WIDTH, D_MODEL), MIX_WIDTH ** -0.5 * DEEP_BETA),
        'ln2_g': gain((L, D_MODEL)),
        'ln2_b': nrm((L, D_MODEL), 0.02),
        'ffn2_w_in': nrm((L, D_MODEL, 2 * D_FF), D_MODEL ** -0.5),
        'ffn2_w_out': nrm((L, D_FF, D_MODEL), D_FF ** -0.5 * DEEP_BETA),
        'ln3_g': gain((L, D_MODEL)),
        'ln3_b': nrm((L, D_MODEL), 0.02),
    }


def reference(x_prompt, x_sample, state_s5_re, state_s5_im, state_gdn, state_conv,
              ln1_g, ln1_b, ffn1_w_in, ffn1_w_out, w_mix_in,
              s5_lambda_re, s5_lambda_im, s5_log_dt, s5_b_re, s5_b_im, s5_c_re, s5_c_im,
              s5_d, s5_glu_w, s5_glu_b, gdn_conv_w, gdn_a_log, gdn_dt_bias, gdn_norm_w,
              w_mix_out, ln2_g, ln2_b, ffn2_w_in, ffn2_w_out, ln3_g, ln3_b):
    bp = x_prompt.shape[0]
    yp = x_prompt.astype(F32)
    ys = x_sample.astype(F32)
    p_re, p_im, p_s, p_buf = [], [], [], []
    s_re, s_im, s_s, s_buf = [], [], [], []
    for l in range(DEPTH):
        w = {
            'ln1_g': ln1_g[l], 'ln1_b': ln1_b[l], 'ffn1_w_in': ffn1_w_in[l], 'ffn1_w_out': ffn1_w_out[l],
            'w_mix_in': w_mix_in[l], 's5_lambda_re': s5_lambda_re[l], 's5_lambda_im': s5_lambda_im[l],
            's5_log_dt': s5_log_dt[l], 's5_b_re': s5_b_re[l], 's5_b_im': s5_b_im[l],
            's5_c_re': s5_c_re[l], 's5_c_im': s5_c_im[l], 's5_d': s5_d[l],
            's5_glu_w': s5_glu_w[l], 's5_glu_b': s5_glu_b[l], 'gdn_conv_w': gdn_conv_w[l],
            'gdn_a_log': gdn_a_log[l], 'gdn_dt_bias': gdn_dt_bias[l], 'gdn_norm_w': gdn_norm_w[l],
            'w_mix_out': w_mix_out[l], 'ln2_g': ln2_g[l], 'ln2_b': ln2_b[l],
            'ffn2_w_in': ffn2_w_in[l], 'ffn2_w_out': ffn2_w_out[l], 'ln3_g': ln3_g[l], 'ln3_b': ln3_b[l],
        }
        z_ssm = jnp.zeros((bp, S5_GROUPS, S5_STATE), F32)
        z_gdn = jnp.zeros((bp, GDN_HEADS, GDN_HEAD_DIM, GDN_HEAD_DIM), F32)
        z_buf = jnp.zeros((bp, GDN_CONV - 1, 3 * GDN_WIDTH), F32)
        yp, a, b, c, d = decoder_layer(yp, z_ssm, z_ssm, z_gdn, z_buf, w)
        p_re.append(a); p_im.append(b); p_s.append(c); p_buf.append(d)
        ys, a, b, c, d = decoder_layer(ys, state_s5_re[l], state_s5_im[l], state_gdn[l], state_conv[l], w)
        s_re.append(a); s_im.append(b); s_s.append(c); s_buf.append(d)
    return (yp.astype(x_prompt.dtype), ys.astype(x_sample.dtype),
            jnp.stack(p_re), jnp.stack(p_im), jnp.stack(p_s), jnp.stack(p_buf),
            jnp.stack(s_re), jnp.stack(s_im), jnp.stack(s_s), jnp.stack(s_buf))
```

```python
import contextlib
import numpy as np
import concourse.bass as bass
import concourse.mybir as mybir
from concourse.bass_utils import run_bass_kernel_spmd

F32 = mybir.dt.float32
BF16 = mybir.dt.bfloat16
AF = mybir.ActivationFunctionType
ALU = mybir.AluOpType
AX = mybir.AxisListType

ENGS = ("pe", "act", "dve", "pool", "sp")


class Tok:
    __slots__ = ("name", "w", "r", "excl")

    def __init__(self, name="", excl=False):
        self.name = name
        self.excl = excl
        self.w = []
        self.r = []


class Op:
    __slots__ = ("eng", "fn", "deps", "dma", "semkey", "sig", "sigval", "inc")

    def __init__(self, eng, fn, dma, inc=16):
        self.eng = eng
        self.fn = fn
        self.deps = []
        self.dma = dma
        self.inc = inc
        self.semkey = None
        self.sig = bool(dma)
        self.sigval = 0


class Prog:
    def __init__(self, nc):
        self.nc = nc
        self.ops = {e: [] for e in ENGS}
        self.all = []
        self.final = []
        self.toks = {}

    def barrier(self, scratch_ap):
        tb = Tok("barrier")
        self.op("pool", lambda e: e.memset(scratch_ap, 0.0), writes=list(self.toks.values()) + [tb])
        for eng in ("pe", "act", "dve", "sp"):
            self.op(eng, None, reads=[tb])

    def op(self, eng, fn, reads=(), writes=(), dma=False, semkey=None, nowaw=False, cc=False):
        if cc:
            dma = True
        o = Op(eng, fn, dma, 1 if cc else 16)
        if any(t.excl for t in reads):
            writes = list(writes) + [t for t in reads if t.excl]
            reads = [t for t in reads if not t.excl]
        deps = []
        for t in list(reads) + list(writes):
            self.toks[id(t)] = t
        for t in reads:
            deps.extend(t.w)
        if not nowaw:
            for t in writes:
                deps.extend(t.w)
                deps.extend(t.r)
        seen = set()
        for d in deps:
            if d is o or id(d) in seen:
                continue
            if d.eng == "pe" and eng == "pe" and not d.dma and not dma:
                continue
            seen.add(id(d))
            o.deps.append(d)
            d.sig = True
        for t in reads:
            if fn is None:
                break
            if not dma:
                t.r = [q for q in t.r if q.dma or q.eng != eng]
            t.r.append(o)
        for t in writes:
            if nowaw:
                t.w = [q for q in t.w if q.dma or q.eng != eng or dma] + [o]
            else:
                t.w = [o]
                t.r = []
        if dma:
            o.semkey = semkey if semkey is not None else (writes[0] if writes else reads[0])
        self.ops[eng].append(o)
        self.all.append(o)
        return o

    def emit(self):
        nc = self.nc
        with contextlib.ExitStack() as es:
            esem = {e: es.enter_context(nc.semaphore("s_" + e)) for e in ENGS}
            dsem, dcnt = {}, {}
            ecnt = {e: 0 for e in ENGS}
            for o in self.final:
                o.sig = True
            for o in self.all:
                if not o.sig:
                    continue
                if o.dma:
                    k = id(o.semkey)
                    if k not in dsem:
                        dsem[k] = es.enter_context(nc.semaphore("d%d" % len(dsem)))
                        dcnt[k] = 0
                    dcnt[k] += o.inc
                    o.sigval = dcnt[k]
                else:
                    ecnt[o.eng] += 1
                    o.sigval = ecnt[o.eng]
            self.nsem = len(dsem) + len(ENGS)
            self.ecnt = ecnt

            def semof(o):
                return dsem[id(o.semkey)] if o.dma else esem[o.eng]

            block = es.enter_context(nc.Block())

            def run(engname, eng):
                known = {}
                for o in self.ops[engname]:
                    need = {}
                    for d in o.deps:
                        s = semof(d)
                        if d.sigval > need.get(id(s), (0, None))[0]:
                            need[id(s)] = (d.sigval, s)
                    for sid, (val, s) in need.items():
                        if known.get(sid, 0) >= val:
                            continue
                        known[sid] = val
                        eng.wait_ge(s, val)
                    if o.fn is None:
                        continue
                    ins = o.fn(eng)
                    if o.sig:
                        ins.then_inc(semof(o), o.inc if o.dma else 1)
                if engname == "sp":
                    need = {}
                    for o in self.final:
                        s = semof(o)
                        if o.sigval > need.get(id(s), (0, None))[0]:
                            need[id(s)] = (o.sigval, s)
                    for sid, (val, s) in need.items():
                        if known.get(sid, 0) >= val:
                            continue
                        known[sid] = val
                        eng.wait_ge(s, val)

            @block.tensor
            def _(e):
                run("pe", e)

            @block.scalar
            def _(e):
                run("act", e)

            @block.vector
            def _(e):
                run("dve", e)

            @block.gpsimd
            def _(e):
                run("pool", e)

            @block.sync
            def _(e):
                run("sp", e)


D = 2048
DFF = 5632
NKC = D // 128
NFT = DFF // 128
S5W = 1024
NPAIR = 32
GW = 1024
NH = 8
MIXC = 5136
ALPHA = 2.0 ** 0.25
LN_EPS = 1e-5
NORM_EPS = 1e-6
BIG = 30000.0

N_CORES = 8
TPC = 2048
NSQ = 16
TS = NSQ * 4
NT = TPC + TS
TL = TPC // 2
NL = TL + TS
CCR = 256
FFN_G = 2


def token_tiles(t0, n):
    out = []
    t = t0
    while t < t0 + n:
        m = min(128, t0 + n - t)
        out.append((t, m))
        t += m
    return out


def chunks(n, maxc=512):
    k = -(-n // maxc)
    base = -(-n // k)
    base = -(-base // 32) * 32
    out = []
    t = 0
    while t < n:
        m = min(base, n - t)
        out.append((t, m))
        t += m
    return out


class K:
    pass


def build_program(debug=False, blocks=None, stop_after=None):
    nc = bass.Bass("TRN2", target_bir_lowering=False)
    k = K()
    k.nc = nc
    k.debug = debug
    P = Prog(nc)
    k.P = P
    if blocks is None:
        blocks = [(0, NL)]
    k.blocks = blocks
    maxb = max(n for _, n in blocks)
    k.maxb = maxb

    def din(name, shape, dt=F32):
        return nc.dram_tensor(name, list(shape), dt, kind="ExternalInput").ap()

    def dout(name, shape, dt=F32):
        return nc.dram_tensor(name, list(shape), dt, kind="ExternalOutput").ap()

    def dscr(name, shape, dt=F32):
        kind = "ExternalOutput" if debug else "Internal"
        return nc.dram_tensor(name, list(shape), dt, kind=kind).ap()

    k.x = din("x", [NL, D])
    k.sel = din("sel", [128, 2])
    k.s5re_in = din("s5re_in", [NSQ, 64 * 64])
    k.s5im_in = din("s5im_in", [NSQ, 64 * 64])
    k.gdn_in = din("gdn_in", [NSQ * NH * 128, 128])
    k.conv_in = din("conv_in", [3 * GW, NSQ, 3])
    k.ln_g = [din("ln%d_g" % i, [D]) for i in (1, 2, 3)]
    k.ln_b = [din("ln%d_b" % i, [D]) for i in (1, 2, 3)]
    k.w_in = [din("ffn%d_w_in" % i, [D, 2 * DFF]) for i in (1, 2)]
    k.w_out = [din("ffn%d_w_out" % i, [DFF, D]) for i in (1, 2)]
    k.w_mix_in = din("w_mix_in", [D, MIXC])
    k.w_mix_out = din("w_mix_out", [D, D])
    k.glu_w = din("s5_glu_w", [S5W, S5W])
    k.c_lamre = din("c_lamre", [128, NPAIR])
    k.c_lamim = din("c_lamim", [128, NPAIR])
    k.c_logdt = din("c_logdt", [128, NPAIR])
    k.c_bre = din("c_bre", [32, NPAIR * 128])
    k.c_bim = din("c_bim", [32, NPAIR * 128])
    k.c_cre = din("c_cre", [128, NPAIR * 32])
    k.c_cim = din("c_cim", [128, NPAIR * 32])
    k.c_d = din("c_d", [32, NPAIR])
    k.c_glub = din("c_glub", [128, 8])
    k.c_convw = din("c_convw", [128, 24 * 4])
    k.c_alog = din("c_alog", [128, NH])
    k.c_dtb = din("c_dtb", [128, NH])
    k.c_normw = din("c_normw", [128, 1])
    k.y = dout("y", [NL, D])
    k.p_s5re = dout("p_s5re", [128, NPAIR // 2])
    k.p_s5im = dout("p_s5im", [128, NPAIR // 2])
    k.p_gdn = dout("p_gdn", [NH // 2 * 128, 128])
    k.p_conv = dout("p_conv", [3, 3 * GW])
    k.s_s5re = dout("s_s5re", [NSQ, 64 * 64])
    k.s_s5im = dout("s_s5im", [NSQ, 64 * 64])
    k.s_gdn = dout("s_gdn", [NSQ * NH * 128, 128])
    k.s_conv = dout("s_conv", [NSQ, 3, 3 * GW])
    dint = lambda name, shape, dt=F32: nc.dram_tensor(name, list(shape), dt, kind="Internal").ap()
    k.X1 = dscr("X1", [NL, D])
    k.PROJP = dint("PROJP", [5120, TL])
    k.PROJS = dint("PROJS", [5120, TS])
    k.PROJG = dint("PROJG", [2 * 5120, TL])
    k.BDP = dint("BDP", [TL, 16]); k.BDS = dint("BDS", [TS, 16]); k.BDG = dint("BDG", [2 * TL, 16])
    k.t_PROJP = Tok("PROJP"); k.t_PROJS = Tok("PROJS"); k.t_BDP = Tok("BDP"); k.t_BDS = Tok("BDS")
    k.ZTM = dint("ZTM", [S5W // 2, TPC])
    k.ZTG = dint("ZTG", [S5W, TPC])
    k.ZS = dint("ZS", [S5W, TS])
    k.YTM = dint("YTM", [GW // 2, TPC], BF16)
    k.YTG = dint("YTG", [GW, TPC], BF16)
    k.t_YTM = Tok("YTM"); k.t_YTG = Tok("YTG")
    k.t_ZS = Tok("ZS")
    k.YT = dscr("YT", [D, NT], BF16)
    if debug:
        k.DBGQ = dout("DBGQ", [3, 128, NT])
        k.DBGG = dout("DBGG", [3, 128, 17 * 8])
        k.DBGC = dout("DBGC", [6, 128, 128])
    k.t_X1 = Tok("X1")
    k.t_PROJT = Tok("PROJT")
    k.t_ZT = Tok("ZT"); k.t_ZTG = Tok("ZTG")
    k.t_YT = Tok("YT")

    with contextlib.ExitStack() as es:
        k.es = es
        sb = lambda name, shape, dt=F32: es.enter_context(nc.sbuf_tensor(name, list(shape), dt))
        k.ident = sb("ident", [128, 128])
        k.identb = sb("identb", [128, 128], BF16)
        k.t_ident = Tok("ident")
        k.BDK = sb("BDK", [128, 17, 16])
        k.t_BDK = [Tok("BDK%d" % i) for i in range(17)]
        k.eps_ln = sb("eps_ln", [128, 1])
        k.eps_nm = sb("eps_nm", [128, 1])
        k.t_eps = Tok("eps")
        P.op("pool", lambda e: e.memset(k.ident[:], 0.0), writes=[k.t_ident])
        P.op("pool", lambda e: e.affine_select(out=k.ident[:], in_=k.ident[:], pattern=[[-1, 128]],
                                               compare_op=ALU.not_equal, fill=1.0, base=0,
                                               channel_multiplier=1),
             reads=[k.t_ident], writes=[k.t_ident])
        P.op("pool", lambda e: e.tensor_copy(k.identb[:], k.ident[:]), reads=[k.t_ident], writes=[k.t_ident])
        P.op("pool", lambda e: e.memset(k.BDK[:], 0.0), writes=list(k.t_BDK))
        P.op("pool", lambda e: e.memset(k.eps_ln[:], LN_EPS), writes=[k.t_eps])
        P.op("pool", lambda e: e.memset(k.eps_nm[:], NORM_EPS), writes=[k.t_eps])

        k.bar = sb("bar", [128, 4])
        k.sel_sb = sb("sel_sb", [128, 2]); k.t_sel = Tok("sel")
        P.op("sp", lambda e: e.dma_start(out=k.sel_sb[:], in_=k.sel), writes=[k.t_sel], dma=True)
        phase_ffn(k, which=0, stop_after=stop_after)
        if stop_after != "A":
            P.barrier(k.bar[:])
            rg = [[2 * i, 2 * i + 1] for i in range(N_CORES // 2)]
            for j in range(5120 // CCR):
                P.op("pool", lambda e, j=j: e.collective_compute(
                    "AllGather", ALU.bypass, replica_groups=rg,
                    ins=[k.PROJP[j * CCR:(j + 1) * CCR, :]], outs=[k.PROJG[j * 2 * CCR:(j + 1) * 2 * CCR, :]]),
                    reads=[k.t_PROJP], writes=[k.t_PROJT], nowaw=True, cc=True, semkey=k.t_PROJT)
            P.op("pool", lambda e: e.collective_compute("AllGather", ALU.bypass, replica_groups=rg,
                                                        ins=[k.BDP], outs=[k.BDG]),
                 reads=[k.t_BDP], writes=[k.t_PROJT], nowaw=True, cc=True, semkey=k.t_PROJT)
            phase_mixers(k, stop_after=stop_after)
        if stop_after is None:
            P.barrier(k.bar[:])
            phase_ffn(k, which=1, stop_after=None)
        P.emit()
    k.nsem = P.nsem
    return nc, k


def phase_ffn(k, which, stop_after=None):
    nc, P = k.nc, k.P
    maxb = k.maxb
    ntile_max = -(-maxb // 128)
    with contextlib.ExitStack() as es:
        sb = lambda name, shape, dt=F32: es.enter_context(nc.sbuf_tensor(name + str(which), list(shape), dt))
        ps = lambda name, shape, dt=F32: es.enter_context(nc.psum_tensor(name + str(which), list(shape), dt))
        xT = sb("xT", [128, NKC, maxb], BF16)
        t_xT = [Tok("xT%d" % i) for i in range(ntile_max)]
        acc = sb("acc", [128, ntile_max, D])
        t_acc = [Tok("acc%d" % i) for i in range(ntile_max)]
        NW = 2
        wring = [sb("wr%d" % i, [128, NKC, 128], BF16) for i in range(NW)]
        t_wr = [Tok("wr%d" % i) for i in range(NW)]
        wrup = [sb("wu%d" % i, [128, NKC, 128], BF16) for i in range(NW)]
        t_wu = [Tok("wu%d" % i) for i in range(NW)]
        wo = [sb("wo%d" % i, [128, FFN_G, D], BF16) for i in range(2)]
        t_wo = [Tok("wo%d" % i) for i in range(2)]
        actT = [sb("actT%d" % i, [128, FFN_G, maxb], BF16) for i in range(2)]
        t_act = [Tok("act%d" % i) for i in range(2)]
        xs = [sb("xs%d" % i, [128, D]) for i in range(2)]
        t_xs = [Tok("xs%d" % i) for i in range(2)]
        xb = sb("xb", [128, D], BF16)
        t_xb = Tok("xb")
        gbc = sb("gbc", [128, D])
        bbc = sb("bbc", [128, D])
        t_gb = Tok("gb")
        sg = [sb("sg%d" % i, [128, 512]) for i in range(2)]
        t_sg = [Tok("sg%d" % i) for i in range(2)]
        stage = [sb("stage0", [128, maxb])] * 2
        t_stage = [Tok("stage0")] * 2
        wbd = sb("wbd", [128, NKC, 16], BF16)
        t_wbd = Tok("wbd")
        st = sb("st", [128, 8])
        t_st = Tok("st")
        junk = xb
        t_junk = t_xb
        convs = sb("convs", [128, 3 * GW])
        t_convs = Tok("convs")
        pg = [ps("pg%d" % i, [128, 512]) for i in range(2)]
        pu = [ps("pu%d" % i, [128, 512]) for i in range(2)]
        po = [ps("po%d" % i, [128, 1024]) for i in range(2)]
        t_pg = [Tok("pg%d" % i) for i in range(2)]
        t_pu = [Tok("pu%d" % i) for i in range(2)]
        t_po = [Tok("po%d" % i) for i in range(2)]
        cnt = {"w": 0, "wo": 0, "xs": 0, "pgu": 0, "po": 0, "act": 0, "sg": 0, "stage": 0}

        def load_ln(idx):
            g, b = k.ln_g[idx], k.ln_b[idx]
            P.op("sp", lambda e: e.dma_start(out=gbc[:], in_=g.partition_broadcast(128)),
                 writes=[t_gb], dma=True)
            P.op("sp", lambda e: e.dma_start(out=bbc[:], in_=b.partition_broadcast(128)),
                 writes=[t_gb], dma=True, nowaw=True)

        def transposes_to_xT(ti, m):
            for q in range(4):
                b = cnt["pgu"] % 2
                cnt["pgu"] += 1
                pt = pg[b][:].bitcast(BF16)
                for j in range(4):
                    kc = q * 4 + j
                    P.op("pe", lambda e, pt=pt, j=j, kc=kc: e.transpose(
                        pt[:, j * 128:j * 128 + m], xb[0:m, kc * 128:(kc + 1) * 128], k.identb[0:m, 0:m]),
                        reads=[t_xb, k.t_ident], writes=[t_pg[b]])
                src = pt[:, 0:512].rearrange("p (j t) -> p j t", j=4)[:, :, 0:m]
                dst = xT[:, q * 4:(q + 1) * 4, ti * 128:ti * 128 + m]
                if q % 2 == 0:
                    P.op("dve", lambda e, dst=dst, src=src: e.tensor_copy(dst, src),
                         reads=[t_pg[b]], writes=[t_xT[ti]])
                else:
                    P.op("act", lambda e, dst=dst, src=src: e.copy(dst, src),
                         reads=[t_pg[b]], writes=[t_xT[ti]])

        def ingest(ti, m, xs_i, do_acc, do_xT):
            if do_acc:
                P.op("act", lambda e: e.mul(acc[0:m, ti, :], xs[xs_i][0:m, :], ALPHA),
                     reads=[t_xs[xs_i]], writes=[t_acc[ti]])
            if do_xT:
                P.op("pool", lambda e: e.tensor_copy(xb[0:m, :], xs[xs_i][0:m, :]),
                     reads=[t_xs[xs_i]], writes=[t_xb])
                transposes_to_xT(ti, m)

        def layer_norm_tile(ti, m, xs_i):
            z = acc[0:m, ti, :]
            P.op("dve", lambda e: e.reduce_sum(st[0:m, 0:1], z, axis=AX.X),
                 reads=[t_acc[ti]], writes=[t_st])
            P.op("dve", lambda e: e.tensor_scalar_mul(st[0:m, 1:2], st[0:m, 0:1], -1.0 / D),
                 reads=[t_st], writes=[t_st])
            P.op("act", lambda e: e.activation(junk[0:m, :], z, AF.Square, bias=st[0:m, 1:2], scale=1.0,
                                               accum_out=st[0:m, 2:3]),
                 reads=[t_acc[ti], t_st], writes=[t_junk, t_st])
            P.op("act", lambda e: e.activation(st[0:m, 3:4], st[0:m, 2:3], AF.Sqrt, bias=k.eps_ln[0:m, :],
                                               scale=1.0 / D),
                 reads=[t_st, k.t_eps], writes=[t_st])
            P.op("dve", lambda e: e.reciprocal(st[0:m, 4:5], st[0:m, 3:4]), reads=[t_st], writes=[t_st])
            P.op("dve", lambda e: e.tensor_scalar(xs[xs_i][0:m, :], z, st[0:m, 1:2], st[0:m, 4:5],
                                                  ALU.add, ALU.mult),
                 reads=[t_acc[ti], t_st], writes=[t_xs[xs_i]])
            P.op("pool", lambda e: e.tensor_tensor(xs[xs_i][0:m, :], xs[xs_i][0:m, :], gbc[0:m, :], ALU.mult),
                 reads=[t_xs[xs_i], t_gb], writes=[t_xs[xs_i]])
            P.op("pool", lambda e: e.tensor_tensor(xs[xs_i][0:m, :], xs[xs_i][0:m, :], bbc[0:m, :], ALU.add),
                 reads=[t_xs[xs_i], t_gb], writes=[t_xs[xs_i]])

        def outproj_group(lhs_buf, t_lhs, w_dram, row0, ng, tiles, scale, after_tile=None):
            wi = cnt["wo"] % 2
            cnt["wo"] += 1
            wv = w_dram[row0:row0 + ng * 128, :].rearrange("(g p) c -> p g c", p=128)
            P.op("pool", lambda e: e.dma_start(out=wo[wi][:, 0:ng, :], in_=wv), writes=[t_wo[wi]], dma=True)
            for li, (t0, m) in enumerate(tiles):
                for half in range(2):
                    pb = cnt["po"] % 2
                    cnt["po"] += 1
                    for g in range(ng):
                        for c in range(2):
                            P.op("pe", lambda e, pb=pb, g=g, c=c, half=half, li=li, m=m: e.matmul(
                                po[pb][0:m, c * 512:(c + 1) * 512],
                                lhs_buf[:, g, li * 128:li * 128 + m],
                                wo[wi][:, g, half * 1024 + c * 512: half * 1024 + (c + 1) * 512],
                                start=(g == 0), stop=(g == ng - 1)),
                                reads=[t_lhs, t_wo[wi]], writes=[t_po[pb]])
                    P.op("dve", lambda e, pb=pb, half=half, li=li, m=m: e.scalar_tensor_tensor(
                        out=acc[0:m, li, half * 1024:(half + 1) * 1024], in0=po[pb][0:m, :], scalar=scale,
                        in1=acc[0:m, li, half * 1024:(half + 1) * 1024], op0=ALU.mult, op1=ALU.add),
                        reads=[t_po[pb], t_acc[li]], writes=[t_acc[li]])
                if after_tile is not None:
                    after_tile(li, t0, m)

        def ffn(widx, nb, tiles, after_tile=None):
            w_in_v = k.w_in[widx].rearrange("(kc p) c -> p kc c", p=128)
            cks = chunks(nb)
            for g0 in range(0, NFT, FFN_G):
                ai = cnt["act"] % 2
                cnt["act"] += 1
                for g in range(FFN_G):
                    j = g0 + g
                    ws = cnt["w"] % NW
                    cnt["w"] += 1
                    P.op("pool", lambda e, ws=ws, j=j: e.dma_start(
                        out=wring[ws][:], in_=w_in_v[:, :, j * 128:(j + 1) * 128]), writes=[t_wr[ws]], dma=True)
                    P.op("pool", lambda e, ws=ws, j=j: e.dma_start(
                        out=wrup[ws][:], in_=w_in_v[:, :, DFF + j * 128:DFF + (j + 1) * 128]),
                        writes=[t_wu[ws]], dma=True)
                    for (c0, cn) in cks:
                        b = cnt["pgu"] % 2
                        cnt["pgu"] += 1
                        rtoks = [t_xT[i] for i in range(c0 // 128, (c0 + cn - 1) // 128 + 1)]
                        for kc in range(NKC):
                            P.op("pe", lambda e, b=b, ws=ws, kc=kc, c0=c0, cn=cn: e.matmul(
                                pg[b][:, 0:cn], wring[ws][:, kc, :], xT[:, kc, c0:c0 + cn],
                                start=(kc == 0), stop=(kc == NKC - 1)),
                                reads=[t_wr[ws]] + rtoks, writes=[t_pg[b]])
                        for kc in range(NKC):
                            P.op("pe", lambda e, b=b, ws=ws, kc=kc, c0=c0, cn=cn: e.matmul(
                                pu[b][:, 0:cn], wrup[ws][:, kc, :], xT[:, kc, c0:c0 + cn],
                                start=(kc == 0), stop=(kc == NKC - 1)),
                                reads=[t_wu[ws]] + rtoks, writes=[t_pu[b]])
                        si = cnt["sg"] % 2
                        cnt["sg"] += 1
                        P.op("act", lambda e, b=b, si=si, cn=cn: e.activation(sg[si][:, 0:cn], pg[b][:, 0:cn], AF.Silu),
                             reads=[t_pg[b]], writes=[t_sg[si]])
                        P.op("dve", lambda e, b=b, si=si, cn=cn, c0=c0, g=g, ai=ai: e.tensor_tensor(
                            actT[ai][:, g, c0:c0 + cn], sg[si][:, 0:cn], pu[b][:, 0:cn], ALU.mult),
                            reads=[t_sg[si], t_pu[b]], writes=[t_act[ai]])
                outproj_group(actT[ai], t_act[ai], k.w_out[widx], g0 * 128, FFN_G, tiles, 0.5,
                              after_tile if g0 + FFN_G >= NFT else None)

        def mix_in(nb, t0, tiles):
            wv = k.w_mix_in.rearrange("(kc p) c -> p kc c", p=128)
            cks = chunks(nb)
            has_prompt_tail = (t0 <= TL - 3) and (t0 + nb >= TL)
            has_sample = (t0 + nb >= NL)
            n_p = min(t0 + nb, TL) - t0
            n_s = nb - n_p
            P.op("pool", lambda e: e.dma_start(out=wbd[:], in_=wv[:, :, 5120:5136]), writes=[t_wbd], dma=True)
            for li, (tt0, m) in enumerate(tiles):
                b = cnt["pgu"] % 2
                cnt["pgu"] += 1
                gi = tt0 // 128
                for kc in range(NKC):
                    P.op("pe", lambda e, b=b, kc=kc, li=li, m=m: e.matmul(
                        pu[b][0:m, 0:16], xT[:, kc, li * 128:li * 128 + m], wbd[:, kc, :],
                        start=(kc == 0), stop=(kc == NKC - 1)),
                        reads=[t_xT[li], t_wbd], writes=[t_pu[b]])
                P.op("act", lambda e, b=b, gi=gi, m=m: e.copy(k.BDK[0:m, gi, :], pu[b][0:m, 0:16]),
                     reads=[t_pu[b]], writes=[k.t_BDK[gi]])
                if tt0 < TL:
                    P.op("sp", lambda e, gi=gi, m=m, tt0=tt0: e.dma_start(out=k.BDP[tt0:tt0 + m, :], in_=k.BDK[0:m, gi, :]),
                         reads=[k.t_BDK[gi]], writes=[k.t_BDP], dma=True, semkey=k.t_BDK[gi], nowaw=True)
                else:
                    P.op("sp", lambda e, gi=gi, m=m: e.dma_start(out=k.BDS[0:m, :], in_=k.BDK[0:m, gi, :]),
                         reads=[k.t_BDK[gi]], writes=[k.t_BDS], dma=True, semkey=k.t_BDK[gi], nowaw=True)
            for ct in range(40):
                ws = cnt["w"] % NW
                cnt["w"] += 1
                P.op("pool", lambda e, ws=ws, ct=ct: e.dma_start(
                    out=wring[ws][:], in_=wv[:, :, ct * 128:(ct + 1) * 128]),
                    writes=[t_wr[ws]], dma=True)
                si = cnt["stage"] % 2
                cnt["stage"] += 1
                for (c0, cn) in cks:
                    b = cnt["pgu"] % 2
                    cnt["pgu"] += 1
                    rtoks = [t_xT[i] for i in range(c0 // 128, (c0 + cn - 1) // 128 + 1)]
                    for kc in range(NKC):
                        P.op("pe", lambda e, b=b, ws=ws, kc=kc, c0=c0, cn=cn: e.matmul(
                            pg[b][:, 0:cn], wring[ws][:, kc, :], xT[:, kc, c0:c0 + cn],
                            start=(kc == 0), stop=(kc == NKC - 1)),
                            reads=[t_wr[ws]] + rtoks, writes=[t_pg[b]])
                    fn = AF.Silu if ct >= 32 else AF.Copy
                    P.op("act", lambda e, b=b, si=si, c0=c0, cn=cn, fn=fn: e.activation(
                        stage[si][:, c0:c0 + cn], pg[b][:, 0:cn], fn),
                        reads=[t_pg[b]], writes=[t_stage[si]])
                P.op("sp", lambda e, si=si, ct=ct: e.dma_start(
                    out=k.PROJP[ct * 128:(ct + 1) * 128, t0:t0 + n_p], in_=stage[si][:, 0:n_p]),
                    reads=[t_stage[si]], writes=[k.t_PROJP], dma=True, semkey=t_stage[si], nowaw=True)
                if n_s:
                    P.op("sp", lambda e, si=si, ct=ct: e.dma_start(
                        out=k.PROJS[ct * 128:(ct + 1) * 128, 0:n_s], in_=stage[si][:, n_p:n_p + n_s]),
                        reads=[t_stage[si]], writes=[k.t_PROJS], dma=True, semkey=t_stage[si], nowaw=True)
                if 8 <= ct < 32 and (has_prompt_tail or has_sample):
                    cc = ct - 8
                    b = cnt["pgu"] % 2
                    cnt["pgu"] += 1
                    if has_sample:
                        s0 = TL - t0
                        P.op("pe", lambda e, b=b, si=si, s0=s0: e.transpose(
                            pu[b][0:TS, 0:128], stage[si][:, s0:s0 + TS], k.ident[:]),
                            reads=[t_stage[si], k.t_ident], writes=[t_pu[b]])
                    if has_prompt_tail:
                        s1 = TL - 32 - t0
                        P.op("pe", lambda e, b=b, si=si, s1=s1: e.transpose(
                            pu[b][0:32, 128:256], stage[si][:, s1:s1 + 32], k.ident[:]),
                            reads=[t_stage[si], k.t_ident], writes=[t_pu[b]])
                    if has_sample:
                        P.op("dve", lambda e, b=b, cc=cc: e.tensor_copy(
                            convs[0:TS, cc * 128:(cc + 1) * 128], pu[b][0:TS, 0:128]),
                            reads=[t_pu[b]], writes=[t_convs])
                    if has_prompt_tail:
                        P.op("dve", lambda e, b=b, cc=cc: e.tensor_copy(
                            convs[64:96, cc * 128:(cc + 1) * 128], pu[b][0:32, 128:256]),
                            reads=[t_pu[b]], writes=[t_convs])
            if has_prompt_tail:
                o = P.op("sp", lambda e: e.dma_start(out=k.p_conv[:, :], in_=convs[93:96, :]),
                         reads=[t_convs], dma=True, semkey=t_convs)
                P.final.append(o)
            if has_sample:
                for r in range(3):
                    src = convs[r + 1:TS:4, :]
                    o = P.op("sp", lambda e, r=r, src=src: e.dma_start(out=k.s_conv[:, r, :], in_=src),
                             reads=[t_convs], dma=True, semkey=t_convs)
                    P.final.append(o)

        def mix_out(nb, t0, tiles, after_tile=None):
            yv = k.YT.rearrange("(kc p) t -> p kc t", p=128)
            n_p = min(t0 + nb, TL) - t0
            n_s = nb - n_p
            HMh = NH // 2

            def ygrows(hh):
                rk_, m_ = divmod(hh, HMh)
                j_, i0_ = divmod(m_ * 128, 256)
                return k.YTG.rearrange("(j h i) t -> h j i t", h=2, i=256)[rk_][j_][i0_:i0_ + 128]
            P.op("sp", lambda e: e.dma_start(out=xT[:, 0:8, 0:n_p], in_=yv[:, 0:8, t0:t0 + n_p]),
                 reads=[k.t_YT], writes=t_xT, dma=True, semkey=t_xT[0])
            for hh in range(NH):
                P.op("sp", lambda e, hh=hh: e.dma_start(out=xT[:, 8 + hh, 0:n_p], in_=ygrows(hh)[:, t0:t0 + n_p]),
                     reads=[k.t_YTG], writes=t_xT, dma=True, semkey=t_xT[0], nowaw=True)
            if n_s:
                P.op("sp", lambda e: e.dma_start(out=xT[:, :, n_p:nb], in_=yv[:, :, TPC:TPC + n_s]),
                     reads=[k.t_YT], writes=t_xT, dma=True, semkey=t_xT[0], nowaw=True)
            for g0 in range(0, NKC, FFN_G):
                ai = (g0 // FFN_G) % 2
                if g0 < 8:
                    P.op("sp", lambda e, g0=g0, ai=ai: e.dma_start(out=actT[ai][:, :, 0:n_p], in_=yv[:, g0:g0 + FFN_G, TL + t0:TL + t0 + n_p]),
                         reads=[k.t_YT], writes=[t_act[ai]], dma=True)
                else:
                    for g in range(FFN_G):
                        P.op("sp", lambda e, g0=g0, g=g, ai=ai: e.dma_start(
                            out=actT[ai][:, g, 0:n_p], in_=ygrows(g0 - 8 + g)[:, TL + t0:TL + t0 + n_p]),
                            reads=[k.t_YTG], writes=[t_act[ai]], dma=True, nowaw=(g > 0))
                P.op("dve", lambda e, g0=g0: e.tensor_scalar_mul(xT[:, g0:g0 + FFN_G, 0:n_p], xT[:, g0:g0 + FFN_G, 0:n_p], k.sel_sb[:, 0:1]),
                     reads=[k.t_sel], writes=t_xT)
                P.op("dve", lambda e, g0=g0, ai=ai: e.scalar_tensor_tensor(
                    out=xT[:, g0:g0 + FFN_G, 0:n_p], in0=actT[ai][:, :, 0:n_p], scalar=k.sel_sb[:, 1:2],
                    in1=xT[:, g0:g0 + FFN_G, 0:n_p], op0=ALU.mult, op1=ALU.add),
                    reads=[k.t_sel, t_act[ai]], writes=t_xT)
            for g0 in range(0, NKC, FFN_G):
                class _V:
                    def __init__(self, g0):
                        self.g0 = g0

                    def __getitem__(self, idx):
                        p, g, t = idx
                        return xT[p, self.g0 + g, t]
                outproj_group(_V(g0), t_xT[0], k.w_mix_out, g0 * 128, FFN_G, tiles, 1.0,
                              after_tile if g0 + FFN_G >= NKC else None)

        for (t0, nb) in k.blocks:
            tiles = token_tiles(t0, nb)

            def nxs():
                xi = cnt["xs"] % 2
                cnt["xs"] += 1
                return xi

            if which == 0:
                load_ln(0)
                for li, (tt0, m) in enumerate(tiles):
                    xi = nxs()
                    P.op("sp", lambda e, xi=xi, tt0=tt0, m=m: e.dma_start(out=xs[xi][0:m, :], in_=k.x[tt0:tt0 + m, :]),
                         writes=[t_xs[xi]], dma=True)
                    ingest(li, m, xi, True, True)

                def ln1_tile(li, tt0, m):
                    xi = nxs()
                    layer_norm_tile(li, m, xi)
                    P.op("sp", lambda e, xi=xi, tt0=tt0, m=m: e.dma_start(out=k.X1[tt0:tt0 + m, :], in_=xs[xi][0:m, :]),
                         reads=[t_xs[xi]], writes=[k.t_X1], dma=True, semkey=t_xs[xi], nowaw=True)
                    ingest(li, m, xi, False, True)
                ffn(0, nb, tiles, ln1_tile)
                mix_in(nb, t0, tiles)
            else:
                load_ln(1)
                for li, (tt0, m) in enumerate(tiles):
                    xi = nxs()
                    P.op("sp", lambda e, xi=xi, tt0=tt0, m=m: e.dma_start(out=xs[xi][0:m, :], in_=k.X1[tt0:tt0 + m, :]),
                         reads=[k.t_X1], writes=[t_xs[xi]], dma=True)
                    ingest(li, m, xi, True, False)

                def ln2_tile(li, tt0, m):
                    xi = nxs()
                    layer_norm_tile(li, m, xi)
                    ingest(li, m, xi, True, True)
                mix_out(nb, t0, tiles, ln2_tile)
                load_ln(2)

                def ln3_tile(li, tt0, m):
                    xi = nxs()
                    layer_norm_tile(li, m, xi)
                    o = P.op("sp", lambda e, xi=xi, tt0=tt0, m=m: e.dma_start(out=k.y[tt0:tt0 + m, :], in_=xs[xi][0:m, :]),
                             reads=[t_xs[xi]], dma=True, semkey=t_xs[xi])
                    P.final.append(o)
                ffn(1, nb, tiles, ln3_tile)


def pg_rows(k, r0, n):
    j, i0 = divmod(r0, CCR)
    assert i0 + n <= CCR
    return k.PROJG.rearrange("(j h i) t -> j i h t", h=2, i=CCR)[j][i0:i0 + n]


def phase_mixers(k, stop_after=None):
    phase_s5(k)
    k.P.barrier(k.bar[:])
    phase_glu(k)
    if stop_after == "S5":
        return
    k.P.barrier(k.bar[:])
    phase_gdn(k)


TWO_PI = 6.283185307179586
CW1 = 6.28125
CW2 = TWO_PI - CW1
SINS = 1.0
PI_CL = 3.141592


def phase_s5(k):
    nc, P = k.nc, k.P
    I32 = mybir.dt.int32
    TP = TPC
    with contextlib.ExitStack() as es:
        sb = lambda name, shape, dt=F32: es.enter_context(nc.sbuf_tensor("s5_" + name, list(shape), dt))
        ps = lambda name, shape, dt=F32: es.enter_context(nc.psum_tensor("s5_" + name, list(shape), dt))
        T = lambda name: Tok(name)
        t_c = T("consts")
        lamre = sb("lamre", [128, NPAIR]); lamim = sb("lamim", [128, NPAIR]); logdt = sb("logdt", [128, NPAIR])
        dcol = sb("dcol", [32, NPAIR])
        for dst, src in ((lamre, k.c_lamre), (lamim, k.c_lamim), (logdt, k.c_logdt), (dcol, k.c_d)):
            P.op("sp", lambda e, dst=dst, src=src: e.dma_start(out=dst[:], in_=src), writes=[t_c], dma=True, nowaw=True)
        bre = sb("bre", [32, NPAIR * 128], BF16); bim = sb("bim", [32, NPAIR * 128], BF16)
        cre = sb("cre", [128, NPAIR * 32], BF16); cim = sb("cim", [128, NPAIR * 32], BF16)
        t_w = T("s5w")
        for dst, src in ((bre, k.c_bre), (bim, k.c_bim), (cre, k.c_cre), (cim, k.c_cim)):
            P.op("pool", lambda e, dst=dst, src=src: e.dma_start(out=dst[:], in_=src), writes=[t_w], dma=True, nowaw=True)

        sc = {n: sb("sc_" + n, [128, NPAIR]) for n in
              ("dt", "th", "rho", "r", "sn", "cs", "nr", "ni", "den", "cre", "cim", "ar", "ai", "t1", "t2", "kf", "rd")}
        sci = sb("sc_ki", [128, NPAIR], I32)
        t_s = T("scal")

        def so(eng, fn, extra=()):
            P.op(eng, fn, reads=[t_s, t_c] + list(extra), writes=[t_s])

        so("act", lambda e: e.activation(sc["dt"][:], logdt[:], AF.Exp))
        so("dve", lambda e: e.tensor_tensor(sc["th"][:], lamim[:], sc["dt"][:], ALU.mult))
        so("dve", lambda e: e.tensor_tensor(sc["rho"][:], lamre[:], sc["dt"][:], ALU.mult))
        so("act", lambda e: e.activation(sc["r"][:], sc["rho"][:], AF.Exp))

        def range_reduce(eng_a, eng_b, out, x, ki, kf, shape_ok=True):
            so(eng_a, lambda e: e.tensor_scalar_mul(kf, x, 1.0 / TWO_PI))
            so(eng_a, lambda e: e.tensor_copy(ki, kf))
            so(eng_b, lambda e: e.tensor_copy(kf, ki))
            so(eng_a, lambda e: e.scalar_tensor_tensor(out=out, in0=kf, scalar=-CW1, in1=x, op0=ALU.mult, op1=ALU.add))
            so(eng_b, lambda e: e.scalar_tensor_tensor(out=out, in0=kf, scalar=-CW2, in1=out, op0=ALU.mult, op1=ALU.add))
            so(eng_b, lambda e: e.tensor_scalar(out, out, -PI_CL, PI_CL, ALU.max, ALU.min))

        range_reduce("dve", "dve", sc["rd"][:], sc["th"][:], sci[:], sc["kf"][:])
        so("act", lambda e: e.activation(sc["sn"][:], sc["rd"][:], AF.Sin, scale=SINS))
        so("act", lambda e: e.activation(sc["t1"][:], sc["rd"][:], AF.Sin, scale=0.5 * SINS))
        so("dve", lambda e: e.tensor_tensor(sc["t1"][:], sc["t1"][:], sc["t1"][:], ALU.mult))
        so("dve", lambda e: e.tensor_scalar(sc["cs"][:], sc["t1"][:], -2.0, 1.0, ALU.mult, ALU.add))
        so("dve", lambda e: e.tensor_tensor(sc["ar"][:], sc["r"][:], sc["cs"][:], ALU.mult))
        so("dve", lambda e: e.tensor_tensor(sc["ai"][:], sc["r"][:], sc["sn"][:], ALU.mult))
        so("dve", lambda e: e.tensor_scalar_add(sc["nr"][:], sc["ar"][:], -1.0))
        so("dve", lambda e: e.tensor_tensor(sc["t1"][:], lamre[:], lamre[:], ALU.mult))
        so("dve", lambda e: e.tensor_tensor(sc["t2"][:], lamim[:], lamim[:], ALU.mult))
        so("dve", lambda e: e.tensor_tensor(sc["den"][:], sc["t1"][:], sc["t2"][:], ALU.add))
        so("dve", lambda e: e.reciprocal(sc["den"][:], sc["den"][:]))
        so("dve", lambda e: e.tensor_tensor(sc["t1"][:], sc["nr"][:], lamre[:], ALU.mult))
        so("dve", lambda e: e.tensor_tensor(sc["t2"][:], sc["ai"][:], lamim[:], ALU.mult))
        so("dve", lambda e: e.tensor_tensor(sc["t1"][:], sc["t1"][:], sc["t2"][:], ALU.add))
        so("dve", lambda e: e.tensor_tensor(sc["cre"][:], sc["t1"][:], sc["den"][:], ALU.mult))
        so("dve", lambda e: e.tensor_tensor(sc["t1"][:], sc["ai"][:], lamre[:], ALU.mult))
        so("dve", lambda e: e.tensor_tensor(sc["t2"][:], sc["nr"][:], lamim[:], ALU.mult))
        so("dve", lambda e: e.tensor_tensor(sc["t1"][:], sc["t1"][:], sc["t2"][:], ALU.subtract))
        so("dve", lambda e: e.tensor_tensor(sc["cim"][:], sc["t1"][:], sc["den"][:], ALU.mult))

        NM = NPAIR // 2
        s0c, s1c = k.sel_sb[:, 0:1], k.sel_sb[:, 1:2]
        msc = {n: sb("msc_" + n, [128, NM]) for n in ("th", "r", "cre", "cim")}
        for n in msc:
            so("dve", lambda e, n=n: e.tensor_scalar_mul(msc[n][:], sc[n][:, 0:NM], s0c), extra=[k.t_sel])
            so("dve", lambda e, n=n: e.scalar_tensor_tensor(out=msc[n][:], in0=sc[n][:, NM:NPAIR], scalar=s1c, in1=msc[n][:],
                                                            op0=ALU.mult, op1=ALU.add), extra=[k.t_sel])
        mdcol = sb("mdcol", [32, NM])
        so("dve", lambda e: e.tensor_scalar_mul(mdcol[:], dcol[:, 0:NM], k.sel_sb[0:32, 0:1]), extra=[k.t_sel])
        so("dve", lambda e: e.scalar_tensor_tensor(out=mdcol[:], in0=dcol[:, NM:NPAIR], scalar=k.sel_sb[0:32, 1:2], in1=mdcol[:],
                                                   op0=ALU.mult, op1=ALU.add), extra=[k.t_sel])
        mB = [[sb("mB%d_%d" % (a, b_), [32, 128], BF16) for b_ in range(2)] for a in range(2)]
        mC = [[sb("mC%d_%d" % (a, b_), [128, 32], BF16) for b_ in range(2)] for a in range(2)]
        t_mw2 = [T("s5mw0"), T("s5mw1")]

        iot = sb("iot", [128, TP])
        t_iot = T("iota")

        pX = [ps("pX%d" % i, [128, 512]) for i in range(4)]
        t_pX = [T("pX%d" % i) for i in range(4)]
        pY = [ps("pY%d" % i, [128, 512]) for i in range(2)]
        t_pY = [T("pY%d" % i) for i in range(2)]
        pT = ps("pT", [128, 512]); t_pT = T("pT")
        pO = ps("pO", [128, 512]); t_pO = T("pO")

        st2 = sb("st_in", [32, 2048]); t_stin = T("stin")
        st_in = st2[0:NSQ, :]
        u_g = st2
        h_re = sb("h_re", [128, NPAIR, NSQ]); h_im = sb("h_im", [128, NPAIR, NSQ])
        t_h = T("h")
        for (src, dst) in ((k.s5re_in, h_re), (k.s5im_in, h_im)):
            for hh in range(2):
                P.op("sp", lambda e, src=src, hh=hh: e.dma_start(out=st_in[:], in_=src[:, hh * 2048:(hh + 1) * 2048]),
                     writes=[t_stin], dma=True)
                for g16 in range(16):
                    P.op("pe", lambda e, g16=g16: e.transpose(pT[:, g16 * NSQ:(g16 + 1) * NSQ],
                                                              st_in[:, g16 * 128:(g16 + 1) * 128], k.ident[0:NSQ, 0:NSQ]),
                         reads=[t_stin, k.t_ident], writes=[t_pT])
                P.op("dve", lambda e, dst=dst, hh=hh: e.tensor_copy(
                    dst[:, hh * 16:(hh + 1) * 16, :].rearrange("p g b -> p (g b)"), pT[:, 0:16 * NSQ]),
                    reads=[t_pT], writes=[t_h])

        xs_re = sb("xs_re", [128, NPAIR, TS]); xs_im = sb("xs_im", [128, NPAIR, TS])
        t_xs = T("xs")
        hs_re = sb("hs_re", [128, NPAIR, TS], BF16); hs_im = sb("hs_nim", [128, NPAIR, TS], BF16)
        t_hs = T("hs")
        pst_re = sb("pst_re", [128, NPAIR // 2]); pst_im = sb("pst_im", [128, NPAIR // 2]); t_pst = T("pst")
        zs = sb("zs", [32, NPAIR, TS]); t_zs = T("zs")

        HT = TP // 2
        u_f = sb("u_f", [32, TP]); u_b = sb("u_b", [32, TP], BF16); t_u = T("u")
        def dbl(name, dt=F32, w=HT, parts=128):
            return [sb("%s%d" % (name, i), [parts, w], dt) for i in range(2)], [T("%s%d" % (name, i)) for i in range(2)]
        ang, t_ang = dbl("ang"); kfb, t_kfb = dbl("kfb"); kib, t_kib = dbl("kib", I32)
        snT, t_sn = dbl("snT"); csT, t_cs = dbl("csT")
        Wr, t_Wr = dbl("Wr"); Wi, t_Wi = dbl("Wi")
        xr, t_xr = dbl("xr"); xi, t_xi = dbl("xi")
        hrb, t_hrb = dbl("hrb", BF16); hib, t_hib = dbl("hib", BF16)
        zst, t_z = dbl("zst", F32, HT, 32)
        tt1, t_t1 = dbl("tt1", F32, 512); tt2, t_t2 = dbl("tt2", F32, 512)
        tt3, t_t3 = dbl("tt3", F32, 512); tt4, t_t4 = dbl("tt4", F32, 512)
        xis, t_xis = dbl("xis", F32, 512)
        xrs, t_xrs = dbl("xrs", F32, 512)
        sm = sb("sm", [128, 4]); t_sm = T("sm")
        for hh in range(2):
            P.op("pool", lambda e, hh=hh: e.iota(kib[hh][:], pattern=[[1, HT]], base=hh * HT, channel_multiplier=0),
                 writes=[t_kib[hh]])
            P.op("pool", lambda e, hh=hh: e.tensor_copy(iot[:, hh * HT:(hh + 1) * HT], kib[hh][:]),
                 reads=[t_kib[hh]], writes=[t_iot])
        cnt = {"x": 0, "y": 0, "c": 0}
        prev_par = None

        for gp in range(NM):
            col = lambda n, gp=gp: msc[n][:, gp:gp + 1]
            mq = gp % 2
            t_mw = t_mw2[mq]
            for (dst, src, pr_, w_) in ((mB[0][mq], bre, 32, 128), (mB[1][mq], bim, 32, 128),
                                        (mC[0][mq], cre, 128, 32), (mC[1][mq], cim, 128, 32)):
                P.op("dve", lambda e, dst=dst, src=src, pr_=pr_, w_=w_, gp=gp: e.tensor_scalar_mul(
                    dst[:], src[:, gp * w_:(gp + 1) * w_], k.sel_sb[0:pr_, 0:1]), reads=[t_w, k.t_sel], writes=[t_mw])
                P.op("dve", lambda e, dst=dst, src=src, pr_=pr_, w_=w_, gp=gp: e.scalar_tensor_tensor(
                    out=dst[:], in0=src[:, (NM + gp) * w_:(NM + gp + 1) * w_], scalar=k.sel_sb[0:pr_, 1:2], in1=dst[:],
                    op0=ALU.mult, op1=ALU.add), reads=[t_w, k.t_sel], writes=[t_mw])
            mbre_, mbim_, mcre_, mcim_ = mB[0][mq], mB[1][mq], mC[0][mq], mC[1][mq]
            P.op("sp", lambda e, gp=gp: e.dma_start(out=u_f[:, 0:TP].rearrange("p (h t) -> p h t", h=2),
                                                    in_=pg_rows(k, gp * 32, 32)),
                 reads=[k.t_PROJT], writes=[t_u], dma=True)
            P.op("sp", lambda e, gp=gp: e.dma_start(out=u_g[:, 0:TP].rearrange("p (h t) -> p h t", h=2),
                                                    in_=pg_rows(k, (NM + gp) * 32, 32)),
                 reads=[k.t_PROJT], writes=[t_stin], dma=True)
            P.op("dve", lambda e: e.tensor_scalar_mul(u_f[:, 0:TP], u_f[:, 0:TP], k.sel_sb[0:32, 0:1]),
                 reads=[k.t_sel], writes=[t_u])
            P.op("dve", lambda e: e.scalar_tensor_tensor(out=u_f[:, 0:TP], in0=u_g[:, 0:TP], scalar=k.sel_sb[0:32, 1:2],
                                                         in1=u_f[:, 0:TP], op0=ALU.mult, op1=ALU.add),
                 reads=[k.t_sel, t_stin], writes=[t_u])
            P.op("act", lambda e: e.copy(u_b[:, 0:TP], u_f[:, 0:TP]), reads=[], writes=[t_u])
            for hf in range(2):
                p = (gp * 2 + hf) % 2
                c_lo = hf * HT
                A, Kf, Ki, SN, CS, WR, WI, XR, XI = ang[p], kfb[p], kib[p], snT[p], csT[p], Wr[p], Wi[p], xr[p], xi[p]
                P.op("dve", lambda e, A=A, col=col, c_lo=c_lo: e.tensor_scalar_mul(A[:], iot[:, c_lo:c_lo + HT], col("th")),
                     reads=[t_iot, t_s], writes=[t_ang[p]])
                P.op("act", lambda e, A=A, Kf=Kf: e.mul(Kf[:], A[:], 1.0 / TWO_PI), reads=[t_ang[p]], writes=[t_kfb[p]])
                P.op("dve", lambda e, Kf=Kf, Ki=Ki: e.tensor_copy(Ki[:], Kf[:]), reads=[t_kfb[p]], writes=[t_kib[p]])
                P.op("pool", lambda e, Kf=Kf, Ki=Ki: e.tensor_copy(Kf[:], Ki[:]), reads=[t_kib[p]], writes=[t_kfb[p]])
                P.op("dve", lambda e, A=A, Kf=Kf: e.scalar_tensor_tensor(out=A[:], in0=Kf[:], scalar=-CW1, in1=A[:],
                                                                       op0=ALU.mult, op1=ALU.add),
                     reads=[t_kfb[p]], writes=[t_ang[p]])
                P.op("dve", lambda e, A=A, Kf=Kf: e.scalar_tensor_tensor(out=A[:], in0=Kf[:], scalar=-CW2, in1=A[:],
                                                                       op0=ALU.mult, op1=ALU.add),
                     reads=[t_kfb[p]], writes=[t_ang[p]])
                P.op("dve", lambda e, A=A: e.tensor_scalar(A[:], A[:], -PI_CL, PI_CL, ALU.max, ALU.min),
                     reads=[], writes=[t_ang[p]])
                P.op("act", lambda e, A=A, SN=SN: e.activation(SN[:], A[:], AF.Sin), reads=[t_ang[p]], writes=[t_sn[p]])
                P.op("act", lambda e, A=A, CS=CS: e.activation(CS[:], A[:], AF.Sin, scale=0.5), reads=[t_ang[p]], writes=[t_cs[p]])
                P.op("act", lambda e, CS=CS: e.activation(CS[:], CS[:], AF.Square), reads=[], writes=[t_cs[p]])
                P.op("act", lambda e, CS=CS: e.mul(CS[:], CS[:], -2.0), reads=[], writes=[t_cs[p]])
                P.op("act", lambda e, CS=CS: e.add(CS[:], CS[:], 1.0), reads=[], writes=[t_cs[p]])
                P.op("dve", lambda e, WR=WR, CS=CS, col=col: e.tensor_scalar_mul(WR[:], CS[:], col("cre")),
                     reads=[t_cs[p], t_s], writes=[t_Wr[p]])
                P.op("dve", lambda e, WR=WR, SN=SN, col=col: e.scalar_tensor_tensor(
                    out=WR[:], in0=SN[:], scalar=col("cim"), in1=WR[:], op0=ALU.mult, op1=ALU.add),
                    reads=[t_sn[p], t_s], writes=[t_Wr[p]])
                P.op("dve", lambda e, WI=WI, SN=SN, col=col: e.tensor_scalar_mul(WI[:], SN[:], col("cre")),
                     reads=[t_sn[p], t_s], writes=[t_Wi[p]])
                P.op("dve", lambda e, WI=WI, CS=CS, col=col: e.scalar_tensor_tensor(
                    out=WI[:], in0=CS[:], scalar=col("cim"), in1=WI[:], op0=ALU.mult, op1=ALU.subtract),
                    reads=[t_cs[p], t_s], writes=[t_Wi[p]])
                cks = [(c_lo + i * 512, 512) for i in range(HT // 512)]
                for (c0, cn) in cks:
                    b = cnt["x"] % 2
                    cnt["x"] += 1
                    pr, pi = pX[2 * b], pX[2 * b + 1]
                    tpr, tpi = t_pX[2 * b], t_pX[2 * b + 1]
                    P.op("pe", lambda e, mbre_=mbre_, pr=pr, c0=c0, cn=cn: e.matmul(
                        pr[:, 0:cn], mbre_[:], u_b[:, c0:c0 + cn], start=True, stop=True),
                        reads=[t_mw, t_u], writes=[tpr])
                    P.op("pe", lambda e, mbim_=mbim_, pi=pi, c0=c0, cn=cn: e.matmul(
                        pi[:, 0:cn], mbim_[:], u_b[:, c0:c0 + cn], start=True, stop=True),
                        reads=[t_mw, t_u], writes=[tpi])
                    if c0 < TP:
                        q = cnt["c"] % 2
                        cnt["c"] += 1
                        sl = slice(c0 - c_lo, c0 - c_lo + 512)
                        T1, T2, T3, T4, XIS = tt1[q], tt2[q], tt3[q], tt4[q], xis[q]
                        XRS = xrs[q]
                        P.op("act", lambda e, XIS=XIS, pi=pi: e.copy(XIS[:], pi[:, 0:512]), reads=[tpi], writes=[t_xis[q]])
                        P.op("act", lambda e, XRS=XRS, pr=pr: e.copy(XRS[:], pr[:, 0:512]), reads=[tpr], writes=[t_xrs[q]])
                        P.op("dve", lambda e, T1=T1, WR=WR, pr=pr, sl=sl: e.tensor_tensor(T1[:], WR[:, sl], pr[:, 0:512], ALU.mult),
                             reads=[t_Wr[p], tpr], writes=[t_t1[q]])
                        P.op("pool", lambda e, T2=T2, WI=WI, XIS=XIS, sl=sl: e.tensor_tensor(T2[:], WI[:, sl], XIS[:], ALU.mult),
                             reads=[t_Wi[p], t_xis[q]], writes=[t_t2[q]])
                        P.op("dve", lambda e, T3=T3, WR=WR, pi=pi, sl=sl: e.tensor_tensor(T3[:], WR[:, sl], pi[:, 0:512], ALU.mult),
                             reads=[t_Wr[p], tpi], writes=[t_t3[q]])
                        P.op("pool", lambda e, T4=T4, WI=WI, XRS=XRS, sl=sl: e.tensor_tensor(T4[:], WI[:, sl], XRS[:], ALU.mult),
                             reads=[t_Wi[p], t_xrs[q]], writes=[t_t4[q]])
                        P.op("dve", lambda e, XR=XR, T1=T1, T2=T2, sl=sl: e.tensor_tensor(XR[:, sl], T1[:], T2[:], ALU.subtract),
                             reads=[t_t1[q], t_t2[q]], writes=[t_xr[p]])
                        P.op("pool", lambda e, XI=XI, T3=T3, T4=T4, sl=sl: e.tensor_tensor(XI[:, sl], T3[:], T4[:], ALU.add),
                             reads=[t_t3[q], t_t4[q]], writes=[t_xi[p]])
                rb = msc["r"][:, gp:gp + 1].to_broadcast([128, HT])
                if hf == 0:
                    ini_r, ini_i, rd_prev = 0.0, 0.0, []
                else:
                    pp = 1 - p
                    ini_r, ini_i = ang[pp][:, HT - 1:HT], kfb[pp][:, HT - 1:HT]
                    rd_prev = [t_ang[pp], t_kfb[pp]]
                P.op("dve", lambda e, A=A, rb=rb, XR=XR, ini_r=ini_r: e.tensor_tensor_scan(A[:], rb, XR[:], ini_r, ALU.mult, ALU.add),
                     reads=[t_s, t_xr[p], t_sn[p], t_cs[p]] + rd_prev, writes=[t_ang[p]])
                P.op("dve", lambda e, Kf=Kf, rb=rb, XI=XI, ini_i=ini_i: e.tensor_tensor_scan(Kf[:], rb, XI[:], ini_i, ALU.mult, ALU.add),
                     reads=[t_s, t_xi[p]] + rd_prev, writes=[t_kfb[p]])
                HR, HI = A, Kf
                P.op("pool", lambda e, XR=XR, SN=SN, HI=HI: e.tensor_tensor(XR[:], SN[:], HI[:], ALU.mult),
                     reads=[t_sn[p], t_kfb[p]], writes=[t_xr[p]])
                P.op("dve", lambda e, WR=WR, CS=CS, HR=HR: e.tensor_tensor(WR[:], CS[:], HR[:], ALU.mult),
                     reads=[t_cs[p], t_ang[p]], writes=[t_Wr[p]])
                P.op("dve", lambda e, WR=WR, XR=XR, p=p: e.tensor_tensor(hrb[p][:], WR[:], XR[:], ALU.subtract),
                     reads=[t_Wr[p], t_xr[p]], writes=[t_hrb[p]])
                P.op("pool", lambda e, XI=XI, SN=SN, HR=HR: e.tensor_tensor(XI[:], SN[:], HR[:], ALU.mult),
                     reads=[t_sn[p], t_ang[p]], writes=[t_xi[p]])
                P.op("pool", lambda e, WI=WI, CS=CS, HI=HI: e.tensor_tensor(WI[:], CS[:], HI[:], ALU.mult),
                     reads=[t_cs[p], t_kfb[p]], writes=[t_Wi[p]])
                P.op("dve", lambda e, WI=WI, XI=XI, p=p: e.scalar_tensor_tensor(out=hib[p][:], in0=WI[:], scalar=-1.0, in1=XI[:],
                                                                                op0=ALU.mult, op1=ALU.subtract),
                     reads=[t_Wi[p], t_xi[p]], writes=[t_hib[p]])
                if hf == 1:
                    L1 = HT - 1
                    P.op("dve", lambda e, gp=gp, CS=CS, HR=HR: e.tensor_tensor(pst_re[:, gp:gp + 1], CS[:, L1:HT], HR[:, L1:HT], ALU.mult),
                         reads=[t_cs[p], t_ang[p]], writes=[t_pst])
                    P.op("dve", lambda e, SN=SN, HI=HI: e.tensor_tensor(sm[:, 0:1], SN[:, L1:HT], HI[:, L1:HT], ALU.mult),
                         reads=[t_sn[p], t_kfb[p]], writes=[t_sm])
                    P.op("dve", lambda e, gp=gp: e.tensor_tensor(pst_re[:, gp:gp + 1], pst_re[:, gp:gp + 1], sm[:, 0:1], ALU.subtract),
                         reads=[t_pst, t_sm], writes=[t_pst])
                    P.op("dve", lambda e, gp=gp, SN=SN, HR=HR: e.tensor_tensor(pst_im[:, gp:gp + 1], SN[:, L1:HT], HR[:, L1:HT], ALU.mult),
                         reads=[t_sn[p], t_ang[p]], writes=[t_pst])
                    P.op("dve", lambda e, CS=CS, HI=HI: e.tensor_tensor(sm[:, 0:1], CS[:, L1:HT], HI[:, L1:HT], ALU.mult),
                         reads=[t_cs[p], t_kfb[p]], writes=[t_sm])
                    P.op("dve", lambda e, gp=gp: e.tensor_tensor(pst_im[:, gp:gp + 1], pst_im[:, gp:gp + 1], sm[:, 0:1], ALU.add),
                         reads=[t_pst, t_sm], writes=[t_pst])
                for ci in range(HT // 512):
                    sl = slice(ci * 512, (ci + 1) * 512)
                    gsl = slice(c_lo + ci * 512, c_lo + (ci + 1) * 512)
                    b = cnt["y"] % 2
                    cnt["y"] += 1
                    P.op("pe", lambda e, mcre_=mcre_, b=b, sl=sl, p=p: e.matmul(pY[b][0:32, :], mcre_[:], hrb[p][:, sl],
                                                                          start=True, stop=False),
                         reads=[t_mw, t_hrb[p]], writes=[t_pY[b]])
                    P.op("pe", lambda e, mcim_=mcim_, b=b, sl=sl, p=p: e.matmul(pY[b][0:32, :], mcim_[:], hib[p][:, sl],
                                                                          start=False, stop=True),
                         reads=[t_mw, t_hib[p]], writes=[t_pY[b]])
                    P.op("dve", lambda e, gp=gp, b=b, sl=sl, gsl=gsl, p=p: e.scalar_tensor_tensor(
                        out=zst[p][:, sl], in0=u_f[:, gsl], scalar=mdcol[:, gp:gp + 1], in1=pY[b][0:32, :],
                        op0=ALU.mult, op1=ALU.add), reads=[t_u, t_s, t_pY[b]], writes=[t_z[p]])
                P.op("act", lambda e, p=p: e.activation(zst[p][:], zst[p][:], AF.Gelu), reads=[], writes=[t_z[p]])
                P.op("sp", lambda e, gp=gp, p=p, c_lo=c_lo: e.dma_start(out=k.ZTM[gp * 32:(gp + 1) * 32, c_lo:c_lo + HT], in_=zst[p][:]),
                     reads=[t_z[p]], writes=[k.t_ZT], dma=True, semkey=t_z[p], nowaw=True)

        rg = [[2 * i, 2 * i + 1] for i in range(N_CORES // 2)]
        for j in range(4):
            P.op("pool", lambda e, j=j: e.collective_compute(
                "AllGather", ALU.bypass, replica_groups=rg,
                ins=[k.ZTM[j * 128:(j + 1) * 128, :]], outs=[k.ZTG[j * 256:(j + 1) * 256, :]]),
                reads=[k.t_ZT], writes=[k.t_ZTG], nowaw=True, cc=True, semkey=k.t_ZTG)

        P.barrier(k.bar[:])
        us_f = u_f[:, :].rearrange("p (g t) -> p g t", t=TS)
        us_b = u_b[:, :].rearrange("p (g t) -> p g t", t=TS)
        t_us = T("us")
        P.op("sp", lambda e: e.dma_start(out=us_f, in_=k.PROJS[0:S5W, :].rearrange("(g q) t -> q g t", q=32)),
             reads=[k.t_PROJS], writes=[t_us], dma=True)
        P.op("act", lambda e: e.copy(us_b, us_f), reads=[t_us], writes=[t_us])
        for gp in range(NPAIR):
            b = cnt["x"] % 2
            cnt["x"] += 1
            pr, pi = pX[2 * b], pX[2 * b + 1]
            tpr, tpi = t_pX[2 * b], t_pX[2 * b + 1]
            colg = lambda n, gp=gp: sc[n][:, gp:gp + 1]
            P.op("pe", lambda e, gp=gp, pr=pr: e.matmul(pr[:, 0:TS], bre[:, gp * 128:(gp + 1) * 128], us_b[:, gp, :],
                                                        start=True, stop=True), reads=[t_w, t_us], writes=[tpr])
            P.op("pe", lambda e, gp=gp, pi=pi: e.matmul(pi[:, 0:TS], bim[:, gp * 128:(gp + 1) * 128], us_b[:, gp, :],
                                                        start=True, stop=True), reads=[t_w, t_us], writes=[tpi])
            P.op("dve", lambda e, pr=pr, gp=gp, colg=colg: e.tensor_scalar_mul(xs_re[:, gp, :], pr[:, 0:TS], colg("cre")),
                 reads=[tpr, t_s], writes=[t_xs])
            P.op("dve", lambda e, pi=pi, gp=gp, colg=colg: e.scalar_tensor_tensor(
                out=xs_re[:, gp, :], in0=pi[:, 0:TS], scalar=colg("cim"), in1=xs_re[:, gp, :],
                op0=ALU.mult, op1=ALU.subtract), reads=[tpi, t_s, t_xs], writes=[t_xs])
            P.op("dve", lambda e, gp=gp: e.tensor_scalar_mul(xs_re[:, gp, :], xs_re[:, gp, :], -1.0),
                 reads=[t_xs], writes=[t_xs])
            P.op("dve", lambda e, pi=pi, gp=gp, colg=colg: e.tensor_scalar_mul(xs_im[:, gp, :], pi[:, 0:TS], colg("cre")),
                 reads=[tpi, t_s], writes=[t_xs])
            P.op("dve", lambda e, pr=pr, gp=gp, colg=colg: e.scalar_tensor_tensor(
                out=xs_im[:, gp, :], in0=pr[:, 0:TS], scalar=colg("cim"), in1=xs_im[:, gp, :],
                op0=ALU.mult, op1=ALU.add), reads=[tpr, t_s, t_xs], writes=[t_xs])
            P.op("dve", lambda e, gp=gp: e.tensor_scalar_mul(zs[:, gp, :], us_f[:, gp, :], dcol[:, gp:gp + 1]),
                 reads=[t_us, t_c], writes=[t_zs])

        arb = sc["ar"][:].unsqueeze(2).to_broadcast([128, NPAIR, NSQ])
        aib = sc["ai"][:].unsqueeze(2).to_broadcast([128, NPAIR, NSQ])
        P.barrier(k.bar[:])
        class _A:
            def __init__(self, t):
                self.t = t

            def __getitem__(self, idx):
                return self.t[:].rearrange("p (g b) -> p g b", b=NSQ)
        w1 = _A(tt1[0]); w2 = _A(tt2[0]); t_w1 = T("w1")
        hn_re = _A(tt3[0]); hn_im = _A(tt4[0])
        xsr4 = xs_re[:].rearrange("p g (b t) -> p g b t", t=4)
        xsi4 = xs_im[:].rearrange("p g (b t) -> p g b t", t=4)
        hsr4 = hs_re[:].rearrange("p g (b t) -> p g b t", t=4)
        hsi4 = hs_im[:].rearrange("p g (b t) -> p g b t", t=4)
        cur = (h_re, h_im)
        nxt = (hn_re, hn_im)
        for t in range(4):
            cr, ci_ = cur
            nr_, ni_ = nxt
            P.op("dve", lambda e, cr=cr: e.tensor_tensor(w1[:], cr[:], arb, ALU.mult), reads=[t_h, t_s], writes=[t_w1])
            P.op("dve", lambda e, ci_=ci_: e.tensor_tensor(w2[:], ci_[:], aib, ALU.mult), reads=[t_h, t_s, t_w1], writes=[t_w1])
            P.op("dve", lambda e: e.tensor_tensor(w1[:], w1[:], w2[:], ALU.subtract), reads=[t_w1], writes=[t_w1])
            P.op("dve", lambda e, nr_=nr_, t=t: e.tensor_tensor(nr_[:], w1[:], xsr4[:, :, :, t], ALU.add),
                 reads=[t_w1, t_xs, t_h], writes=[t_h], nowaw=True)
            P.op("dve", lambda e, cr=cr: e.tensor_tensor(w1[:], cr[:], aib, ALU.mult), reads=[t_h, t_s, t_w1], writes=[t_w1])
            P.op("dve", lambda e, ci_=ci_: e.tensor_tensor(w2[:], ci_[:], arb, ALU.mult), reads=[t_h, t_s, t_w1], writes=[t_w1])
            P.op("dve", lambda e: e.tensor_tensor(w1[:], w1[:], w2[:], ALU.add), reads=[t_w1], writes=[t_w1])
            P.op("dve", lambda e, ni_=ni_, t=t: e.tensor_tensor(ni_[:], w1[:], xsi4[:, :, :, t], ALU.add),
                 reads=[t_w1, t_xs, t_h], writes=[t_h])
            P.op("pool", lambda e, nr_=nr_, t=t: e.tensor_copy(hsr4[:, :, :, t], nr_[:]), reads=[t_h], writes=[t_hs])
            P.op("pool", lambda e, ni_=ni_, t=t: e.tensor_scalar_mul(hsi4[:, :, :, t], ni_[:], -1.0), reads=[t_h, t_hs], writes=[t_hs])
            cur, nxt = nxt, cur
        fin_re, fin_im = cur
        for half in range(4):
            for gq in range(8):
                gp = half * 8 + gq
                P.op("pe", lambda e, gp=gp, gq=gq: e.matmul(pO[0:32, gq * TS:(gq + 1) * TS], cre[:, gp * 32:(gp + 1) * 32],
                                                            hs_re[:, gp, :], start=True, stop=False),
                     reads=[t_w, t_hs], writes=[t_pO])
                P.op("pe", lambda e, gp=gp, gq=gq: e.matmul(pO[0:32, gq * TS:(gq + 1) * TS], cim[:, gp * 32:(gp + 1) * 32],
                                                            hs_im[:, gp, :], start=False, stop=True),
                     reads=[t_w, t_hs], writes=[t_pO])
            P.op("dve", lambda e, half=half: e.tensor_tensor(
                zs[:, half * 8:(half + 1) * 8, :].rearrange("p g t -> p (g t)"),
                zs[:, half * 8:(half + 1) * 8, :].rearrange("p g t -> p (g t)"), pO[0:32, :], ALU.add),
                reads=[t_zs, t_pO], writes=[t_zs])
        P.op("act", lambda e: e.activation(zs[:], zs[:], AF.Gelu), reads=[t_zs], writes=[t_zs])
        P.op("sp", lambda e: e.dma_start(out=k.ZS.rearrange("(gp q) t -> q gp t", q=32), in_=zs[:]),
             reads=[t_zs], writes=[k.t_ZS], dma=True, semkey=t_zs)
        st_out = st_in; t_sto = t_stin
        for (src, dst) in ((fin_re, k.s_s5re), (fin_im, k.s_s5im)):
            for hh in range(2):
                for g0 in range(0, 16, 4):
                    for j in range(4):
                        P.op("pe", lambda e, g0=g0, j=j, src=src, hh=hh: e.transpose(
                            pT[0:NSQ, j * 128:(j + 1) * 128], src[:, hh * 16 + g0 + j, :], k.ident[:]),
                            reads=[t_h, k.t_ident], writes=[t_pT])
                    P.op("dve", lambda e, g0=g0: e.tensor_copy(st_out[:, g0 * 128:(g0 + 4) * 128], pT[0:NSQ, :]),
                         reads=[t_pT], writes=[t_sto])
                o = P.op("sp", lambda e, dst=dst, hh=hh: e.dma_start(out=dst[:, hh * 2048:(hh + 1) * 2048], in_=st_out[:]),
                         reads=[t_sto], dma=True, semkey=t_sto)
                P.final.append(o)
        for (src, dst) in ((pst_re, k.p_s5re), (pst_im, k.p_s5im)):
            o = P.op("sp", lambda e, src=src, dst=dst: e.dma_start(out=dst, in_=src[:]), reads=[t_pst], dma=True, semkey=t_pst)
            P.final.append(o)


def phase_glu(k):
    nc, P = k.nc, k.P
    with contextlib.ExitStack() as es:
        sb = lambda name, shape, dt=F32: es.enter_context(nc.sbuf_tensor("gl_" + name, list(shape), dt))
        ps = lambda name, shape, dt=F32: es.enter_context(nc.psum_tensor("gl_" + name, list(shape), dt))
        gw = sb("gw", [128, 8, S5W], BF16); t_gw = Tok("gw")
        glub = sb("glub", [128, 8]); t_gb = Tok("glub")
        P.op("pool", lambda e: e.dma_start(out=gw[:], in_=k.glu_w.rearrange("(kc p) c -> p kc c", p=128)),
             writes=[t_gw], dma=True)
        P.op("sp", lambda e: e.dma_start(out=glub[:], in_=k.c_glub), writes=[t_gb], dma=True)
        zb = [sb("zb%d" % i, [128, 8, 512], BF16) for i in range(2)]; t_zb = [Tok("zb%d" % i) for i in range(2)]
        zf = [sb("zf%d" % i, [128, 8, 512]) for i in range(2)]; t_zf = [Tok("zf%d" % i) for i in range(2)]
        sg = [sb("sg%d" % i, [128, 512]) for i in range(2)]; t_sg = [Tok("gsg%d" % i) for i in range(2)]
        yb = [sb("yb%d" % i, [128, 512], BF16) for i in range(2)]; t_yb = [Tok("yb%d" % i) for i in range(2)]
        pp = [ps("pp%d" % i, [128, 512]) for i in range(2)]; t_pp = [Tok("pp%d" % i) for i in range(2)]
        zg = k.ZTG.rearrange("(j h i) t -> h i j t", h=2, i=128)
        zsv = k.ZS.rearrange("(kc p) t -> p kc t", p=128)
        n = 0
        for ci, (c0, cn) in enumerate([(i * 512, 512) for i in range(TPC // 512)] + [(TPC, TS)]):
            bi = ci % 2
            if c0 < TPC:
                for hh in range(2):
                    P.op("pool", lambda e, bi=bi, c0=c0, cn=cn, hh=hh: e.dma_start(
                        out=zb[bi][:, hh * 4:(hh + 1) * 4, 0:cn], in_=zg[hh][:, :, c0:c0 + cn]),
                        reads=[k.t_ZTG], writes=[t_zb[bi]], dma=True, nowaw=(hh == 1))
                    P.op("sp", lambda e, bi=bi, c0=c0, cn=cn, hh=hh: e.dma_start(
                        out=zf[bi][:, hh * 4:(hh + 1) * 4, 0:cn], in_=zg[hh][:, :, c0:c0 + cn]),
                        reads=[k.t_ZTG], writes=[t_zf[bi]], dma=True, nowaw=(hh == 1))
            else:
                P.op("pool", lambda e, bi=bi, cn=cn: e.dma_start(out=zb[bi][:, :, 0:cn], in_=zsv[:, :, 0:cn]),
                     reads=[k.t_ZS], writes=[t_zb[bi]], dma=True)
                P.op("sp", lambda e, bi=bi, cn=cn: e.dma_start(out=zf[bi][:, :, 0:cn], in_=zsv[:, :, 0:cn]),
                     reads=[k.t_ZS], writes=[t_zf[bi]], dma=True)
            for ot in range(8):
                b = n % 2
                n += 1
                for kc in range(8):
                    P.op("pe", lambda e, b=b, bi=bi, kc=kc, ot=ot, cn=cn: e.matmul(
                        pp[b][:, 0:cn], gw[:, kc, ot * 128:(ot + 1) * 128], zb[bi][:, kc, 0:cn],
                        start=(kc == 0), stop=(kc == 7)), reads=[t_gw, t_zb[bi]], writes=[t_pp[b]])
                P.op("act", lambda e, b=b, ot=ot, cn=cn: e.activation(sg[b][:, 0:cn], pp[b][:, 0:cn], AF.Sigmoid,
                                                                      bias=glub[:, ot:ot + 1], scale=1.0),
                     reads=[t_pp[b], t_gb], writes=[t_sg[b]])
                P.op("dve", lambda e, b=b, bi=bi, ot=ot, cn=cn: e.tensor_tensor(yb[b][:, 0:cn], sg[b][:, 0:cn],
                                                                                zf[bi][:, ot, 0:cn], ALU.mult),
                     reads=[t_sg[b], t_zf[bi]], writes=[t_yb[b]])
                P.op("sp", lambda e, b=b, ot=ot, c0=c0, cn=cn: e.dma_start(
                    out=k.YT[ot * 128:(ot + 1) * 128, c0:c0 + cn], in_=yb[b][:, 0:cn]),
                    reads=[t_yb[b]], writes=[k.t_YT], dma=True, semkey=t_yb[b], nowaw=True)


def phase_gdn(k):
    nc, P = k.nc, k.P
    TP = TPC
    NCH = TP // 128 + 1
    with contextlib.ExitStack() as es:
        sb = lambda name, shape, dt=F32: es.enter_context(nc.sbuf_tensor("g_" + name, list(shape), dt))
        ps = lambda name, shape, dt=F32: es.enter_context(nc.psum_tensor("g_" + name, list(shape), dt))
        T = lambda name: Tok(name)
        t_c = T("gconst")
        convw = sb("convw", [128, 96]); alog = sb("alog", [128, NH]); dtb = sb("dtb", [128, NH]); normw = sb("normw", [128, 1])
        for dst, src in ((convw, k.c_convw), (alog, k.c_alog), (dtb, k.c_dtb), (normw, k.c_normw)):
            P.op("sp", lambda e, dst=dst, src=src: e.dma_start(out=dst[:], in_=src), writes=[t_c], dma=True, nowaw=True)
        nea = sb("nea", [128, NH])
        P.op("act", lambda e: e.activation(nea[:], alog[:], AF.Exp), reads=[t_c], writes=[t_c])
        P.op("dve", lambda e: e.tensor_scalar_mul(nea[:], nea[:], -1.0), reads=[t_c], writes=[t_c])
        t_m = T("masks")
        ones = sb("ones", [128, 128]); zeros = sb("zeros", [128, 128])
        TriU = sb("TriU", [128, 128]); MAs = sb("MAs", [128, 128]); MB = sb("MB", [128, 128]); MBs = sb("MBs", [128, 128])
        TriUS = sb("TriUS", [64, 64]); MAsS = sb("MAsS", [64, 64]); MBS = sb("MBS", [64, 64]); MBsS = sb("MBsS", [64, 64])
        Emat = sb("Emat", [16, 64]); SSm = sb("SSm", [64, 64]); tmpm = sb("tmpm", [64, 64])
        Msel = sb("Msel", [128, NSQ, TS]); Msel2 = sb("Msel2", [TS, NSQ, 128])
        mo = lambda fn, r=(): P.op("pool", fn, reads=[t_m] + list(r), writes=[t_m])
        mo(lambda e: e.memset(ones[:], 1.0)); mo(lambda e: e.memset(zeros[:], 0.0))
        mo(lambda e: e.affine_select(out=TriU[:], in_=ones[:], pattern=[[1, 128]], compare_op=ALU.is_ge, fill=0.0,
                                     base=0, channel_multiplier=-1))
        mo(lambda e: e.affine_select(out=MAs[:], in_=zeros[:], pattern=[[-1, 128]], compare_op=ALU.is_gt, fill=BIG,
                                     base=0, channel_multiplier=1))
        mo(lambda e: e.affine_select(out=MB[:], in_=zeros[:], pattern=[[1, 128]], compare_op=ALU.is_ge, fill=-BIG,
                                     base=0, channel_multiplier=-1))
        mo(lambda e: e.affine_select(out=MBs[:], in_=zeros[:], pattern=[[1, 128]], compare_op=ALU.is_gt, fill=-BIG,
                                     base=0, channel_multiplier=-1))
        mo(lambda e: e.affine_select(out=Emat[:], in_=ones[0:16, 0:64], pattern=[[1, 64]], compare_op=ALU.is_ge, fill=0.0,
                                     base=0, channel_multiplier=-4))
        mo(lambda e: e.affine_select(out=Emat[:], in_=Emat[:], pattern=[[-1, 64]], compare_op=ALU.is_ge, fill=0.0,
                                     base=3, channel_multiplier=4))
        mo(lambda e: e.memset(Msel[:], 1.0)); mo(lambda e: e.memset(Msel2[:], 1.0))
        mo(lambda e: e.affine_select(out=Msel[:], in_=Msel[:], pattern=[[-4, NSQ], [1, TS]], compare_op=ALU.is_ge,
                                     fill=0.0, base=0, channel_multiplier=0))
        mo(lambda e: e.affine_select(out=Msel[:], in_=Msel[:], pattern=[[4, NSQ], [-1, TS]], compare_op=ALU.is_ge,
                                     fill=0.0, base=3, channel_multiplier=0))
        mo(lambda e: e.affine_select(out=Msel2[:], in_=Msel2[:], pattern=[[-4, NSQ], [0, 128]], compare_op=ALU.is_ge,
                                     fill=0.0, base=0, channel_multiplier=1))
        mo(lambda e: e.affine_select(out=Msel2[:], in_=Msel2[:], pattern=[[4, NSQ], [0, 128]], compare_op=ALU.is_ge,
                                     fill=0.0, base=3, channel_multiplier=-1))
        pS = [ps("pS%d" % i, [128, 512]) for i in range(4)]
        t_pS = [[Tok("pS%d" % i, excl=True)] * 4 for i in range(4)]
        pQ = [ps("pQ%d" % i, [128, 512]) for i in range(2)]
        t_pQ = [[Tok("pQ%d" % i, excl=True)] * 4 for i in range(2)]
        pM = ps("pM", [128, 512]); t_pM = Tok("pM", excl=True)
        pN = ps("pN", [128, 512]); t_pN = Tok("pN", excl=True)
        P.op("pe", lambda e: e.matmul(pM[0:64, 0:64], Emat[:], Emat[:], start=True, stop=True), reads=[t_m], writes=[t_pM])
        P.op("dve", lambda e: e.tensor_copy(SSm[:], pM[0:64, 0:64]), reads=[t_pM], writes=[t_m])
        P.op("dve", lambda e: e.tensor_scalar(tmpm[:], SSm[:], -BIG, BIG, ALU.mult, ALU.add), reads=[t_m], writes=[t_m])
        P.op("dve", lambda e: e.tensor_tensor(MAsS[:], MAs[0:64, 0:64], tmpm[:], ALU.max), reads=[t_m], writes=[t_m])
        P.op("dve", lambda e: e.tensor_scalar_mul(tmpm[:], tmpm[:], -1.0), reads=[t_m], writes=[t_m])
        P.op("dve", lambda e: e.tensor_tensor(MBS[:], MB[0:64, 0:64], tmpm[:], ALU.min), reads=[t_m], writes=[t_m])
        P.op("dve", lambda e: e.tensor_tensor(MBsS[:], MBs[0:64, 0:64], tmpm[:], ALU.min), reads=[t_m], writes=[t_m])
        P.op("dve", lambda e: e.tensor_tensor(TriUS[:], TriU[0:64, 0:64], SSm[:], ALU.mult), reads=[t_m], writes=[t_m])

        NTL = 17
        t_g = T("gates")
        tmpa = sb("tmpa", [128, NTL, NH]); GK = sb("GK", [128, NTL, NH]); BETA = sb("BETA", [128, NTL, NH])
        GC = sb("GC", [128, NTL, NH]); EG = sb("EG", [128, NTL, NH]); BEG = sb("BEG", [128, NTL, NH]); NB = sb("NB", [128, NTL, NH])
        bdk_all = list(k.t_BDK)
        P.op("sp", lambda e: e.dma_start(out=k.BDK[:, 0:16, :], in_=k.BDG.rearrange("(t p) c -> p t c", p=128)),
             reads=[k.t_PROJT], writes=bdk_all, dma=True, semkey=bdk_all[0])
        P.op("sp", lambda e: e.dma_start(out=k.BDK[0:TS, 16, :], in_=k.BDS), reads=[k.t_BDS], writes=bdk_all,
             dma=True, semkey=bdk_all[0], nowaw=True)
        dtb_b = dtb[:].unsqueeze(1).to_broadcast([128, NTL, NH])
        nea_b = nea[:].unsqueeze(1).to_broadcast([128, NTL, NH])
        P.op("dve", lambda e: e.tensor_tensor(tmpa[:], k.BDK[:, :, 8:16], dtb_b, ALU.add), reads=bdk_all + [t_c], writes=[t_g])
        P.op("act", lambda e: e.activation(tmpa[:], tmpa[:], AF.Exp), reads=[t_g], writes=[t_g])
        P.op("act", lambda e: e.activation(tmpa[:], tmpa[:], AF.Ln, bias=1.0, scale=1.0), reads=[t_g], writes=[t_g])
        P.op("dve", lambda e: e.tensor_tensor(GK[:], tmpa[:], nea_b, ALU.mult), reads=[t_g, t_c], writes=[t_g])
        P.op("act", lambda e: e.activation(BETA[:], k.BDK[:, :, 0:8], AF.Sigmoid), reads=bdk_all + [t_g], writes=[t_g])
        for tl in range(16):
            P.op("pe", lambda e, tl=tl: e.matmul(pM[:, tl * 8:(tl + 1) * 8], TriU[:], GK[:, tl, :], start=True, stop=True),
                 reads=[t_m, t_g], writes=[t_pM])
        P.op("pe", lambda e: e.matmul(pM[0:64, 128:136], TriUS[:], GK[0:64, 16, :], start=True, stop=True),
             reads=[t_m, t_g], writes=[t_pM])
        P.op("pe", lambda e: e.matmul(pM[0:64, 136:144], SSm[:], GK[0:64, 16, :], start=True, stop=True),
             reads=[t_m, t_g], writes=[t_pM])
        GLS = sb("GLS", [64, NH])
        P.op("dve", lambda e: e.tensor_copy(GLS[:], pM[0:64, 136:144]), reads=[t_pM, t_g], writes=[t_g])
        P.op("pool", lambda e: e.memset(GC[:], 0.0), reads=[t_g], writes=[t_g])
        P.op("dve", lambda e: e.tensor_copy(GC[:, 0:16, :].rearrange("p a b -> p (a b)"), pM[:, 0:128]), reads=[t_pM, t_g], writes=[t_g])
        P.op("dve", lambda e: e.tensor_copy(GC[0:64, 16, :], pM[0:64, 128:136]), reads=[t_pM, t_g], writes=[t_g])
        NGC = sb("NGC", [128, NTL, NH])
        P.op("dve", lambda e: e.tensor_scalar_mul(NGC[:], GC[:], -1.0), reads=[t_g], writes=[t_g])
        P.op("act", lambda e: e.activation(EG[:], GC[:], AF.Exp), reads=[t_g], writes=[t_g])
        P.op("dve", lambda e: e.tensor_tensor(BEG[:], BETA[:], EG[:], ALU.mult), reads=[t_g], writes=[t_g])
        P.op("dve", lambda e: e.tensor_scalar_mul(NB[:], BETA[:], -1.0), reads=[t_g], writes=[t_g])

        import os
        GSTOP = int(os.environ.get("GDN_STOP", "99"))
        if GSTOP <= 1:
            return
        HM = NH // 2
        gt_all = {"GK": GK, "BETA": BETA, "GC": GC, "NGC": NGC, "NB": NB, "BEG": BEG}
        gt_m = {}
        for nm_, tl_ in gt_all.items():
            tm_ = sb("m_" + nm_, [128, NTL, HM])
            gt_m[nm_] = tm_
            P.op("dve", lambda e, tm_=tm_, tl_=tl_: e.tensor_scalar_mul(tm_[:], tl_[:, :, 0:HM], k.sel_sb[:, 0:1]),
                 reads=[t_g, k.t_sel], writes=[t_g])
            P.op("dve", lambda e, tm_=tm_, tl_=tl_: e.scalar_tensor_tensor(out=tm_[:], in0=tl_[:, :, HM:NH], scalar=k.sel_sb[:, 1:2],
                                                                           in1=tm_[:], op0=ALU.mult, op1=ALU.add),
                 reads=[t_g, k.t_sel], writes=[t_g])
        convwm = sb("convwm", [128, 3 * HM * 4])
        cv4 = convw[:].rearrange("p (x h j) -> p x h j", x=3, h=NH)
        cm4 = convwm[:].rearrange("p (x h j) -> p x h j", x=3, h=HM)
        for x in range(3):
            P.op("dve", lambda e, x=x: e.tensor_scalar_mul(cm4[:, x], cv4[:, x, 0:HM, :], k.sel_sb[:, 0:1]),
                 reads=[t_c, k.t_sel], writes=[t_c])
            P.op("dve", lambda e, x=x: e.scalar_tensor_tensor(out=cm4[:, x], in0=cv4[:, x, HM:NH, :], scalar=k.sel_sb[:, 1:2],
                                                              in1=cm4[:, x], op0=ALU.mult, op1=ALU.add),
                 reads=[t_c, k.t_sel], writes=[t_c])
        XP = sb("XP", [128, 3 + NT]); t_XP = T("XP")
        XS = sb("XS", [128, NSQ, 7]); t_XS = T("XS")
        CV = [sb("CV%d" % i, [128, NT]) for i in range(3)]; t_CV = [T("CV%d" % i) for i in range(3)]
        SG = sb("SG", [128, NT]); t_SG = T("SG")
        RN = [sb("RN%d" % i, [128, 512]) for i in range(2)]; t_RN = [T("RN%d" % i) for i in range(2)]
        YG = sb("YG", [128, NT], BF16); t_YG = T("YG")
        P.op("pool", lambda e: e.memset(XP[:, 0:3], 0.0), writes=[t_XP])
        WKT = sb("WKT", [128, NCH, 128]); QGT = sb("QGT", [128, NCH, 128]); ATT = sb("ATT", [128, NCH, 128])
        KD = sb("KD", [128, NCH, 128]); UU = sb("UU", [128, NCH, 128]); GLA = sb("GLA", [128, NCH])
        t_co = [T("co%d" % c) for c in range(NCH)]
        NSL = 4
        wk = [{n: sb("w%d_%s" % (sl, n), [128, 128]) for n in
               ("gmat", "bmat", "gd", "tA", "tB", "tC", "Ds", "DsT", "DT", "W", "Na", "Nb", "NTa", "NTb", "TTa", "TTb", "kbg", "vb", "EGR")}
              for sl in range(NSL)]
        wc = [sb("wc%d" % sl, [128, 8]) for sl in range(NSL)]
        t_wk = [{n: T("w%d_%s" % (sl, n)) for n in list(wk[0].keys()) + ["wc"]} for sl in range(NSL)]
        Sst = sb("Sst", [128, 128]); t_S = T("S")
        VN = sb("VN", [128, 128]); t_VN = T("VN")
        ON = sb("ON", [128, 128]); t_ON = T("ON")
        sq = sb("sqs", [128, 8]); t_sq = T("sq"); junk = sb("junk", [128, 128]); t_junk = T("junk")
        Sall = sb("Sall", [128, NSQ, 128]); t_Sall = T("Sall")
        Snew = sb("Snew", [128, NSQ, 128]); t_Snew = T("Snew")
        WKTm = sb("WKTm", [128, NSQ, TS]); QGTm = sb("QGTm", [128, NSQ, TS]); KDm = sb("KDm", [TS, NSQ, 128]); t_mk = T("mk")
        cntq = {"q": 0, "rn": 0}

        def chunk_pre(gt, hi, c, sl):
            n = 128 if c < 16 else TS
            t0 = c * 128
            w = wk[sl]; tw = t_wk[sl]; pq = pS[sl]; tq = t_pS[sl]
            QH, KH, VS = CV[0], CV[1], CV[2]
            qc, kc, vc = QH[:, t0:t0 + n], KH[:, t0:t0 + n], VS[:, t0:t0 + n]
            if c < 16:
                tri, mas, mb, mbs = TriU[:], MAs[:], MB[:], MBs[:]
            else:
                tri, mas, mb, mbs = TriUS[:], MAsS[:], MBS[:], MBsS[:]
            gcol = gt["GC"][0:n, c, hi:hi + 1]
            cols = wc[sl]
            steps = []
            def s0():
                P.op("dve", lambda e: e.tensor_scalar_mul(w["gmat"][0:n, :], ones[0:n, :], gt["GK"][0:n, c, hi:hi + 1]),
                     reads=[t_g, t_m], writes=[tw["gmat"]])
                P.op("dve", lambda e: e.tensor_scalar_mul(w["bmat"][0:n, 0:n], ones[0:n, 0:n], gt["BETA"][0:n, c, hi:hi + 1]),
                     reads=[t_g, t_m], writes=[tw["bmat"]])
                P.op("pe", lambda e: e.matmul(pq[:, 0:n], w["gmat"][0:n, :], tri, start=True, stop=True),
                     reads=[tw["gmat"], t_m], writes=[tq[0]])
                P.op("pe", lambda e: e.matmul(pq[0:n, 128:128 + n], w["bmat"][0:n, 0:n], k.ident[0:n, 0:n], start=True, stop=True),
                     reads=[tw["bmat"], k.t_ident], writes=[tq[1]])
                P.op("pe", lambda e: e.matmul(pq[0:n, 256:256 + n], kc, kc, start=True, stop=True),
                     reads=[t_CV[1]], writes=[tq[2]])
                P.op("pe", lambda e: e.matmul(pq[0:n, 384:384 + n], kc, qc, start=True, stop=True),
                     reads=[t_CV[1], t_CV[0]], writes=[tq[3]])
            steps.append(s0)
            def s1():
                _k = [0]; _lim = int(os.environ.get('S1_OPS', '99'))
                GB = pq[0:n, 0:n]
                ngcol = gt["NGC"][0:n, c, hi:hi + 1]
                _k[0] += 1
                if _k[0] > _lim: return
                P.op("dve", lambda e: e.tensor_scalar_add(w["gd"][0:n, 0:n], GB, ngcol),
                     reads=[tq[0], t_g], writes=[tw["gd"]])
                P.op("dve", lambda e: e.tensor_tensor(w["tA"][0:n, 0:n], w["gd"][0:n, 0:n], mas, ALU.max),
                     reads=[tw["gd"], t_m], writes=[tw["tA"]])
                P.op("act", lambda e: e.activation(w["Ds"][0:n, 0:n], w["tA"][0:n, 0:n], AF.Exp, scale=-1.0),
                     reads=[tw["tA"]], writes=[tw["Ds"]])
                _k[0] += 1
                if _k[0] > _lim: return
                P.op("dve", lambda e: e.tensor_tensor(w["tB"][0:n, 0:n], w["gd"][0:n, 0:n], mbs, ALU.min),
                     reads=[tw["gd"], t_m], writes=[tw["tB"]])
                P.op("act", lambda e: e.activation(w["DsT"][0:n, 0:n], w["tB"][0:n, 0:n], AF.Exp),
                     reads=[tw["tB"]], writes=[tw["DsT"]])
                P.op("dve", lambda e: e.tensor_tensor(w["tC"][0:n, 0:n], w["gd"][0:n, 0:n], mb, ALU.min),
                     reads=[tw["gd"], t_m], writes=[tw["tC"]])
                P.op("act", lambda e: e.activation(w["DT"][0:n, 0:n], w["tC"][0:n, 0:n], AF.Exp),
                     reads=[tw["tC"]], writes=[tw["DT"]])
                _k[0] += 1
                if _k[0] > _lim: return
                P.op("act", lambda e: e.activation(w["EGR"][:, 0:n], pq[:, 0:n], AF.Exp), reads=[tq[0]], writes=[tw["EGR"]])
                if c < 16:
                    P.op("dve", lambda e: e.tensor_copy(cols[:, 0:1], pq[:, n - 1:n]), reads=[tq[0]], writes=[tw["wc"]])
                else:
                    P.op("dve", lambda e: e.tensor_copy(cols[0:n, 0:1], GLS[:, hi:hi + 1]), reads=[t_g], writes=[tw["wc"]])
                _k[0] += 1
                if _k[0] > _lim: return
                P.op("act", lambda e: e.activation(cols[0:n, 1:2], gcol, AF.Exp, bias=cols[0:n, 0:1], scale=-1.0),
                     reads=[tw["wc"], t_g], writes=[tw["wc"]])
                _k[0] += 1
                if _k[0] > _lim: return
                P.op("dve", lambda e: e.tensor_copy(GLA[:, c:c + 1], w["EGR"][:, n - 1:n]), reads=[tw["EGR"]], writes=[t_co[c]])
                _k[0] += 1
                if _k[0] > _lim: return
                P.op("dve", lambda e: e.scalar_tensor_tensor(out=w["Na"][0:n, 0:n], in0=pq[0:n, 256:256 + n], scalar=gt["NB"][0:n, c, hi:hi + 1],
                                                             in1=w["Ds"][0:n, 0:n], op0=ALU.mult, op1=ALU.mult),
                     reads=[tq[2], t_g, tw["Ds"]], writes=[tw["Na"]])
                _k[0] += 1
                if _k[0] > _lim: return
                P.op("dve", lambda e: e.scalar_tensor_tensor(out=w["W"][0:n, 0:n], in0=pq[0:n, 128:128 + n], scalar=-1.0,
                                                             in1=w["DsT"][0:n, 0:n], op0=ALU.mult, op1=ALU.mult),
                     reads=[tq[1], tw["DsT"]], writes=[tw["W"]])
                _k[0] += 1
                if _k[0] > _lim: return
                P.op("dve", lambda e: e.tensor_tensor(w["NTa"][0:n, 0:n], pq[0:n, 256:256 + n], w["W"][0:n, 0:n], ALU.mult),
                     reads=[tq[2], tw["W"]], writes=[tw["NTa"]])
                _k[0] += 1
                if _k[0] > _lim: return
                P.op("dve", lambda e: e.tensor_tensor(ATT[0:n, c, 0:n], pq[0:n, 384:384 + n], w["DT"][0:n, 0:n], ALU.mult),
                     reads=[tq[3], tw["DT"]], writes=[t_co[c]])
                _k[0] += 1
                if _k[0] > _lim: return
                P.op("pool", lambda e: e.tensor_tensor(w["TTa"][0:n, 0:n], w["NTa"][0:n, 0:n], k.ident[0:n, 0:n], ALU.add),
                     reads=[tw["NTa"], k.t_ident], writes=[tw["TTa"]])
                _k[0] += 1
                if _k[0] > _lim: return
                P.op("pool", lambda e: e.tensor_tensor(QGT[:, c, 0:n], qc, w["EGR"][:, 0:n], ALU.mult),
                     reads=[t_CV[0], tw["EGR"]], writes=[t_co[c]])
            steps.append(s1)
            def s2():
                P.op("pe", lambda e: e.matmul(pq[0:n, 0:128], kc, k.ident[:], start=True, stop=True),
                     reads=[t_CV[1], k.t_ident], writes=[tq[0]])
                P.op("pe", lambda e: e.matmul(pq[0:n, 128:256], vc, k.ident[:], start=True, stop=True),
                     reads=[t_CV[2], k.t_ident], writes=[tq[1]])
                P.op("dve", lambda e: e.tensor_scalar_mul(w["kbg"][0:n, :], pq[0:n, 0:128], gt["BEG"][0:n, c, hi:hi + 1]),
                     reads=[tq[0], t_g], writes=[tw["kbg"]])
                P.op("dve", lambda e: e.tensor_scalar_mul(KD[0:n, c, :], pq[0:n, 0:128], cols[0:n, 1:2]),
                     reads=[tq[0], tw["wc"]], writes=[t_co[c]])
                P.op("dve", lambda e: e.tensor_scalar_mul(w["vb"][0:n, :], pq[0:n, 128:256], gt["BETA"][0:n, c, hi:hi + 1]),
                     reads=[tq[1], t_g], writes=[tw["vb"]])
            steps.append(s2)
            L = 6 if c < 16 else 1
            names = [("Na", "NTa", "TTa"), ("Nb", "NTb", "TTb")]
            for lv in range(1, L + 1):
                def sl_a(lv=lv):
                    pn, pnt, ptt = names[(lv - 1) % 2]
                    cn_, cnt_, ctt = names[lv % 2]
                    P.op("pe", lambda e: e.matmul(pq[0:n, 256:256 + n], w[pnt][0:n, 0:n], w[pn][0:n, 0:n], start=True, stop=True),
                         reads=[tw[pn], tw[pnt]], writes=[tq[2]])
                    if lv < L:
                        P.op("pe", lambda e: e.matmul(pq[0:n, 384:384 + n], w[pn][0:n, 0:n], w[pnt][0:n, 0:n], start=True, stop=True),
                             reads=[tw[pn], tw[pnt]], writes=[tq[3]])
                    P.op("act", lambda e: e.copy(w[cn_][0:n, 0:n], pq[0:n, 256:256 + n]), reads=[tq[2]], writes=[tw[cn_]])
                    if lv < L:
                        P.op("dve", lambda e: e.tensor_copy(w[cnt_][0:n, 0:n], pq[0:n, 384:384 + n]), reads=[tq[3]], writes=[tw[cnt_]])
                def sl_b(lv=lv):
                    pn, pnt, ptt = names[(lv - 1) % 2]
                    cn_, cnt_, ctt = names[lv % 2]
                    P.op("pe", lambda e: e.matmul(pq[0:n, 0:n], w[cn_][0:n, 0:n], w[ptt][0:n, 0:n], start=True, stop=True),
                         reads=[tw[cn_], tw[ptt]], writes=[tq[0]])
                    P.op("dve", lambda e: e.tensor_tensor(w[ctt][0:n, 0:n], w[ptt][0:n, 0:n], pq[0:n, 0:n], ALU.add),
                         reads=[tq[0], tw[ptt]], writes=[tw[ctt]])
                steps.append(sl_a)
                steps.append(sl_b)
            def sf():
                ftt = names[L % 2][2]
                P.op("pe", lambda e: e.matmul(pq[0:n, 128:256], w[ftt][0:n, 0:n], w["vb"][0:n, :], start=True, stop=True),
                     reads=[tw[ftt], tw["vb"]], writes=[tq[1]])
                P.op("pe", lambda e: e.matmul(pq[:, 256:256 + n], w["kbg"][0:n, :], w[ftt][0:n, 0:n], start=True, stop=True),
                     reads=[tw[ftt], tw["kbg"]], writes=[tq[2]])
                P.op("act", lambda e: e.copy(UU[0:n, c, :], pq[0:n, 128:256]), reads=[tq[1]], writes=[t_co[c]])
                P.op("dve", lambda e: e.tensor_copy(WKT[:, c, 0:n], pq[:, 256:256 + n]), reads=[tq[2]], writes=[t_co[c]])
            steps.append(sf)
            return steps

        def out_norm(h, c, n, opsum, t_op):
            t0 = c * 128
            P.op("act", lambda e: e.activation(junk[0:n, :], opsum, AF.Square, accum_out=sq[0:n, 0:1]),
                 reads=[t_op], writes=[t_junk, t_sq])
            P.op("act", lambda e: e.activation(sq[0:n, 1:2], sq[0:n, 0:1], AF.Sqrt, bias=k.eps_nm[0:n, :], scale=1.0 / 128),
                 reads=[t_sq, k.t_eps], writes=[t_sq])
            P.op("dve", lambda e: e.reciprocal(sq[0:n, 2:3], sq[0:n, 1:2]), reads=[t_sq], writes=[t_sq])
            P.op("dve", lambda e: e.tensor_scalar_mul(ON[0:n, :], opsum, sq[0:n, 2:3]), reads=[t_op, t_sq], writes=[t_ON])
            P.op("pe", lambda e: e.matmul(pN[:, 0:n], ON[0:n, :], k.ident[0:n, 0:n], start=True, stop=True),
                 reads=[t_ON, k.t_ident], writes=[t_pN])
            P.op("dve", lambda e: e.scalar_tensor_tensor(out=YG[:, t0:t0 + n], in0=pN[:, 0:n], scalar=normw[:, 0:1],
                                                         in1=SG[:, t0:t0 + n], op0=ALU.mult, op1=ALU.mult),
                 reads=[t_pN, t_c, t_SG], writes=[t_YG])

        def l2norm_qk(cks):
            for x, scale in ((0, 128.0 ** -0.5), (1, 1.0)):
                SQ = XP[:, 3:3 + NT]
                c_lo, c_hi = cks[0][0], cks[-1][0] + cks[-1][1]
                P.op("act", lambda e, x=x, SQ=SQ, c_lo=c_lo, c_hi=c_hi: e.activation(SQ[:, c_lo:c_hi], CV[x][:, c_lo:c_hi], AF.Square),
                     reads=[t_CV[x]], writes=[t_XP])
                for (c0, cn) in cks:
                    ri = cntq["rn"] % 2
                    cntq["rn"] += 1
                    P.op("pe", lambda e, SQ=SQ, c0=c0, cn=cn: e.matmul(pN[:, 0:cn], ones[:], SQ[:, c0:c0 + cn], start=True, stop=True),
                         reads=[t_XP, t_m], writes=[t_pN])
                    P.op("act", lambda e, ri=ri, cn=cn: e.activation(RN[ri][:, 0:cn], pN[:, 0:cn], AF.Sqrt, bias=k.eps_nm[:, :], scale=1.0),
                         reads=[t_pN, k.t_eps], writes=[t_RN[ri]])
                    P.op("dve", lambda e, ri=ri, cn=cn: e.reciprocal(RN[ri][:, 0:cn], RN[ri][:, 0:cn]), reads=[t_RN[ri]], writes=[t_RN[ri]])
                    P.op("dve", lambda e, x=x, ri=ri, c0=c0, cn=cn, scale=scale: e.scalar_tensor_tensor(
                        out=CV[x][:, c0:c0 + cn], in0=CV[x][:, c0:c0 + cn], scalar=scale, in1=RN[ri][:, 0:cn],
                        op0=ALU.mult, op1=ALU.mult), reads=[t_CV[x], t_RN[ri]], writes=[t_CV[x]])

        def run_chunks(gt, hi, cs, g0):
            plans = [chunk_pre(gt, hi, c, c - g0) for c in cs]
            for si in range(max(len(p) for p in plans)):
                for p in plans:
                    if si < len(p):
                        p[si]()

        s0c, s1c = k.sel_sb[:, 0:1], k.sel_sb[:, 1:2]
        for m in range(HM):
            for x in range(3):
                rowA = 1024 + x * 1024 + m * 128
                rowB = rowA + HM * 128
                P.op("sp", lambda e, rowA=rowA: e.dma_start(out=XP[:, 3:3 + TP].rearrange("p (h t) -> p h t", h=2),
                                                            in_=pg_rows(k, rowA, 128)),
                     reads=[k.t_PROJT], writes=[t_XP], dma=True)
                P.op("sp", lambda e, rowB=rowB, x=x: e.dma_start(out=CV[x][:, 0:TP].rearrange("p (h t) -> p h t", h=2),
                                                                 in_=pg_rows(k, rowB, 128)),
                     reads=[k.t_PROJT], writes=[t_CV[x]], dma=True)
                P.op("dve", lambda e: e.tensor_scalar_mul(XP[:, 3:3 + TP], XP[:, 3:3 + TP], s0c), reads=[k.t_sel], writes=[t_XP])
                P.op("dve", lambda e, x=x: e.scalar_tensor_tensor(out=XP[:, 3:3 + TP], in0=CV[x][:, 0:TP], scalar=s1c, in1=XP[:, 3:3 + TP],
                                                                  op0=ALU.mult, op1=ALU.add),
                     reads=[k.t_sel, t_CV[x]], writes=[t_XP])
                cw = lambda j, x=x, m=m: convwm[:, (x * HM + m) * 4 + j:(x * HM + m) * 4 + j + 1]
                cvp = CV[x][:, 0:TP]
                P.op("dve", lambda e, cvp=cvp, cw=cw: e.tensor_scalar_mul(cvp, XP[:, 0:TP], cw(0)),
                     reads=[t_XP, t_c], writes=[t_CV[x]])
                for j in range(1, 4):
                    P.op("dve", lambda e, cvp=cvp, cw=cw, j=j: e.scalar_tensor_tensor(
                        out=cvp, in0=XP[:, j:j + TP], scalar=cw(j), in1=cvp, op0=ALU.mult, op1=ALU.add),
                        reads=[t_XP, t_c, t_CV[x]], writes=[t_CV[x]])
                P.op("act", lambda e, x=x: e.activation(CV[x][:, 0:TP], CV[x][:, 0:TP], AF.Silu), reads=[t_CV[x]], writes=[t_CV[x]])
            gA = 4096 + m * 128
            P.op("sp", lambda e, gA=gA: e.dma_start(out=SG[:, 0:TP].rearrange("p (h t) -> p h t", h=2), in_=pg_rows(k, gA, 128)),
                 reads=[k.t_PROJT], writes=[t_SG], dma=True)
            P.op("sp", lambda e, gA=gA: e.dma_start(out=XP[:, 3:3 + TP].rearrange("p (h t) -> p h t", h=2),
                                                    in_=pg_rows(k, gA + HM * 128, 128)),
                 reads=[k.t_PROJT], writes=[t_XP], dma=True)
            P.op("dve", lambda e: e.tensor_scalar_mul(SG[:, 0:TP], SG[:, 0:TP], s0c), reads=[k.t_sel], writes=[t_SG])
            P.op("dve", lambda e: e.scalar_tensor_tensor(out=SG[:, 0:TP], in0=XP[:, 3:3 + TP], scalar=s1c, in1=SG[:, 0:TP],
                                                         op0=ALU.mult, op1=ALU.add),
                 reads=[k.t_sel, t_XP], writes=[t_SG])
            l2norm_qk([(i * 512, 512) for i in range(TP // 512)])
            P.op("pool", lambda e: e.memset(Sst[:], 0.0), writes=[t_S])
            def pre_steps(g0):
                plans = [chunk_pre(gt_m, m, c, c - g0) for c in range(g0, g0 + NSL)]
                out = []
                for si in range(max(len(p) for p in plans)):
                    for p in plans:
                        if si < len(p):
                            out.append(p[si])
                return out

            def seq_chunk(c):
                def f():
                    qi = cntq["q"] % 2
                    cntq["q"] += 1
                    pq, tq = pQ[qi], t_pQ[qi]
                    P.op("pe", lambda e, pq=pq, c=c: e.matmul(pq[:, 0:128], WKT[:, c, :], Sst[:], start=True, stop=True),
                         reads=[t_co[c], t_S], writes=[tq[0]])
                    P.op("dve", lambda e, pq=pq, c=c: e.tensor_tensor(VN[:], UU[:, c, :], pq[:, 0:128], ALU.subtract),
                         reads=[t_co[c], tq[0]], writes=[t_VN])
                    P.op("pe", lambda e, pq=pq, c=c: e.matmul(pq[:, 128:256], QGT[:, c, :], Sst[:], start=True, stop=False),
                         reads=[t_co[c], t_S], writes=[tq[1]])
                    P.op("pe", lambda e, pq=pq, c=c: e.matmul(pq[:, 128:256], ATT[:, c, :], VN[:], start=False, stop=True),
                         reads=[t_co[c], t_VN], writes=[tq[1]])
                    P.op("pe", lambda e, pq=pq, c=c: e.matmul(pq[:, 256:384], KD[:, c, :], VN[:], start=True, stop=True),
                         reads=[t_co[c], t_VN], writes=[tq[2]])
                    P.op("dve", lambda e, pq=pq, c=c: e.scalar_tensor_tensor(
                        out=Sst[:], in0=Sst[:], scalar=GLA[:, c:c + 1], in1=pq[:, 256:384], op0=ALU.mult, op1=ALU.add),
                        reads=[t_S, t_co[c], tq[2]], writes=[t_S])
                    out_norm(m, c, 128, pq[:, 128:256], tq[1])
                return f

            for st_ in pre_steps(0):
                st_()
            for g0 in range(0, 16, NSL):
                nxt = pre_steps(g0 + NSL) if g0 + NSL < 16 else []
                seqs = [seq_chunk(c) for c in range(g0, g0 + NSL)]
                if nxt:
                    per = -(-len(nxt) // len(seqs))
                    for qi_, sq_ in enumerate(seqs):
                        for st_ in nxt[qi_ * per:(qi_ + 1) * per]:
                            st_()
                        sq_()
                else:
                    for sq_ in seqs:
                        sq_()
            o = P.op("sp", lambda e, m=m: e.dma_start(out=k.p_gdn[m * 128:(m + 1) * 128, :], in_=Sst[:]),
                     reads=[t_S], dma=True, semkey=t_S)
            P.final.append(o)
            P.op("sp", lambda e, m=m: e.dma_start(out=k.YTM[m * 128:(m + 1) * 128, :], in_=YG[:, 0:TP]),
                 reads=[t_YG], writes=[k.t_YTM], dma=True, semkey=t_YG, nowaw=True)
        rg = [[2 * i, 2 * i + 1] for i in range(N_CORES // 2)]
        for j in range(2):
            P.op("pool", lambda e, j=j: e.collective_compute(
                "AllGather", ALU.bypass, replica_groups=rg,
                ins=[k.YTM[j * 256:(j + 1) * 256, :]], outs=[k.YTG[j * 512:(j + 1) * 512, :]]),
                reads=[k.t_YTM], writes=[k.t_YTG], nowaw=True, cc=True, semkey=k.t_YTG)

        SC = 16
        for h in range(NH):
            for x in range(3):
                row0 = 1024 + x * 1024 + h * 128
                ch0 = x * 1024 + h * 128
                P.op("sp", lambda e, row0=row0: e.dma_start(
                    out=XS[:, :, 3:7], in_=k.PROJS[row0:row0 + 128, :].rearrange("p (b t) -> p b t", t=4)),
                    reads=[k.t_PROJS], writes=[t_XS], dma=True)
                P.op("sp", lambda e, ch0=ch0: e.dma_start(
                    out=XS[:, :, 0:3], in_=k.conv_in[ch0:ch0 + 128, :, :]), writes=[t_XS], dma=True)
                cw = lambda j, x=x, h=h: convw[:, (x * 8 + h) * 4 + j:(x * 8 + h) * 4 + j + 1]
                cvs = CV[x][:, TP:NT].rearrange("p (b t) -> p b t", t=4)
                P.op("dve", lambda e, cvs=cvs, cw=cw: e.tensor_scalar_mul(cvs, XS[:, :, 0:4], cw(0)),
                     reads=[t_XS, t_c], writes=[t_CV[x]])
                for j in range(1, 4):
                    P.op("dve", lambda e, cvs=cvs, cw=cw, j=j: e.scalar_tensor_tensor(
                        out=cvs, in0=XS[:, :, j:j + 4], scalar=cw(j), in1=cvs, op0=ALU.mult, op1=ALU.add),
                        reads=[t_XS, t_c, t_CV[x]], writes=[t_CV[x]])
                P.op("act", lambda e, x=x: e.activation(CV[x][:, TP:NT], CV[x][:, TP:NT], AF.Silu), reads=[t_CV[x]], writes=[t_CV[x]])
            P.op("sp", lambda e, h=h: e.dma_start(out=SG[:, TP:NT], in_=k.PROJS[4096 + h * 128:4096 + (h + 1) * 128, :]),
                 reads=[k.t_PROJS], writes=[t_SG], dma=True)
            l2norm_qk([(TP, TS)])
            P.op("sp", lambda e, h=h: e.dma_start(
                out=Sall[:], in_=k.gdn_in.rearrange("(b hh d) v -> hh d b v", hh=NH, d=128)[h]),
                writes=[t_Sall], dma=True)
            run_chunks(gt_all, h, [SC], SC)
            c = SC
            n = TS
            qi = cntq["q"] % 2
            cntq["q"] += 1
            pq, tq = pQ[qi], t_pQ[qi]
            wkb = WKT[:, c, 0:n].unsqueeze(1).to_broadcast([128, NSQ, n])
            qgb = QGT[:, c, 0:n].unsqueeze(1).to_broadcast([128, NSQ, n])
            kdb = KD[0:n, c, :].unsqueeze(1).to_broadcast([n, NSQ, 128])
            P.op("dve", lambda e, wkb=wkb: e.tensor_tensor(WKTm[:], Msel[:], wkb, ALU.mult), reads=[t_co[c], t_m], writes=[t_mk])
            P.op("pool", lambda e, qgb=qgb: e.tensor_tensor(QGTm[:], Msel[:], qgb, ALU.mult), reads=[t_co[c], t_m], writes=[t_mk], nowaw=True)
            P.op("pool", lambda e, kdb=kdb: e.tensor_tensor(KDm[:], Msel2[:], kdb, ALU.mult), reads=[t_co[c], t_m], writes=[t_mk], nowaw=True)
            for b in range(NSQ):
                P.op("pe", lambda e, pq=pq, b=b: e.matmul(pq[0:TS, 0:128], WKTm[:, b, :], Sall[:, b, :],
                                                          start=(b == 0), stop=(b == NSQ - 1)),
                     reads=[t_mk, t_Sall], writes=[tq[0]])
            P.op("dve", lambda e, pq=pq: e.tensor_tensor(VN[0:TS, :], UU[0:TS, SC, :], pq[0:TS, 0:128], ALU.subtract),
                 reads=[t_co[c], tq[0]], writes=[t_VN])
            for b in range(NSQ):
                P.op("pe", lambda e, pq=pq, b=b: e.matmul(pq[0:TS, 128:256], QGTm[:, b, :], Sall[:, b, :],
                                                          start=(b == 0), stop=False),
                     reads=[t_mk, t_Sall], writes=[tq[1]])
            P.op("pe", lambda e, pq=pq: e.matmul(pq[0:TS, 128:256], ATT[0:TS, SC, 0:TS], VN[0:TS, :], start=False, stop=True),
                 reads=[t_co[c], t_VN], writes=[tq[1]])
            out_norm(h, c, TS, pq[0:TS, 128:256], tq[1])
            egl = wk[0]["EGR"]
            for b in range(NSQ):
                sl4, off = divmod(b * 128, 512)
                P.op("pe", lambda e, b=b, sl4=sl4, off=off: e.matmul(pS[sl4][:, off:off + 128], KDm[:, b, :], VN[0:TS, :],
                                                                     start=True, stop=True),
                     reads=[t_mk, t_VN], writes=[t_pS[sl4][off // 128]])
            for b in range(NSQ):
                sl4, off = divmod(b * 128, 512)
                P.op("dve", lambda e, b=b, sl4=sl4, off=off, egl=egl: e.scalar_tensor_tensor(
                    out=Snew[:, b, :], in0=Sall[:, b, :], scalar=egl[:, 4 * b + 3:4 * b + 4], in1=pS[sl4][:, off:off + 128],
                    op0=ALU.mult, op1=ALU.add),
                    reads=[t_Sall, t_wk[0]["EGR"], t_pS[sl4][off // 128]], writes=[t_Snew])
            o = P.op("sp", lambda e, h=h: e.dma_start(
                out=k.s_gdn.rearrange("(b hh d) v -> hh d b v", hh=NH, d=128)[h], in_=Snew[:]),
                reads=[t_Snew], dma=True, semkey=t_Snew)
            P.final.append(o)
            P.op("sp", lambda e, h=h: e.dma_start(out=k.YT[1024 + h * 128:1024 + (h + 1) * 128, TP:NT], in_=YG[:, TP:NT]),
                 reads=[t_YG], writes=[k.t_YT], dma=True, semkey=t_YG, nowaw=True)


def make_in_maps(inp):
    f = lambda a: np.ascontiguousarray(np.asarray(a, dtype=np.float32))
    L = 0
    lam_re = f(inp["s5_lambda_re"])[L]
    lam_im = f(inp["s5_lambda_im"])[L]
    log_dt = f(inp["s5_log_dt"])[L]
    b_re = f(inp["s5_b_re"])[L]
    b_im = f(inp["s5_b_im"])[L]
    c_re = f(inp["s5_c_re"])[L]
    c_im = f(inp["s5_c_im"])[L]

    def pairlay(a):
        return np.ascontiguousarray(a.reshape(32, 2, 64).transpose(1, 2, 0).reshape(128, 32))

    c_lamre = pairlay(lam_re)
    c_lamim = pairlay(lam_im)
    c_logdt = pairlay(np.broadcast_to(log_dt[:, None], (64, 64)))

    def blay(b):
        out = np.zeros((2, 16, 32, 2, 64), np.float32)
        bb = b.reshape(32, 2, 64, 16)
        for g2 in range(2):
            out[g2, :, :, g2, :] = bb[:, g2].transpose(2, 0, 1)
        return np.ascontiguousarray(out.reshape(32, 32 * 128))

    def clay(c):
        out = np.zeros((2, 64, 32, 2, 16), np.float32)
        cc = c.reshape(32, 2, 16, 64)
        for g2 in range(2):
            out[g2, :, :, g2, :] = cc[:, g2].transpose(2, 0, 1)
        return np.ascontiguousarray(out.reshape(128, 32 * 32))

    s5_d = f(inp["s5_d"])[L]
    c_d = np.ascontiguousarray(s5_d.reshape(32, 32).T)
    c_glub = np.ascontiguousarray(f(inp["s5_glu_b"])[L].reshape(8, 128).T)
    convw = f(inp["gdn_conv_w"])[L]
    c_convw = np.ascontiguousarray(convw.reshape(4, 24, 128).transpose(2, 1, 0).reshape(128, 96))
    c_alog = np.ascontiguousarray(np.broadcast_to(f(inp["gdn_a_log"])[L][None, :], (128, NH)))
    c_dtb = np.ascontiguousarray(np.broadcast_to(f(inp["gdn_dt_bias"])[L][None, :], (128, NH)))
    c_normw = np.ascontiguousarray(f(inp["gdn_norm_w"])[L].reshape(128, 1))
    shared = {
        "c_lamre": c_lamre, "c_lamim": c_lamim, "c_logdt": c_logdt,
        "c_bre": blay(b_re), "c_bim": blay(b_im), "c_cre": clay(c_re), "c_cim": clay(c_im),
        "c_d": c_d, "c_glub": c_glub, "c_convw": c_convw, "c_alog": c_alog, "c_dtb": c_dtb,
        "c_normw": c_normw,
        "w_mix_in": f(inp["w_mix_in"])[L], "w_mix_out": f(inp["w_mix_out"])[L],
        "s5_glu_w": f(inp["s5_glu_w"])[L],
    }
    for name in ("ln1_g", "ln1_b", "ln2_g", "ln2_b", "ln3_g", "ln3_b",
                 "ffn1_w_in", "ffn1_w_out", "ffn2_w_in", "ffn2_w_out"):
        shared[name] = f(inp[name])[L]
    xp = f(inp["x_prompt"])
    xsm = f(inp["x_sample"])
    s5re = f(inp["state_s5_re"])[L]
    s5im = f(inp["state_s5_im"])[L]
    sg = f(inp["state_gdn"])[L]
    sc = f(inp["state_conv"])[L]
    maps = []
    for c in range(N_CORES):
        m = dict(shared)
        sq_, rk = c // 2, c % 2
        xc = np.empty((NL, D), np.float32)
        xc[:TL] = xp[sq_, rk * TL:(rk + 1) * TL]
        xc[TL:] = xsm[c * NSQ:(c + 1) * NSQ].reshape(TS, D)
        m["x"] = xc
        sel = np.zeros((128, 2), np.float32)
        sel[:, rk] = 1.0
        m["sel"] = sel
        m["s5re_in"] = np.ascontiguousarray(s5re[c * NSQ:(c + 1) * NSQ].reshape(NSQ, 4096))
        m["s5im_in"] = np.ascontiguousarray(s5im[c * NSQ:(c + 1) * NSQ].reshape(NSQ, 4096))
        m["gdn_in"] = np.ascontiguousarray(sg[c * NSQ:(c + 1) * NSQ].reshape(NSQ * NH * 128, 128))
        m["conv_in"] = np.ascontiguousarray(sc[c * NSQ:(c + 1) * NSQ].transpose(2, 0, 1))
        maps.append(m)
    return maps


def unpairlay(a):
    n = a.shape[1]
    return np.ascontiguousarray(a.reshape(2, 64, n).transpose(2, 0, 1).reshape(2 * n, 64))


_CACHE = {}


def kernel(**inputs):
    maps = make_in_maps(inputs)
    if "nc" not in _CACHE:
        _CACHE["nc"] = build_program()[0]
    nc = _CACHE["nc"]
    res = run_bass_kernel_spmd(nc, maps, core_ids=list(range(N_CORES))).results
    nb = 4
    yp = np.stack([np.concatenate([res[2 * s_]["y"][:TL], res[2 * s_ + 1]["y"][:TL]]) for s_ in range(nb)])
    ys = np.concatenate([res[c]["y"][TL:].reshape(NSQ, 4, D) for c in range(N_CORES)])
    p_re = np.stack([unpairlay(np.concatenate([res[2 * c]["p_s5re"], res[2 * c + 1]["p_s5re"]], axis=1)) for c in range(nb)])[None]
    p_im = np.stack([unpairlay(np.concatenate([res[2 * c]["p_s5im"], res[2 * c + 1]["p_s5im"]], axis=1)) for c in range(nb)])[None]
    p_gdn = np.stack([np.concatenate([res[2 * c]["p_gdn"], res[2 * c + 1]["p_gdn"]]).reshape(NH, 128, 128) for c in range(nb)])[None]
    p_conv = np.stack([res[2 * c + 1]["p_conv"] for c in range(nb)])[None]
    s_re = np.concatenate([res[c]["s_s5re"].reshape(NSQ, 64, 64) for c in range(N_CORES)])[None]
    s_im = np.concatenate([res[c]["s_s5im"].reshape(NSQ, 64, 64) for c in range(N_CORES)])[None]
    s_gdn = np.concatenate([res[c]["s_gdn"].reshape(NSQ, NH, 128, 128) for c in range(N_CORES)])[None]
    s_conv = np.concatenate([res[c]["s_conv"] for c in range(N_CORES)])[None]
    outs = (yp, ys, p_re, p_im, p_gdn, p_conv, s_re, s_im, s_gdn, s_conv)
    return tuple(np.ascontiguousarray(o, dtype=np.float32) for o in outs)
```

```python
import contextlib
import numpy as np
import concourse.bass as bass
import concourse.mybir as mybir
from concourse.bass_utils import run_bass_kernel_spmd

F32 = mybir.dt.float32
BF16 = mybir.dt.bfloat16
AF = mybir.ActivationFunctionType
ALU = mybir.AluOpType
AX = mybir.AxisListType

ENGS = ("pe", "act", "dve", "pool", "sp")


class Tok:
    __slots__ = ("name", "w", "r", "excl")

    def __init__(self, name="", excl=False):
        self.name = name
        self.excl = excl
        self.w = []
        self.r = []


class Op:
    __slots__ = ("eng", "fn", "deps", "dma", "semkey", "sig", "sigval", "inc")

    def __init__(self, eng, fn, dma, inc=16):
        self.eng = eng
        self.fn = fn
        self.deps = []
        self.dma = dma
        self.inc = inc
        self.semkey = None
        self.sig = bool(dma)
        self.sigval = 0


class Prog:
    def __init__(self, nc):
        self.nc = nc
        self.ops = {e: [] for e in ENGS}
        self.all = []
        self.final = []
        self.toks = {}

    def barrier(self, scratch_ap):
        tb = Tok("barrier")
        self.op("pool", lambda e: e.memset(scratch_ap, 0.0), writes=list(self.toks.values()) + [tb])
        for eng in ("pe", "act", "dve", "sp"):
            self.op(eng, None, reads=[tb])

    def op(self, eng, fn, reads=(), writes=(), dma=False, semkey=None, nowaw=False, cc=False):
        if cc:
            dma = True
        o = Op(eng, fn, dma, 1 if cc else 16)
        if any(t.excl for t in reads):
            writes = list(writes) + [t for t in reads if t.excl]
            reads = [t for t in reads if not t.excl]
        deps = []
        for t in list(reads) + list(writes):
            self.toks[id(t)] = t
        for t in reads:
            deps.extend(t.w)
        if not nowaw:
            for t in writes:
                deps.extend(t.w)
                deps.extend(t.r)
        seen = set()
        for d in deps:
            if d is o or id(d) in seen:
                continue
            if d.eng == "pe" and eng == "pe" and not d.dma and not dma:
                continue
            seen.add(id(d))
            o.deps.append(d)
            d.sig = True
        for t in reads:
            if fn is None:
                break
            if not dma:
                t.r = [q for q in t.r if q.dma or q.eng != eng]
            t.r.append(o)
        for t in writes:
            if nowaw:
                t.w = [q for q in t.w if q.dma or q.eng != eng or dma] + [o]
            else:
                t.w = [o]
                t.r = []
        if dma:
            o.semkey = semkey if semkey is not None else (writes[0] if writes else reads[0])
        self.ops[eng].append(o)
        self.all.append(o)
        return o

    def emit(self):
        nc = self.nc
        with contextlib.ExitStack() as es:
            esem = {e: es.enter_context(nc.semaphore("s_" + e)) for e in ENGS}
            dsem, dcnt = {}, {}
            ecnt = {e: 0 for e in ENGS}
            for o in self.final:
                o.sig = True
            for o in self.all:
                if not o.sig:
                    continue
                if o.dma:
                    k = id(o.semkey)
                    if k not in dsem:
                        dsem[k] = es.enter_context(nc.semaphore("d%d" % len(dsem)))
                        dcnt[k] = 0
                    dcnt[k] += o.inc
                    o.sigval = dcnt[k]
                else:
                    ecnt[o.eng] += 1
                    o.sigval = ecnt[o.eng]
            self.nsem = len(dsem) + len(ENGS)
            self.ecnt = ecnt

            def semof(o):
                return dsem[id(o.semkey)] if o.dma else esem[o.eng]

            block = es.enter_context(nc.Block())

            def run(engname, eng):
                known = {}
                for o in self.ops[engname]:
                    need = {}
                    for d in o.deps:
                        s = semof(d)
                        if d.sigval > need.get(id(s), (0, None))[0]:
                            need[id(s)] = (d.sigval, s)
                    for sid, (val, s) in need.items():
                        if known.get(sid, 0) >= val:
                            continue
                        known[sid] = val
                        eng.wait_ge(s, val)
                    if o.fn is None:
                        continue
                    ins = o.fn(eng)
                    if o.sig:
                        ins.then_inc(semof(o), o.inc if o.dma else 1)
                if engname == "sp":
                    need = {}
                    for o in self.final:
                        s = semof(o)
                        if o.sigval > need.get(id(s), (0, None))[0]:
                            need[id(s)] = (o.sigval, s)
                    for sid, (val, s) in need.items():
                        if known.get(sid, 0) >= val:
                            continue
                        known[sid] = val
                        eng.wait_ge(s, val)

            @block.tensor
            def _(e):
                run("pe", e)

            @block.scalar
            def _(e):
                run("act", e)

            @block.vector
            def _(e):
                run("dve", e)

            @block.gpsimd
            def _(e):
                run("pool", e)

            @block.sync
            def _(e):
                run("sp", e)


D = 2048
DFF = 5632
NKC = D // 128
NFT = DFF // 128
S5W = 1024
NPAIR = 32
GW = 1024
NH = 8
MIXC = 5136
ALPHA = 2.0 ** 0.25
LN_EPS = 1e-5
NORM_EPS = 1e-6
BIG = 30000.0

N_CORES = 8
TPC = 2048
NSQ = 16
TS = NSQ * 4
NT = TPC + TS
TL = TPC // 2
NL = TL + TS
CCR = 256
FFN_G = 2


def token_tiles(t0, n):
    out = []
    t = t0
    while t < t0 + n:
        m = min(128, t0 + n - t)
        out.append((t, m))
        t += m
    return out


def chunks(n, maxc=512):
    k = -(-n // maxc)
    base = -(-n // k)
    base = -(-base // 32) * 32
    out = []
    t = 0
    while t < n:
        m = min(base, n - t)
        out.append((t, m))
        t += m
    return out


class K:
    pass


def build_program(debug=False, blocks=None, stop_after=None):
    nc = bass.Bass("TRN2", target_bir_lowering=False)
    k = K()
    k.nc = nc
    k.debug = debug
    P = Prog(nc)
    k.P = P
    if blocks is None:
        blocks = [(0, NL)]
    k.blocks = blocks
    maxb = max(n for _, n in blocks)
    k.maxb = maxb

    def din(name, shape, dt=F32):
        return nc.dram_tensor(name, list(shape), dt, kind="ExternalInput").ap()

    def dout(name, shape, dt=F32):
        return nc.dram_tensor(name, list(shape), dt, kind="ExternalOutput").ap()

    def dscr(name, shape, dt=F32):
        kind = "ExternalOutput" if debug else "Internal"
        return nc.dram_tensor(name, list(shape), dt, kind=kind).ap()

    k.x = din("x", [NL, D])
    k.sel = din("sel", [128, 2])
    k.s5re_in = din("s5re_in", [NSQ, 64 * 64])
    k.s5im_in = din("s5im_in", [NSQ, 64 * 64])
    k.gdn_in = din("gdn_in", [NSQ * NH * 128, 128])
    k.conv_in = din("conv_in", [3 * GW, NSQ, 3])
    k.ln_g = [din("ln%d_g" % i, [D]) for i in (1, 2, 3)]
    k.ln_b = [din("ln%d_b" % i, [D]) for i in (1, 2, 3)]
    k.w_in = [din("ffn%d_w_in" % i, [D, 2 * DFF]) for i in (1, 2)]
    k.w_out = [din("ffn%d_w_out" % i, [DFF, D]) for i in (1, 2)]
    k.w_mix_in = din("w_mix_in", [D, MIXC])
    k.w_mix_out = din("w_mix_out", [D, D])
    k.glu_w = din("s5_glu_w", [S5W, S5W])
    k.c_lamre = din("c_lamre", [128, NPAIR])
    k.c_lamim = din("c_lamim", [128, NPAIR])
    k.c_logdt = din("c_logdt", [128, NPAIR])
    k.c_bre = din("c_bre", [32, NPAIR * 128])
    k.c_bim = din("c_bim", [32, NPAIR * 128])
    k.c_cre = din("c_cre", [128, NPAIR * 32])
    k.c_cim = din("c_cim", [128, NPAIR * 32])
    k.c_d = din("c_d", [32, NPAIR])
    k.c_glub = din("c_glub", [128, 8])
    k.c_convw = din("c_convw", [128, 24 * 4])
    k.c_alog = din("c_alog", [128, NH])
    k.c_dtb = din("c_dtb", [128, NH])
    k.c_normw = din("c_normw", [128, 1])
    k.y = dout("y", [NL, D])
    k.p_s5re = dout("p_s5re", [128, NPAIR // 2])
    k.p_s5im = dout("p_s5im", [128, NPAIR // 2])
    k.p_gdn = dout("p_gdn", [NH // 2 * 128, 128])
    k.p_conv = dout("p_conv", [3, 3 * GW])
    k.s_s5re = dout("s_s5re", [NSQ, 64 * 64])
    k.s_s5im = dout("s_s5im", [NSQ, 64 * 64])
    k.s_gdn = dout("s_gdn", [NSQ * NH * 128, 128])
    k.s_conv = dout("s_conv", [NSQ, 3, 3 * GW])
    dint = lambda name, shape, dt=F32: nc.dram_tensor(name, list(shape), dt, kind="Internal").ap()
    k.X1 = dscr("X1", [NL, D])
    k.PROJP = dint("PROJP", [5120, TL])
    k.PROJS = dint("PROJS", [5120, TS])
    k.PROJG = dint("PROJG", [2 * 5120, TL])
    k.BDP = dint("BDP", [TL, 16]); k.BDS = dint("BDS", [TS, 16]); k.BDG = dint("BDG", [2 * TL, 16])
    k.t_PROJP = Tok("PROJP"); k.t_PROJS = Tok("PROJS"); k.t_BDP = Tok("BDP"); k.t_BDS = Tok("BDS")
    k.ZTM = dint("ZTM", [S5W // 2, TPC])
    k.ZTG = dint("ZTG", [S5W, TPC])
    k.ZS = dint("ZS", [S5W, TS])
    k.YTM = dint("YTM", [GW // 2, TPC], BF16)
    k.YTG = dint("YTG", [GW, TPC], BF16)
    k.t_YTM = Tok("YTM"); k.t_YTG = Tok("YTG")
    k.t_ZS = Tok("ZS")
    k.YT = dscr("YT", [D, NT], BF16)
    if debug:
        k.DBGQ = dout("DBGQ", [3, 128, NT])
        k.DBGG = dout("DBGG", [3, 128, 17 * 8])
        k.DBGC = dout("DBGC", [6, 128, 128])
    k.t_X1 = Tok("X1")
    k.t_PROJT = Tok("PROJT")
    k.t_ZT = Tok("ZT"); k.t_ZTG = Tok("ZTG")
    k.t_YT = Tok("YT")

    with contextlib.ExitStack() as es:
        k.es = es
        sb = lambda name, shape, dt=F32: es.enter_context(nc.sbuf_tensor(name, list(shape), dt))
        k.ident = sb("ident", [128, 128])
        k.identb = sb("identb", [128, 128], BF16)
        k.t_ident = Tok("ident")
        k.BDK = sb("BDK", [128, 17, 16])
        k.t_BDK = [Tok("BDK%d" % i) for i in range(17)]
        k.eps_ln = sb("eps_ln", [128, 1])
        k.eps_nm = sb("eps_nm", [128, 1])
        k.t_eps = Tok("eps")
        P.op("pool", lambda e: e.memset(k.ident[:], 0.0), writes=[k.t_ident])
        P.op("pool", lambda e: e.affine_select(out=k.ident[:], in_=k.ident[:], pattern=[[-1, 128]],
                                               compare_op=ALU.not_equal, fill=1.0, base=0,
                                               channel_multiplier=1),
             reads=[k.t_ident], writes=[k.t_ident])
        P.op("pool", lambda e: e.tensor_copy(k.identb[:], k.ident[:]), reads=[k.t_ident], writes=[k.t_ident])
        P.op("pool", lambda e: e.memset(k.BDK[:], 0.0), writes=list(k.t_BDK))
        P.op("pool", lambda e: e.memset(k.eps_ln[:], LN_EPS), writes=[k.t_eps])
        P.op("pool", lambda e: e.memset(k.eps_nm[:], NORM_EPS), writes=[k.t_eps])

        k.bar = sb("bar", [128, 4])
        k.sel_sb = sb("sel_sb", [128, 2]); k.t_sel = Tok("sel")
        P.op("sp", lambda e: e.dma_start(out=k.sel_sb[:], in_=k.sel), writes=[k.t_sel], dma=True)
        phase_ffn(k, which=0, stop_after=stop_after)
        if stop_after != "A":
            P.barrier(k.bar[:])
            rg = [[2 * i, 2 * i + 1] for i in range(N_CORES // 2)]
            for j in range(5120 // CCR):
                P.op("pool", lambda e, j=j: e.collective_compute(
                    "AllGather", ALU.bypass, replica_groups=rg,
                    ins=[k.PROJP[j * CCR:(j + 1) * CCR, :]], outs=[k.PROJG[j * 2 * CCR:(j + 1) * 2 * CCR, :]]),
                    reads=[k.t_PROJP], writes=[k.t_PROJT], nowaw=True, cc=True, semkey=k.t_PROJT)
            P.op("pool", lambda e: e.collective_compute("AllGather", ALU.bypass, replica_groups=rg,
                                                        ins=[k.BDP], outs=[k.BDG]),
                 reads=[k.t_BDP], writes=[k.t_PROJT], nowaw=True, cc=True, semkey=k.t_PROJT)
            phase_mixers(k, stop_after=stop_after)
        if stop_after is None:
            P.barrier(k.bar[:])
            phase_ffn(k, which=1, stop_after=None)
        P.emit()
    k.nsem = P.nsem
    return nc, k


def phase_ffn(k, which, stop_after=None):
    nc, P = k.nc, k.P
    maxb = k.maxb
    ntile_max = -(-maxb // 128)
    with contextlib.ExitStack() as es:
        sb = lambda name, shape, dt=F32: es.enter_context(nc.sbuf_tensor(name + str(which), list(shape), dt))
        ps = lambda name, shape, dt=F32: es.enter_context(nc.psum_tensor(name + str(which), list(shape), dt))
        xT = sb("xT", [128, NKC, maxb], BF16)
        t_xT = [Tok("xT%d" % i) for i in range(ntile_max)]
        acc = sb("acc", [128, ntile_max, D])
        t_acc = [Tok("acc%d" % i) for i in range(ntile_max)]
        NW = 2
        wring = [sb("wr%d" % i, [128, NKC, 128], BF16) for i in range(NW)]
        t_wr = [Tok("wr%d" % i) for i in range(NW)]
        wrup = [sb("wu%d" % i, [128, NKC, 128], BF16) for i in range(NW)]
        t_wu = [Tok("wu%d" % i) for i in range(NW)]
        wo = [sb("wo%d" % i, [128, FFN_G, D], BF16) for i in range(2)]
        t_wo = [Tok("wo%d" % i) for i in range(2)]
        actT = [sb("actT%d" % i, [128, FFN_G, maxb], BF16) for i in range(2)]
        t_act = [Tok("act%d" % i) for i in range(2)]
        xs = [sb("xs%d" % i, [128, D]) for i in range(2)]
        t_xs = [Tok("xs%d" % i) for i in range(2)]
        xb = sb("xb", [128, D], BF16)
        t_xb = Tok("xb")
        gbc = sb("gbc", [128, D])
        bbc = sb("bbc", [128, D])
        t_gb = Tok("gb")
        sg = [sb("sg%d" % i, [128, 512]) for i in range(2)]
        t_sg = [Tok("sg%d" % i) for i in range(2)]
        stage = [sb("stage0", [128, maxb])] * 2
        t_stage = [Tok("stage0")] * 2
        wbd = sb("wbd", [128, NKC, 16], BF16)
        t_wbd = Tok("wbd")
        st = sb("st", [128, 8])
        t_st = Tok("st")
        junk = xb
        t_junk = t_xb
        convs = sb("convs", [128, 3 * GW])
        t_convs = Tok("convs")
        pg = [ps("pg%d" % i, [128, 512]) for i in range(2)]
        pu = [ps("pu%d" % i, [128, 512]) for i in range(2)]
        po = [ps("po%d" % i, [128, 1024]) for i in range(2)]
        t_pg = [Tok("pg%d" % i) for i in range(2)]
        t_pu = [Tok("pu%d" % i) for i in range(2)]
        t_po = [Tok("po%d" % i) for i in range(2)]
        cnt = {"w": 0, "wo": 0, "xs": 0, "pgu": 0, "po": 0, "act": 0, "sg": 0, "stage": 0}

        def load_ln(idx):
            g, b = k.ln_g[idx], k.ln_b[idx]
            P.op("sp", lambda e: e.dma_start(out=gbc[:], in_=g.partition_broadcast(128)),
                 writes=[t_gb], dma=True)
            P.op("sp", lambda e: e.dma_start(out=bbc[:], in_=b.partition_broadcast(128)),
                 writes=[t_gb], dma=True, nowaw=True)

        def transposes_to_xT(ti, m):
            for q in range(4):
                b = cnt["pgu"] % 2
                cnt["pgu"] += 1
                pt = pg[b][:].bitcast(BF16)
                for j in range(4):
                    kc = q * 4 + j
                    P.op("pe", lambda e, pt=pt, j=j, kc=kc: e.transpose(
                        pt[:, j * 128:j * 128 + m], xb[0:m, kc * 128:(kc + 1) * 128], k.identb[0:m, 0:m]),
                        reads=[t_xb, k.t_ident], writes=[t_pg[b]])
                src = pt[:, 0:512].rearrange("p (j t) -> p j t", j=4)[:, :, 0:m]
                dst = xT[:, q * 4:(q + 1) * 4, ti * 128:ti * 128 + m]
                if q % 2 == 0:
                    P.op("dve", lambda e, dst=dst, src=src: e.tensor_copy(dst, src),
                         reads=[t_pg[b]], writes=[t_xT[ti]])
                else:
                    P.op("act", lambda e, dst=dst, src=src: e.copy(dst, src),
                         reads=[t_pg[b]], writes=[t_xT[ti]])

        def ingest(ti, m, xs_i, do_acc, do_xT):
            if do_acc:
                P.op("act", lambda e: e.mul(acc[0:m, ti, :], xs[xs_i][0:m, :], ALPHA),
                     reads=[t_xs[xs_i]], writes=[t_acc[ti]])
            if do_xT:
                P.op("act", lambda e: e.copy(xb[0:m, :], xs[xs_i][0:m, :]),
                     reads=[t_xs[xs_i]], writes=[t_xb])
                transposes_to_xT(ti, m)

        def layer_norm_tile(ti, m, xs_i):
            z = acc[0:m, ti, :]
            P.op("dve", lambda e: e.reduce_sum(st[0:m, 0:1], z, axis=AX.X),
                 reads=[t_acc[ti]], writes=[t_st])
            P.op("dve", lambda e: e.tensor_scalar_mul(st[0:m, 1:2], st[0:m, 0:1], -1.0 / D),
                 reads=[t_st], writes=[t_st])
            P.op("act", lambda e: e.activation(junk[0:m, :], z, AF.Square, bias=st[0:m, 1:2], scale=1.0,
                                               accum_out=st[0:m, 2:3]),
                 reads=[t_acc[ti], t_st], writes=[t_junk, t_st])
            P.op("act", lambda e: e.activation(st[0:m, 3:4], st[0:m, 2:3], AF.Sqrt, bias=k.eps_ln[0:m, :],
                                               scale=1.0 / D),
                 reads=[t_st, k.t_eps], writes=[t_st])
            P.op("dve", lambda e: e.reciprocal(st[0:m, 4:5], st[0:m, 3:4]), reads=[t_st], writes=[t_st])
            P.op("dve", lambda e: e.tensor_scalar(xs[xs_i][0:m, :], z, st[0:m, 1:2], st[0:m, 4:5],
                                                  ALU.add, ALU.mult),
                 reads=[t_acc[ti], t_st], writes=[t_xs[xs_i]])
            P.op("dve", lambda e: e.tensor_tensor(xs[xs_i][0:m, :], xs[xs_i][0:m, :], gbc[0:m, :], ALU.mult),
                 reads=[t_xs[xs_i], t_gb], writes=[t_xs[xs_i]])
            P.op("dve", lambda e: e.tensor_tensor(xs[xs_i][0:m, :], xs[xs_i][0:m, :], bbc[0:m, :], ALU.add),
                 reads=[t_xs[xs_i], t_gb], writes=[t_xs[xs_i]])

        def outproj_group(lhs_buf, t_lhs, w_dram, row0, ng, tiles, scale, after_tile=None):
            wi = cnt["wo"] % 2
            cnt["wo"] += 1
            wv = w_dram[row0:row0 + ng * 128, :].rearrange("(g p) c -> p g c", p=128)
            P.op("pool", lambda e: e.dma_start(out=wo[wi][:, 0:ng, :], in_=wv), writes=[t_wo[wi]], dma=True)
            for li, (t0, m) in enumerate(tiles):
                for half in range(2):
                    pb = cnt["po"] % 2
                    cnt["po"] += 1
                    for g in range(ng):
                        for c in range(2):
                            P.op("pe", lambda e, pb=pb, g=g, c=c, half=half, li=li, m=m: e.matmul(
                                po[pb][0:m, c * 512:(c + 1) * 512],
                                lhs_buf[:, g, li * 128:li * 128 + m],
                                wo[wi][:, g, half * 1024 + c * 512: half * 1024 + (c + 1) * 512],
                                start=(g == 0), stop=(g == ng - 1)),
                                reads=[t_lhs, t_wo[wi]], writes=[t_po[pb]])
                    P.op("dve", lambda e, pb=pb, half=half, li=li, m=m: e.scalar_tensor_tensor(
                        out=acc[0:m, li, half * 1024:(half + 1) * 1024], in0=po[pb][0:m, :], scalar=scale,
                        in1=acc[0:m, li, half * 1024:(half + 1) * 1024], op0=ALU.mult, op1=ALU.add),
                        reads=[t_po[pb], t_acc[li]], writes=[t_acc[li]])
                if after_tile is not None:
                    after_tile(li, t0, m)

        def ffn(widx, nb, tiles, after_tile=None):
            w_in_v = k.w_in[widx].rearrange("(kc p) c -> p kc c", p=128)
            cks = chunks(nb)
            for g0 in range(0, NFT, FFN_G):
                ai = cnt["act"] % 2
                cnt["act"] += 1
                for g in range(FFN_G):
                    j = g0 + g
                    ws = cnt["w"] % NW
                    cnt["w"] += 1
                    P.op("pool", lambda e, ws=ws, j=j: e.dma_start(
                        out=wring[ws][:], in_=w_in_v[:, :, j * 128:(j + 1) * 128]), writes=[t_wr[ws]], dma=True)
                    P.op("pool", lambda e, ws=ws, j=j: e.dma_start(
                        out=wrup[ws][:], in_=w_in_v[:, :, DFF + j * 128:DFF + (j + 1) * 128]),
                        writes=[t_wu[ws]], dma=True)
                    for (c0, cn) in cks:
                        b = cnt["pgu"] % 2
                        cnt["pgu"] += 1
                        rtoks = [t_xT[i] for i in range(c0 // 128, (c0 + cn - 1) // 128 + 1)]
                        for kc in range(NKC):
                            P.op("pe", lambda e, b=b, ws=ws, kc=kc, c0=c0, cn=cn: e.matmul(
                                pg[b][:, 0:cn], wring[ws][:, kc, :], xT[:, kc, c0:c0 + cn],
                                start=(kc == 0), stop=(kc == NKC - 1)),
                                reads=[t_wr[ws]] + rtoks, writes=[t_pg[b]])
                        for kc in range(NKC):
                            P.op("pe", lambda e, b=b, ws=ws, kc=kc, c0=c0, cn=cn: e.matmul(
                                pu[b][:, 0:cn], wrup[ws][:, kc, :], xT[:, kc, c0:c0 + cn],
                                start=(kc == 0), stop=(kc == NKC - 1)),
                                reads=[t_wu[ws]] + rtoks, writes=[t_pu[b]])
                        si = cnt["sg"] % 2
                        cnt["sg"] += 1
                        P.op("act", lambda e, b=b, si=si, cn=cn: e.activation(sg[si][:, 0:cn], pg[b][:, 0:cn], AF.Silu),
                             reads=[t_pg[b]], writes=[t_sg[si]])
                        P.op("dve", lambda e, b=b, si=si, cn=cn, c0=c0, g=g, ai=ai: e.tensor_tensor(
                            actT[ai][:, g, c0:c0 + cn], sg[si][:, 0:cn], pu[b][:, 0:cn], ALU.mult),
                            reads=[t_sg[si], t_pu[b]], writes=[t_act[ai]])
                outproj_group(actT[ai], t_act[ai], k.w_out[widx], g0 * 128, FFN_G, tiles, 0.5,
                              after_tile if g0 + FFN_G >= NFT else None)

        def mix_in(nb, t0, tiles):
            wv = k.w_mix_in.rearrange("(kc p) c -> p kc c", p=128)
            cks = chunks(nb)
            has_prompt_tail = (t0 <= TL - 3) and (t0 + nb >= TL)
            has_sample = (t0 + nb >= NL)
            n_p = min(t0 + nb, TL) - t0
            n_s = nb - n_p
            P.op("pool", lambda e: e.dma_start(out=wbd[:], in_=wv[:, :, 5120:5136]), writes=[t_wbd], dma=True)
            for li, (tt0, m) in enumerate(tiles):
                b = cnt["pgu"] % 2
                cnt["pgu"] += 1
                gi = tt0 // 128
                for kc in range(NKC):
                    P.op("pe", lambda e, b=b, kc=kc, li=li, m=m: e.matmul(
                        pu[b][0:m, 0:16], xT[:, kc, li * 128:li * 128 + m], wbd[:, kc, :],
                        start=(kc == 0), stop=(kc == NKC - 1)),
                        reads=[t_xT[li], t_wbd], writes=[t_pu[b]])
                P.op("act", lambda e, b=b, gi=gi, m=m: e.copy(k.BDK[0:m, gi, :], pu[b][0:m, 0:16]),
                     reads=[t_pu[b]], writes=[k.t_BDK[gi]])
                if tt0 < TL:
                    P.op("sp", lambda e, gi=gi, m=m, tt0=tt0: e.dma_start(out=k.BDP[tt0:tt0 + m, :], in_=k.BDK[0:m, gi, :]),
                         reads=[k.t_BDK[gi]], writes=[k.t_BDP], dma=True, semkey=k.t_BDK[gi], nowaw=True)
                else:
                    P.op("sp", lambda e, gi=gi, m=m: e.dma_start(out=k.BDS[0:m, :], in_=k.BDK[0:m, gi, :]),
                         reads=[k.t_BDK[gi]], writes=[k.t_BDS], dma=True, semkey=k.t_BDK[gi], nowaw=True)
            for ct in range(40):
                ws = cnt["w"] % NW
                cnt["w"] += 1
                P.op("pool", lambda e, ws=ws, ct=ct: e.dma_start(
                    out=wring[ws][:], in_=wv[:, :, ct * 128:(ct + 1) * 128]),
                    writes=[t_wr[ws]], dma=True)
                si = cnt["stage"] % 2
                cnt["stage"] += 1
                for (c0, cn) in cks:
                    b = cnt["pgu"] % 2
                    cnt["pgu"] += 1
                    rtoks = [t_xT[i] for i in range(c0 // 128, (c0 + cn - 1) // 128 + 1)]
                    for kc in range(NKC):
                        P.op("pe", lambda e, b=b, ws=ws, kc=kc, c0=c0, cn=cn: e.matmul(
                            pg[b][:, 0:cn], wring[ws][:, kc, :], xT[:, kc, c0:c0 + cn],
                            start=(kc == 0), stop=(kc == NKC - 1)),
                            reads=[t_wr[ws]] + rtoks, writes=[t_pg[b]])
                    fn = AF.Silu if ct >= 32 else AF.Copy
                    P.op("act", lambda e, b=b, si=si, c0=c0, cn=cn, fn=fn: e.activation(
                        stage[si][:, c0:c0 + cn], pg[b][:, 0:cn], fn),
                        reads=[t_pg[b]], writes=[t_stage[si]])
                P.op("sp", lambda e, si=si, ct=ct: e.dma_start(
                    out=k.PROJP[ct * 128:(ct + 1) * 128, t0:t0 + n_p], in_=stage[si][:, 0:n_p]),
                    reads=[t_stage[si]], writes=[k.t_PROJP], dma=True, semkey=t_stage[si], nowaw=True)
                if n_s:
                    P.op("sp", lambda e, si=si, ct=ct: e.dma_start(
                        out=k.PROJS[ct * 128:(ct + 1) * 128, 0:n_s], in_=stage[si][:, n_p:n_p + n_s]),
                        reads=[t_stage[si]], writes=[k.t_PROJS], dma=True, semkey=t_stage[si], nowaw=True)
                if 8 <= ct < 32 and (has_prompt_tail or has_sample):
                    cc = ct - 8
                    b = cnt["pgu"] % 2
                    cnt["pgu"] += 1
                    if has_sample:
                        s0 = TL - t0
                        P.op("pe", lambda e, b=b, si=si, s0=s0: e.transpose(
                            pu[b][0:TS, 0:128], stage[si][:, s0:s0 + TS], k.ident[:]),
                            reads=[t_stage[si], k.t_ident], writes=[t_pu[b]])
                    if has_prompt_tail:
                        s1 = TL - 32 - t0
                        P.op("pe", lambda e, b=b, si=si, s1=s1: e.transpose(
                            pu[b][0:32, 128:256], stage[si][:, s1:s1 + 32], k.ident[:]),
                            reads=[t_stage[si], k.t_ident], writes=[t_pu[b]])
                    if has_sample:
                        P.op("dve", lambda e, b=b, cc=cc: e.tensor_copy(
                            convs[0:TS, cc * 128:(cc + 1) * 128], pu[b][0:TS, 0:128]),
                            reads=[t_pu[b]], writes=[t_convs])
                    if has_prompt_tail:
                        P.op("dve", lambda e, b=b, cc=cc: e.tensor_copy(
                            convs[64:96, cc * 128:(cc + 1) * 128], pu[b][0:32, 128:256]),
                            reads=[t_pu[b]], writes=[t_convs])
            if has_prompt_tail:
                o = P.op("sp", lambda e: e.dma_start(out=k.p_conv[:, :], in_=convs[93:96, :]),
                         reads=[t_convs], dma=True, semkey=t_convs)
                P.final.append(o)
            if has_sample:
                for r in range(3):
                    src = convs[r + 1:TS:4, :]
                    o = P.op("sp", lambda e, r=r, src=src: e.dma_start(out=k.s_conv[:, r, :], in_=src),
                             reads=[t_convs], dma=True, semkey=t_convs)
                    P.final.append(o)

        def mix_out(nb, t0, tiles, after_tile=None):
            yv = k.YT.rearrange("(kc p) t -> p kc t", p=128)
            n_p = min(t0 + nb, TL) - t0
            n_s = nb - n_p
            HMh = NH // 2

            def ygrows(hh):
                rk_, m_ = divmod(hh, HMh)
                j_, i0_ = divmod(m_ * 128, 256)
                return k.YTG.rearrange("(j h i) t -> h j i t", h=2, i=256)[rk_][j_][i0_:i0_ + 128]
            P.op("sp", lambda e: e.dma_start(out=xT[:, 0:8, 0:n_p], in_=yv[:, 0:8, t0:t0 + n_p]),
                 reads=[k.t_YT], writes=t_xT, dma=True, semkey=t_xT[0])
            for hh in range(NH):
                P.op("sp", lambda e, hh=hh: e.dma_start(out=xT[:, 8 + hh, 0:n_p], in_=ygrows(hh)[:, t0:t0 + n_p]),
                     reads=[k.t_YTG], writes=t_xT, dma=True, semkey=t_xT[0], nowaw=True)
            if n_s:
                P.op("sp", lambda e: e.dma_start(out=xT[:, :, n_p:nb], in_=yv[:, :, TPC:TPC + n_s]),
                     reads=[k.t_YT], writes=t_xT, dma=True, semkey=t_xT[0], nowaw=True)
            for g0 in range(0, NKC, FFN_G):
                ai = (g0 // FFN_G) % 2
                if g0 < 8:
                    P.op("sp", lambda e, g0=g0, ai=ai: e.dma_start(out=actT[ai][:, :, 0:n_p], in_=yv[:, g0:g0 + FFN_G, TL + t0:TL + t0 + n_p]),
                         reads=[k.t_YT], writes=[t_act[ai]], dma=True)
                else:
                    for g in range(FFN_G):
                        P.op("sp", lambda e, g0=g0, g=g, ai=ai: e.dma_start(
                            out=actT[ai][:, g, 0:n_p], in_=ygrows(g0 - 8 + g)[:, TL + t0:TL + t0 + n_p]),
                            reads=[k.t_YTG], writes=[t_act[ai]], dma=True, nowaw=(g > 0))
                P.op("dve", lambda e, g0=g0: e.tensor_scalar_mul(xT[:, g0:g0 + FFN_G, 0:n_p], xT[:, g0:g0 + FFN_G, 0:n_p], k.sel_sb[:, 0:1]),
                     reads=[k.t_sel], writes=t_xT)
                P.op("dve", lambda e, g0=g0, ai=ai: e.scalar_tensor_tensor(
                    out=xT[:, g0:g0 + FFN_G, 0:n_p], in0=actT[ai][:, :, 0:n_p], scalar=k.sel_sb[:, 1:2],
                    in1=xT[:, g0:g0 + FFN_G, 0:n_p], op0=ALU.mult, op1=ALU.add),
                    reads=[k.t_sel, t_act[ai]], writes=t_xT)
            for g0 in range(0, NKC, FFN_G):
                class _V:
                    def __init__(self, g0):
                        self.g0 = g0

                    def __getitem__(self, idx):
                        p, g, t = idx
                        return xT[p, self.g0 + g, t]
                outproj_group(_V(g0), t_xT[0], k.w_mix_out, g0 * 128, FFN_G, tiles, 1.0,
                              after_tile if g0 + FFN_G >= NKC else None)

        for (t0, nb) in k.blocks:
            tiles = token_tiles(t0, nb)

            def nxs():
                xi = cnt["xs"] % 2
                cnt["xs"] += 1
                return xi

            if which == 0:
                load_ln(0)
                for li, (tt0, m) in enumerate(tiles):
                    xi = nxs()
                    P.op("sp", lambda e, xi=xi, tt0=tt0, m=m: e.dma_start(out=xs[xi][0:m, :], in_=k.x[tt0:tt0 + m, :]),
                         writes=[t_xs[xi]], dma=True)
                    ingest(li, m, xi, True, True)

                def ln1_tile(li, tt0, m):
                    xi = nxs()
                    layer_norm_tile(li, m, xi)
                    P.op("sp", lambda e, xi=xi, tt0=tt0, m=m: e.dma_start(out=k.X1[tt0:tt0 + m, :], in_=xs[xi][0:m, :]),
                         reads=[t_xs[xi]], writes=[k.t_X1], dma=True, semkey=t_xs[xi], nowaw=True)
                    ingest(li, m, xi, False, True)
                ffn(0, nb, tiles, ln1_tile)
                mix_in(nb, t0, tiles)
            else:
                load_ln(1)
                for li, (tt0, m) in enumerate(tiles):
                    xi = nxs()
                    P.op("sp", lambda e, xi=xi, tt0=tt0, m=m: e.dma_start(out=xs[xi][0:m, :], in_=k.X1[tt0:tt0 + m, :]),
                         reads=[k.t_X1], writes=[t_xs[xi]], dma=True)
                    ingest(li, m, xi, True, False)

                def ln2_tile(li, tt0, m):
                    xi = nxs()
                    layer_norm_tile(li, m, xi)
                    ingest(li, m, xi, True, True)
                mix_out(nb, t0, tiles, ln2_tile)
                load_ln(2)

                def ln3_tile(li, tt0, m):
                    xi = nxs()
                    layer_norm_tile(li, m, xi)
                    o = P.op("sp", lambda e, xi=xi, tt0=tt0, m=m: e.dma_start(out=k.y[tt0:tt0 + m, :], in_=xs[xi][0:m, :]),
                             reads=[t_xs[xi]], dma=True, semkey=t_xs[xi])
                    P.final.append(o)
                ffn(1, nb, tiles, ln3_tile)


def pg_rows(k, r0, n):
    j, i0 = divmod(r0, CCR)
    assert i0 + n <= CCR
    return k.PROJG.rearrange("(j h i) t -> j i h t", h=2, i=CCR)[j][i0:i0 + n]


def phase_mixers(k, stop_after=None):
    phase_s5(k)
    k.P.barrier(k.bar[:])
    phase_glu(k)
    if stop_after == "S5":
        return
    k.P.barrier(k.bar[:])
    phase_gdn(k)


TWO_PI = 6.283185307179586
CW1 = 6.28125
CW2 = TWO_PI - CW1
SINS = 1.0
PI_CL = 3.141592


def phase_s5(k):
    nc, P = k.nc, k.P
    I32 = mybir.dt.int32
    TP = TPC
    with contextlib.ExitStack() as es:
        sb = lambda name, shape, dt=F32: es.enter_context(nc.sbuf_tensor("s5_" + name, list(shape), dt))
        ps = lambda name, shape, dt=F32: es.enter_context(nc.psum_tensor("s5_" + name, list(shape), dt))
        T = lambda name: Tok(name)
        t_c = T("consts")
        lamre = sb("lamre", [128, NPAIR]); lamim = sb("lamim", [128, NPAIR]); logdt = sb("logdt", [128, NPAIR])
        dcol = sb("dcol", [32, NPAIR])
        for dst, src in ((lamre, k.c_lamre), (lamim, k.c_lamim), (logdt, k.c_logdt), (dcol, k.c_d)):
            P.op("sp", lambda e, dst=dst, src=src: e.dma_start(out=dst[:], in_=src), writes=[t_c], dma=True, nowaw=True)
        bre = sb("bre", [32, NPAIR * 128], BF16); bim = sb("bim", [32, NPAIR * 128], BF16)
        cre = sb("cre", [128, NPAIR * 32], BF16); cim = sb("cim", [128, NPAIR * 32], BF16)
        t_w = T("s5w")
        for dst, src in ((bre, k.c_bre), (bim, k.c_bim), (cre, k.c_cre), (cim, k.c_cim)):
            P.op("pool", lambda e, dst=dst, src=src: e.dma_start(out=dst[:], in_=src), writes=[t_w], dma=True, nowaw=True)

        sc = {n: sb("sc_" + n, [128, NPAIR]) for n in
              ("dt", "th", "rho", "r", "sn", "cs", "nr", "ni", "den", "cre", "cim", "ar", "ai", "t1", "t2", "kf", "rd")}
        sci = sb("sc_ki", [128, NPAIR], I32)
        t_s = T("scal")

        def so(eng, fn, extra=()):
            P.op(eng, fn, reads=[t_s, t_c] + list(extra), writes=[t_s])

        so("act", lambda e: e.activation(sc["dt"][:], logdt[:], AF.Exp))
        so("dve", lambda e: e.tensor_tensor(sc["th"][:], lamim[:], sc["dt"][:], ALU.mult))
        so("dve", lambda e: e.tensor_tensor(sc["rho"][:], lamre[:], sc["dt"][:], ALU.mult))
        so("act", lambda e: e.activation(sc["r"][:], sc["rho"][:], AF.Exp))

        def range_reduce(eng_a, eng_b, out, x, ki, kf, shape_ok=True):
            so(eng_a, lambda e: e.tensor_scalar_mul(kf, x, 1.0 / TWO_PI))
            so(eng_a, lambda e: e.tensor_copy(ki, kf))
            so(eng_b, lambda e: e.tensor_copy(kf, ki))
            so(eng_a, lambda e: e.scalar_tensor_tensor(out=out, in0=kf, scalar=-CW1, in1=x, op0=ALU.mult, op1=ALU.add))
            so(eng_b, lambda e: e.scalar_tensor_tensor(out=out, in0=kf, scalar=-CW2, in1=out, op0=ALU.mult, op1=ALU.add))
            so(eng_b, lambda e: e.tensor_scalar(out, out, -PI_CL, PI_CL, ALU.max, ALU.min))

        range_reduce("dve", "dve", sc["rd"][:], sc["th"][:], sci[:], sc["kf"][:])
        so("act", lambda e: e.activation(sc["sn"][:], sc["rd"][:], AF.Sin, scale=SINS))
        so("act", lambda e: e.activation(sc["t1"][:], sc["rd"][:], AF.Sin, scale=0.5 * SINS))
        so("dve", lambda e: e.tensor_tensor(sc["t1"][:], sc["t1"][:], sc["t1"][:], ALU.mult))
        so("dve", lambda e: e.tensor_scalar(sc["cs"][:], sc["t1"][:], -2.0, 1.0, ALU.mult, ALU.add))
        so("dve", lambda e: e.tensor_tensor(sc["ar"][:], sc["r"][:], sc["cs"][:], ALU.mult))
        so("dve", lambda e: e.tensor_tensor(sc["ai"][:], sc["r"][:], sc["sn"][:], ALU.mult))
        so("dve", lambda e: e.tensor_scalar_add(sc["nr"][:], sc["ar"][:], -1.0))
        so("dve", lambda e: e.tensor_tensor(sc["t1"][:], lamre[:], lamre[:], ALU.mult))
        so("dve", lambda e: e.tensor_tensor(sc["t2"][:], lamim[:], lamim[:], ALU.mult))
        so("dve", lambda e: e.tensor_tensor(sc["den"][:], sc["t1"][:], sc["t2"][:], ALU.add))
        so("dve", lambda e: e.reciprocal(sc["den"][:], sc["den"][:]))
        so("dve", lambda e: e.tensor_tensor(sc["t1"][:], sc["nr"][:], lamre[:], ALU.mult))
        so("dve", lambda e: e.tensor_tensor(sc["t2"][:], sc["ai"][:], lamim[:], ALU.mult))
        so("dve", lambda e: e.tensor_tensor(sc["t1"][:], sc["t1"][:], sc["t2"][:], ALU.add))
        so("dve", lambda e: e.tensor_tensor(sc["cre"][:], sc["t1"][:], sc["den"][:], ALU.mult))
        so("dve", lambda e: e.tensor_tensor(sc["t1"][:], sc["ai"][:], lamre[:], ALU.mult))
        so("dve", lambda e: e.tensor_tensor(sc["t2"][:], sc["nr"][:], lamim[:], ALU.mult))
        so("dve", lambda e: e.tensor_tensor(sc["t1"][:], sc["t1"][:], sc["t2"][:], ALU.subtract))
        so("dve", lambda e: e.tensor_tensor(sc["cim"][:], sc["t1"][:], sc["den"][:], ALU.mult))

        NM = NPAIR // 2
        s0c, s1c = k.sel_sb[:, 0:1], k.sel_sb[:, 1:2]
        msc = {n: sb("msc_" + n, [128, NM]) for n in ("th", "r", "cre", "cim")}
        for n in msc:
            so("dve", lambda e, n=n: e.tensor_scalar_mul(msc[n][:], sc[n][:, 0:NM], s0c), extra=[k.t_sel])
            so("dve", lambda e, n=n: e.scalar_tensor_tensor(out=msc[n][:], in0=sc[n][:, NM:NPAIR], scalar=s1c, in1=msc[n][:],
                                                            op0=ALU.mult, op1=ALU.add), extra=[k.t_sel])
        mdcol = sb("mdcol", [32, NM])
        so("dve", lambda e: e.tensor_scalar_mul(mdcol[:], dcol[:, 0:NM], k.sel_sb[0:32, 0:1]), extra=[k.t_sel])
        so("dve", lambda e: e.scalar_tensor_tensor(out=mdcol[:], in0=dcol[:, NM:NPAIR], scalar=k.sel_sb[0:32, 1:2], in1=mdcol[:],
                                                   op0=ALU.mult, op1=ALU.add), extra=[k.t_sel])
        mB = [[sb("mB%d_%d" % (a, b_), [32, 128], BF16) for b_ in range(2)] for a in range(2)]
        mC = [[sb("mC%d_%d" % (a, b_), [128, 32], BF16) for b_ in range(2)] for a in range(2)]
        t_mw2 = [T("s5mw0"), T("s5mw1")]

        iot = sb("iot", [128, TP])
        t_iot = T("iota")

        pX = [ps("pX%d" % i, [128, 512]) for i in range(4)]
        t_pX = [T("pX%d" % i) for i in range(4)]
        pY = [ps("pY%d" % i, [128, 512]) for i in range(2)]
        t_pY = [T("pY%d" % i) for i in range(2)]
        pT = ps("pT", [128, 512]); t_pT = T("pT")
        pO = ps("pO", [128, 512]); t_pO = T("pO")

        st2 = sb("st_in", [32, 2048]); t_stin = T("stin")
        st_in = st2[0:NSQ, :]
        u_g = st2
        h_re = sb("h_re", [128, NPAIR, NSQ]); h_im = sb("h_im", [128, NPAIR, NSQ])
        t_h = T("h")
        for (src, dst) in ((k.s5re_in, h_re), (k.s5im_in, h_im)):
            for hh in range(2):
                P.op("sp", lambda e, src=src, hh=hh: e.dma_start(out=st_in[:], in_=src[:, hh * 2048:(hh + 1) * 2048]),
                     writes=[t_stin], dma=True)
                for g16 in range(16):
                    P.op("pe", lambda e, g16=g16: e.transpose(pT[:, g16 * NSQ:(g16 + 1) * NSQ],
                                                              st_in[:, g16 * 128:(g16 + 1) * 128], k.ident[0:NSQ, 0:NSQ]),
                         reads=[t_stin, k.t_ident], writes=[t_pT])
                P.op("dve", lambda e, dst=dst, hh=hh: e.tensor_copy(
                    dst[:, hh * 16:(hh + 1) * 16, :].rearrange("p g b -> p (g b)"), pT[:, 0:16 * NSQ]),
                    reads=[t_pT], writes=[t_h])

        xs_re = sb("xs_re", [128, NPAIR, TS]); xs_im = sb("xs_im", [128, NPAIR, TS])
        t_xs = T("xs")
        hs_re = sb("hs_re", [128, NPAIR, TS], BF16); hs_im = sb("hs_nim", [128, NPAIR, TS], BF16)
        t_hs = T("hs")
        pst_re = sb("pst_re", [128, NPAIR // 2]); pst_im = sb("pst_im", [128, NPAIR // 2]); t_pst = T("pst")
        zs = sb("zs", [32, NPAIR, TS]); t_zs = T("zs")

        HT = TP // 2
        u_f = sb("u_f", [32, TP]); u_b = sb("u_b", [32, TP], BF16); t_u = T("u")
        def dbl(name, dt=F32, w=HT, parts=128):
            return [sb("%s%d" % (name, i), [parts, w], dt) for i in range(2)], [T("%s%d" % (name, i)) for i in range(2)]
        ang, t_ang = dbl("ang"); kfb, t_kfb = dbl("kfb"); kib, t_kib = dbl("kib", I32)
        snT, t_sn = dbl("snT"); csT, t_cs = dbl("csT")
        Wr, t_Wr = dbl("Wr"); Wi, t_Wi = dbl("Wi")
        xr, t_xr = dbl("xr"); xi, t_xi = dbl("xi")
        hrb, t_hrb = dbl("hrb", BF16); hib, t_hib = dbl("hib", BF16)
        zst, t_z = dbl("zst", F32, HT, 32)
        tt1, t_t1 = dbl("tt1", F32, 512); tt2, t_t2 = dbl("tt2", F32, 512)
        tt3, t_t3 = dbl("tt3", F32, 512); tt4, t_t4 = dbl("tt4", F32, 512)
        xis, t_xis = dbl("xis", F32, 512)
        xrs, t_xrs = dbl("xrs", F32, 512)
        sm = sb("sm", [128, 4]); t_sm = T("sm")
        for hh in range(2):
            P.op("pool", lambda e, hh=hh: e.iota(kib[hh][:], pattern=[[1, HT]], base=hh * HT, channel_multiplier=0),
                 writes=[t_kib[hh]])
            P.op("pool", lambda e, hh=hh: e.tensor_copy(iot[:, hh * HT:(hh + 1) * HT], kib[hh][:]),
                 reads=[t_kib[hh]], writes=[t_iot])
        cnt = {"x": 0, "y": 0, "c": 0}
        prev_par = None

        for gp in range(NM):
            col = lambda n, gp=gp: msc[n][:, gp:gp + 1]
            mq = gp % 2
            t_mw = t_mw2[mq]
            for (dst, src, pr_, w_) in ((mB[0][mq], bre, 32, 128), (mB[1][mq], bim, 32, 128),
                                        (mC[0][mq], cre, 128, 32), (mC[1][mq], cim, 128, 32)):
                P.op("dve", lambda e, dst=dst, src=src, pr_=pr_, w_=w_, gp=gp: e.tensor_scalar_mul(
                    dst[:], src[:, gp * w_:(gp + 1) * w_], k.sel_sb[0:pr_, 0:1]), reads=[t_w, k.t_sel], writes=[t_mw])
                P.op("dve", lambda e, dst=dst, src=src, pr_=pr_, w_=w_, gp=gp: e.scalar_tensor_tensor(
                    out=dst[:], in0=src[:, (NM + gp) * w_:(NM + gp + 1) * w_], scalar=k.sel_sb[0:pr_, 1:2], in1=dst[:],
                    op0=ALU.mult, op1=ALU.add), reads=[t_w, k.t_sel], writes=[t_mw])
            mbre_, mbim_, mcre_, mcim_ = mB[0][mq], mB[1][mq], mC[0][mq], mC[1][mq]
            P.op("sp", lambda e, gp=gp: e.dma_start(out=u_f[:, 0:TP].rearrange("p (h t) -> p h t", h=2),
                                                    in_=pg_rows(k, gp * 32, 32)),
                 reads=[k.t_PROJT], writes=[t_u], dma=True)
            P.op("sp", lambda e, gp=gp: e.dma_start(out=u_g[:, 0:TP].rearrange("p (h t) -> p h t", h=2),
                                                    in_=pg_rows(k, (NM + gp) * 32, 32)),
                 reads=[k.t_PROJT], writes=[t_stin], dma=True)
            P.op("dve", lambda e: e.tensor_scalar_mul(u_f[:, 0:TP], u_f[:, 0:TP], k.sel_sb[0:32, 0:1]),
                 reads=[k.t_sel], writes=[t_u])
            P.op("dve", lambda e: e.scalar_tensor_tensor(out=u_f[:, 0:TP], in0=u_g[:, 0:TP], scalar=k.sel_sb[0:32, 1:2],
                                                         in1=u_f[:, 0:TP], op0=ALU.mult, op1=ALU.add),
                 reads=[k.t_sel, t_stin], writes=[t_u])
            P.op("act", lambda e: e.copy(u_b[:, 0:TP], u_f[:, 0:TP]), reads=[], writes=[t_u])
            for hf in range(2):
                p = (gp * 2 + hf) % 2
                c_lo = hf * HT
                A, Kf, Ki, SN, CS, WR, WI, XR, XI = ang[p], kfb[p], kib[p], snT[p], csT[p], Wr[p], Wi[p], xr[p], xi[p]
                P.op("dve", lambda e, A=A, col=col, c_lo=c_lo: e.tensor_scalar_mul(A[:], iot[:, c_lo:c_lo + HT], col("th")),
                     reads=[t_iot, t_s], writes=[t_ang[p]])
                P.op("act", lambda e, A=A, Kf=Kf: e.mul(Kf[:], A[:], 1.0 / TWO_PI), reads=[t_ang[p]], writes=[t_kfb[p]])
                P.op("dve", lambda e, Kf=Kf, Ki=Ki: e.tensor_copy(Ki[:], Kf[:]), reads=[t_kfb[p]], writes=[t_kib[p]])
                P.op("pool", lambda e, Kf=Kf, Ki=Ki: e.tensor_copy(Kf[:], Ki[:]), reads=[t_kib[p]], writes=[t_kfb[p]])
                P.op("dve", lambda e, A=A, Kf=Kf: e.scalar_tensor_tensor(out=A[:], in0=Kf[:], scalar=-CW1, in1=A[:],
                                                                       op0=ALU.mult, op1=ALU.add),
                     reads=[t_kfb[p]], writes=[t_ang[p]])
                P.op("dve", lambda e, A=A, Kf=Kf: e.scalar_tensor_tensor(out=A[:], in0=Kf[:], scalar=-CW2, in1=A[:],
                                                                       op0=ALU.mult, op1=ALU.add),
                     reads=[t_kfb[p]], writes=[t_ang[p]])
                P.op("dve", lambda e, A=A: e.tensor_scalar(A[:], A[:], -PI_CL, PI_CL, ALU.max, ALU.min),
                     reads=[], writes=[t_ang[p]])
                P.op("act", lambda e, A=A, SN=SN: e.activation(SN[:], A[:], AF.Sin), reads=[t_ang[p]], writes=[t_sn[p]])
                P.op("act", lambda e, A=A, CS=CS: e.activation(CS[:], A[:], AF.Sin, scale=0.5), reads=[t_ang[p]], writes=[t_cs[p]])
                P.op("act", lambda e, CS=CS: e.activation(CS[:], CS[:], AF.Square), reads=[], writes=[t_cs[p]])
                P.op("act", lambda e, CS=CS: e.mul(CS[:], CS[:], -2.0), reads=[], writes=[t_cs[p]])
                P.op("act", lambda e, CS=CS: e.add(CS[:], CS[:], 1.0), reads=[], writes=[t_cs[p]])
                P.op("dve", lambda e, WR=WR, CS=CS, col=col: e.tensor_scalar_mul(WR[:], CS[:], col("cre")),
                     reads=[t_cs[p], t_s], writes=[t_Wr[p]])
                P.op("dve", lambda e, WR=WR, SN=SN, col=col: e.scalar_tensor_tensor(
                    out=WR[:], in0=SN[:], scalar=col("cim"), in1=WR[:], op0=ALU.mult, op1=ALU.add),
                    reads=[t_sn[p], t_s], writes=[t_Wr[p]])
                P.op("dve", lambda e, WI=WI, SN=SN, col=col: e.tensor_scalar_mul(WI[:], SN[:], col("cre")),
                     reads=[t_sn[p], t_s], writes=[t_Wi[p]])
                P.op("dve", lambda e, WI=WI, CS=CS, col=col: e.scalar_tensor_tensor(
                    out=WI[:], in0=CS[:], scalar=col("cim"), in1=WI[:], op0=ALU.mult, op1=ALU.subtract),
                    reads=[t_cs[p], t_s], writes=[t_Wi[p]])
                cks = [(c_lo + i * 512, 512) for i in range(HT // 512)]
                for (c0, cn) in cks:
                    b = cnt["x"] % 2
                    cnt["x"] += 1
                    pr, pi = pX[2 * b], pX[2 * b + 1]
                    tpr, tpi = t_pX[2 * b], t_pX[2 * b + 1]
                    P.op("pe", lambda e, mbre_=mbre_, pr=pr, c0=c0, cn=cn: e.matmul(
                        pr[:, 0:cn], mbre_[:], u_b[:, c0:c0 + cn], start=True, stop=True),
                        reads=[t_mw, t_u], writes=[tpr])
                    P.op("pe", lambda e, mbim_=mbim_, pi=pi, c0=c0, cn=cn: e.matmul(
                        pi[:, 0:cn], mbim_[:], u_b[:, c0:c0 + cn], start=True, stop=True),
                        reads=[t_mw, t_u], writes=[tpi])
                    if c0 < TP:
                        q = cnt["c"] % 2
                        cnt["c"] += 1
                        sl = slice(c0 - c_lo, c0 - c_lo + 512)
                        T1, T2, T3, T4, XIS = tt1[q], tt2[q], tt3[q], tt4[q], xis[q]
                        XRS = xrs[q]
                        P.op("act", lambda e, XIS=XIS, pi=pi: e.copy(XIS[:], pi[:, 0:512]), reads=[tpi], writes=[t_xis[q]])
                        P.op("act", lambda e, XRS=XRS, pr=pr: e.copy(XRS[:], pr[:, 0:512]), reads=[tpr], writes=[t_xrs[q]])
                        P.op("dve", lambda e, T1=T1, WR=WR, pr=pr, sl=sl: e.tensor_tensor(T1[:], WR[:, sl], pr[:, 0:512], ALU.mult),
                             reads=[t_Wr[p], tpr], writes=[t_t1[q]])
                        P.op("pool", lambda e, T2=T2, WI=WI, XIS=XIS, sl=sl: e.tensor_tensor(T2[:], WI[:, sl], XIS[:], ALU.mult),
                             reads=[t_Wi[p], t_xis[q]], writes=[t_t2[q]])
                        P.op("dve", lambda e, T3=T3, WR=WR, pi=pi, sl=sl: e.tensor_tensor(T3[:], WR[:, sl], pi[:, 0:512], ALU.mult),
                             reads=[t_Wr[p], tpi], writes=[t_t3[q]])
                        P.op("pool", lambda e, T4=T4, WI=WI, XRS=XRS, sl=sl: e.tensor_tensor(T4[:], WI[:, sl], XRS[:], ALU.mult),
                             reads=[t_Wi[p], t_xrs[q]], writes=[t_t4[q]])
                        P.op("dve", lambda e, XR=XR, T1=T1, T2=T2, sl=sl: e.tensor_tensor(XR[:, sl], T1[:], T2[:], ALU.subtract),
                             reads=[t_t1[q], t_t2[q]], writes=[t_xr[p]])
                        P.op("pool", lambda e, XI=XI, T3=T3, T4=T4, sl=sl: e.tensor_tensor(XI[:, sl], T3[:], T4[:], ALU.add),
                             reads=[t_t3[q], t_t4[q]], writes=[t_xi[p]])
                rb = msc["r"][:, gp:gp + 1].to_broadcast([128, HT])
                if hf == 0:
                    ini_r, ini_i, rd_prev = 0.0, 0.0, []
                else:
                    pp = 1 - p
                    ini_r, ini_i = ang[pp][:, HT - 1:HT], kfb[pp][:, HT - 1:HT]
                    rd_prev = [t_ang[pp], t_kfb[pp]]
                P.op("dve", lambda e, A=A, rb=rb, XR=XR, ini_r=ini_r: e.tensor_tensor_scan(A[:], rb, XR[:], ini_r, ALU.mult, ALU.add),
                     reads=[t_s, t_xr[p], t_sn[p], t_cs[p]] + rd_prev, writes=[t_ang[p]])
                P.op("dve", lambda e, Kf=Kf, rb=rb, XI=XI, ini_i=ini_i: e.tensor_tensor_scan(Kf[:], rb, XI[:], ini_i, ALU.mult, ALU.add),
                     reads=[t_s, t_xi[p]] + rd_prev, writes=[t_kfb[p]])
                HR, HI = A, Kf
                P.op("pool", lambda e, XR=XR, SN=SN, HI=HI: e.tensor_tensor(XR[:], SN[:], HI[:], ALU.mult),
                     reads=[t_sn[p], t_kfb[p]], writes=[t_xr[p]])
                P.op("dve", lambda e, WR=WR, CS=CS, HR=HR: e.tensor_tensor(WR[:], CS[:], HR[:], ALU.mult),
                     reads=[t_cs[p], t_ang[p]], writes=[t_Wr[p]])
                P.op("dve", lambda e, WR=WR, XR=XR, p=p: e.tensor_tensor(hrb[p][:], WR[:], XR[:], ALU.subtract),
                     reads=[t_Wr[p], t_xr[p]], writes=[t_hrb[p]])
                P.op("pool", lambda e, XI=XI, SN=SN, HR=HR: e.tensor_tensor(XI[:], SN[:], HR[:], ALU.mult),
                     reads=[t_sn[p], t_ang[p]], writes=[t_xi[p]])
                P.op("pool", lambda e, WI=WI, CS=CS, HI=HI: e.tensor_tensor(WI[:], CS[:], HI[:], ALU.mult),
                     reads=[t_cs[p], t_kfb[p]], writes=[t_Wi[p]])
                P.op("dve", lambda e, WI=WI, XI=XI, p=p: e.scalar_tensor_tensor(out=hib[p][:], in0=WI[:], scalar=-1.0, in1=XI[:],
                                                                                op0=ALU.mult, op1=ALU.subtract),
                     reads=[t_Wi[p], t_xi[p]], writes=[t_hib[p]])
                if hf == 1:
                    L1 = HT - 1
                    P.op("dve", lambda e, gp=gp, CS=CS, HR=HR: e.tensor_tensor(pst_re[:, gp:gp + 1], CS[:, L1:HT], HR[:, L1:HT], ALU.mult),
                         reads=[t_cs[p], t_ang[p]], writes=[t_pst])
                    P.op("dve", lambda e, SN=SN, HI=HI: e.tensor_tensor(sm[:, 0:1], SN[:, L1:HT], HI[:, L1:HT], ALU.mult),
                         reads=[t_sn[p], t_kfb[p]], writes=[t_sm])
                    P.op("dve", lambda e, gp=gp: e.tensor_tensor(pst_re[:, gp:gp + 1], pst_re[:, gp:gp + 1], sm[:, 0:1], ALU.subtract),
                         reads=[t_pst, t_sm], writes=[t_pst])
                    P.op("dve", lambda e, gp=gp, SN=SN, HR=HR: e.tensor_tensor(pst_im[:, gp:gp + 1], SN[:, L1:HT], HR[:, L1:HT], ALU.mult),
                         reads=[t_sn[p], t_ang[p]], writes=[t_pst])
                    P.op("dve", lambda e, CS=CS, HI=HI: e.tensor_tensor(sm[:, 0:1], CS[:, L1:HT], HI[:, L1:HT], ALU.mult),
                         reads=[t_cs[p], t_kfb[p]], writes=[t_sm])
                    P.op("dve", lambda e, gp=gp: e.tensor_tensor(pst_im[:, gp:gp + 1], pst_im[:, gp:gp + 1], sm[:, 0:1], ALU.add),
                         reads=[t_pst, t_sm], writes=[t_pst])
                for ci in range(HT // 512):
                    sl = slice(ci * 512, (ci + 1) * 512)
                    gsl = slice(c_lo + ci * 512, c_lo + (ci + 1) * 512)
                    b = cnt["y"] % 2
                    cnt["y"] += 1
                    P.op("pe", lambda e, mcre_=mcre_, b=b, sl=sl, p=p: e.matmul(pY[b][0:32, :], mcre_[:], hrb[p][:, sl],
                                                                          start=True, stop=False),
                         reads=[t_mw, t_hrb[p]], writes=[t_pY[b]])
                    P.op("pe", lambda e, mcim_=mcim_, b=b, sl=sl, p=p: e.matmul(pY[b][0:32, :], mcim_[:], hib[p][:, sl],
                                                                          start=False, stop=True),
                         reads=[t_mw, t_hib[p]], writes=[t_pY[b]])
                    P.op("dve", lambda e, gp=gp, b=b, sl=sl, gsl=gsl, p=p: e.scalar_tensor_tensor(
                        out=zst[p][:, sl], in0=u_f[:, gsl], scalar=mdcol[:, gp:gp + 1], in1=pY[b][0:32, :],
                        op0=ALU.mult, op1=ALU.add), reads=[t_u, t_s, t_pY[b]], writes=[t_z[p]])
                P.op("act", lambda e, p=p: e.activation(zst[p][:], zst[p][:], AF.Gelu), reads=[], writes=[t_z[p]])
                P.op("sp", lambda e, gp=gp, p=p, c_lo=c_lo: e.dma_start(out=k.ZTM[gp * 32:(gp + 1) * 32, c_lo:c_lo + HT], in_=zst[p][:]),
                     reads=[t_z[p]], writes=[k.t_ZT], dma=True, semkey=t_z[p], nowaw=True)

        rg = [[2 * i, 2 * i + 1] for i in range(N_CORES // 2)]
        for j in range(4):
            P.op("pool", lambda e, j=j: e.collective_compute(
                "AllGather", ALU.bypass, replica_groups=rg,
                ins=[k.ZTM[j * 128:(j + 1) * 128, :]], outs=[k.ZTG[j * 256:(j + 1) * 256, :]]),
                reads=[k.t_ZT], writes=[k.t_ZTG], nowaw=True, cc=True, semkey=k.t_ZTG)

        P.barrier(k.bar[:])
        us_f = u_f[:, :].rearrange("p (g t) -> p g t", t=TS)
        us_b = u_b[:, :].rearrange("p (g t) -> p g t", t=TS)
        t_us = T("us")
        P.op("sp", lambda e: e.dma_start(out=us_f, in_=k.PROJS[0:S5W, :].rearrange("(g q) t -> q g t", q=32)),
             reads=[k.t_PROJS], writes=[t_us], dma=True)
        P.op("act", lambda e: e.copy(us_b, us_f), reads=[t_us], writes=[t_us])
        for gp in range(NPAIR):
            b = cnt["x"] % 2
            cnt["x"] += 1
            pr, pi = pX[2 * b], pX[2 * b + 1]
            tpr, tpi = t_pX[2 * b], t_pX[2 * b + 1]
            colg = lambda n, gp=gp: sc[n][:, gp:gp + 1]
            P.op("pe", lambda e, gp=gp, pr=pr: e.matmul(pr[:, 0:TS], bre[:, gp * 128:(gp + 1) * 128], us_b[:, gp, :],
                                                        start=True, stop=True), reads=[t_w, t_us], writes=[tpr])
            P.op("pe", lambda e, gp=gp, pi=pi: e.matmul(pi[:, 0:TS], bim[:, gp * 128:(gp + 1) * 128], us_b[:, gp, :],
                                                        start=True, stop=True), reads=[t_w, t_us], writes=[tpi])
            P.op("dve", lambda e, pr=pr, gp=gp, colg=colg: e.tensor_scalar_mul(xs_re[:, gp, :], pr[:, 0:TS], colg("cre")),
                 reads=[tpr, t_s], writes=[t_xs])
            P.op("dve", lambda e, pi=pi, gp=gp, colg=colg: e.scalar_tensor_tensor(
                out=xs_re[:, gp, :], in0=pi[:, 0:TS], scalar=colg("cim"), in1=xs_re[:, gp, :],
                op0=ALU.mult, op1=ALU.subtract), reads=[tpi, t_s, t_xs], writes=[t_xs])
            P.op("dve", lambda e, gp=gp: e.tensor_scalar_mul(xs_re[:, gp, :], xs_re[:, gp, :], -1.0),
                 reads=[t_xs], writes=[t_xs])
            P.op("dve", lambda e, pi=pi, gp=gp, colg=colg: e.tensor_scalar_mul(xs_im[:, gp, :], pi[:, 0:TS], colg("cre")),
                 reads=[tpi, t_s], writes=[t_xs])
            P.op("dve", lambda e, pr=pr, gp=gp, colg=colg: e.scalar_tensor_tensor(
                out=xs_im[:, gp, :], in0=pr[:, 0:TS], scalar=colg("cim"), in1=xs_im[:, gp, :],
                op0=ALU.mult, op1=ALU.add), reads=[tpr, t_s, t_xs], writes=[t_xs])
            P.op("dve", lambda e, gp=gp: e.tensor_scalar_mul(zs[:, gp, :], us_f[:, gp, :], dcol[:, gp:gp + 1]),
                 reads=[t_us, t_c], writes=[t_zs])

        arb = sc["ar"][:].unsqueeze(2).to_broadcast([128, NPAIR, NSQ])
        aib = sc["ai"][:].unsqueeze(2).to_broadcast([128, NPAIR, NSQ])
        P.barrier(k.bar[:])
        class _A:
            def __init__(self, t):
                self.t = t

            def __getitem__(self, idx):
                return self.t[:].rearrange("p (g b) -> p g b", b=NSQ)
        w1 = _A(tt1[0]); w2 = _A(tt2[0]); t_w1 = T("w1")
        hn_re = _A(tt3[0]); hn_im = _A(tt4[0])
        xsr4 = xs_re[:].rearrange("p g (b t) -> p g b t", t=4)
        xsi4 = xs_im[:].rearrange("p g (b t) -> p g b t", t=4)
        hsr4 = hs_re[:].rearrange("p g (b t) -> p g b t", t=4)
        hsi4 = hs_im[:].rearrange("p g (b t) -> p g b t", t=4)
        cur = (h_re, h_im)
        nxt = (hn_re, hn_im)
        for t in range(4):
            cr, ci_ = cur
            nr_, ni_ = nxt
            P.op("dve", lambda e, cr=cr: e.tensor_tensor(w1[:], cr[:], arb, ALU.mult), reads=[t_h, t_s], writes=[t_w1])
            P.op("dve", lambda e, ci_=ci_: e.tensor_tensor(w2[:], ci_[:], aib, ALU.mult), reads=[t_h, t_s, t_w1], writes=[t_w1])
            P.op("dve", lambda e: e.tensor_tensor(w1[:], w1[:], w2[:], ALU.subtract), reads=[t_w1], writes=[t_w1])
            P.op("dve", lambda e, nr_=nr_, t=t: e.tensor_tensor(nr_[:], w1[:], xsr4[:, :, :, t], ALU.add),
                 reads=[t_w1, t_xs, t_h], writes=[t_h], nowaw=True)
            P.op("dve", lambda e, cr=cr: e.tensor_tensor(w1[:], cr[:], aib, ALU.mult), reads=[t_h, t_s, t_w1], writes=[t_w1])
            P.op("dve", lambda e, ci_=ci_: e.tensor_tensor(w2[:], ci_[:], arb, ALU.mult), reads=[t_h, t_s, t_w1], writes=[t_w1])
            P.op("dve", lambda e: e.tensor_tensor(w1[:], w1[:], w2[:], ALU.add), reads=[t_w1], writes=[t_w1])
            P.op("dve", lambda e, ni_=ni_, t=t: e.tensor_tensor(ni_[:], w1[:], xsi4[:, :, :, t], ALU.add),
                 reads=[t_w1, t_xs, t_h], writes=[t_h])
            P.op("pool", lambda e, nr_=nr_, t=t: e.tensor_copy(hsr4[:, :, :, t], nr_[:]), reads=[t_h], writes=[t_hs])
            P.op("pool", lambda e, ni_=ni_, t=t: e.tensor_scalar_mul(hsi4[:, :, :, t], ni_[:], -1.0), reads=[t_h, t_hs], writes=[t_hs])
            cur, nxt = nxt, cur
        fin_re, fin_im = cur
        for half in range(4):
            for gq in range(8):
                gp = half * 8 + gq
                P.op("pe", lambda e, gp=gp, gq=gq: e.matmul(pO[0:32, gq * TS:(gq + 1) * TS], cre[:, gp * 32:(gp + 1) * 32],
                                                            hs_re[:, gp, :], start=True, stop=False),
                     reads=[t_w, t_hs], writes=[t_pO])
                P.op("pe", lambda e, gp=gp, gq=gq: e.matmul(pO[0:32, gq * TS:(gq + 1) * TS], cim[:, gp * 32:(gp + 1) * 32],
                                                            hs_im[:, gp, :], start=False, stop=True),
                     reads=[t_w, t_hs], writes=[t_pO])
            P.op("dve", lambda e, half=half: e.tensor_tensor(
                zs[:, half * 8:(half + 1) * 8, :].rearrange("p g t -> p (g t)"),
                zs[:, half * 8:(half + 1) * 8, :].rearrange("p g t -> p (g t)"), pO[0:32, :], ALU.add),
                reads=[t_zs, t_pO], writes=[t_zs])
        P.op("act", lambda e: e.activation(zs[:], zs[:], AF.Gelu), reads=[t_zs], writes=[t_zs])
        P.op("sp", lambda e: e.dma_start(out=k.ZS.rearrange("(gp q) t -> q gp t", q=32), in_=zs[:]),
             reads=[t_zs], writes=[k.t_ZS], dma=True, semkey=t_zs)
        st_out = st_in; t_sto = t_stin
        for (src, dst) in ((fin_re, k.s_s5re), (fin_im, k.s_s5im)):
            for hh in range(2):
                for g0 in range(0, 16, 4):
                    for j in range(4):
                        P.op("pe", lambda e, g0=g0, j=j, src=src, hh=hh: e.transpose(
                            pT[0:NSQ, j * 128:(j + 1) * 128], src[:, hh * 16 + g0 + j, :], k.ident[:]),
                            reads=[t_h, k.t_ident], writes=[t_pT])
                    P.op("dve", lambda e, g0=g0: e.tensor_copy(st_out[:, g0 * 128:(g0 + 4) * 128], pT[0:NSQ, :]),
                         reads=[t_pT], writes=[t_sto])
                o = P.op("sp", lambda e, dst=dst, hh=hh: e.dma_start(out=dst[:, hh * 2048:(hh + 1) * 2048], in_=st_out[:]),
                         reads=[t_sto], dma=True, semkey=t_sto)
                P.final.append(o)
        for (src, dst) in ((pst_re, k.p_s5re), (pst_im, k.p_s5im)):
            o = P.op("sp", lambda e, src=src, dst=dst: e.dma_start(out=dst, in_=src[:]), reads=[t_pst], dma=True, semkey=t_pst)
            P.final.append(o)


def phase_glu(k):
    nc, P = k.nc, k.P
    with contextlib.ExitStack() as es:
        sb = lambda name, shape, dt=F32: es.enter_context(nc.sbuf_tensor("gl_" + name, list(shape), dt))
        ps = lambda name, shape, dt=F32: es.enter_context(nc.psum_tensor("gl_" + name, list(shape), dt))
        gw = sb("gw", [128, 8, S5W], BF16); t_gw = Tok("gw")
        glub = sb("glub", [128, 8]); t_gb = Tok("glub")
        P.op("pool", lambda e: e.dma_start(out=gw[:], in_=k.glu_w.rearrange("(kc p) c -> p kc c", p=128)),
             writes=[t_gw], dma=True)
        P.op("sp", lambda e: e.dma_start(out=glub[:], in_=k.c_glub), writes=[t_gb], dma=True)
        zb = [sb("zb%d" % i, [128, 8, 512], BF16) for i in range(2)]; t_zb = [Tok("zb%d" % i) for i in range(2)]
        zf = [sb("zf%d" % i, [128, 8, 512]) for i in range(2)]; t_zf = [Tok("zf%d" % i) for i in range(2)]
        sg = [sb("sg%d" % i, [128, 512]) for i in range(2)]; t_sg = [Tok("gsg%d" % i) for i in range(2)]
        yb = [sb("yb%d" % i, [128, 512], BF16) for i in range(2)]; t_yb = [Tok("yb%d" % i) for i in range(2)]
        pp = [ps("pp%d" % i, [128, 512]) for i in range(2)]; t_pp = [Tok("pp%d" % i) for i in range(2)]
        zg = k.ZTG.rearrange("(j h i) t -> h i j t", h=2, i=128)
        zsv = k.ZS.rearrange("(kc p) t -> p kc t", p=128)
        n = 0
        for ci, (c0, cn) in enumerate([(i * 512, 512) for i in range(TPC // 512)] + [(TPC, TS)]):
            bi = ci % 2
            if c0 < TPC:
                for hh in range(2):
                    P.op("pool", lambda e, bi=bi, c0=c0, cn=cn, hh=hh: e.dma_start(
                        out=zb[bi][:, hh * 4:(hh + 1) * 4, 0:cn], in_=zg[hh][:, :, c0:c0 + cn]),
                        reads=[k.t_ZTG], writes=[t_zb[bi]], dma=True, nowaw=(hh == 1))
                    P.op("sp", lambda e, bi=bi, c0=c0, cn=cn, hh=hh: e.dma_start(
                        out=zf[bi][:, hh * 4:(hh + 1) * 4, 0:cn], in_=zg[hh][:, :, c0:c0 + cn]),
                        reads=[k.t_ZTG], writes=[t_zf[bi]], dma=True, nowaw=(hh == 1))
            else:
                P.op("pool", lambda e, bi=bi, cn=cn: e.dma_start(out=zb[bi][:, :, 0:cn], in_=zsv[:, :, 0:cn]),
                     reads=[k.t_ZS], writes=[t_zb[bi]], dma=True)
                P.op("sp", lambda e, bi=bi, cn=cn: e.dma_start(out=zf[bi][:, :, 0:cn], in_=zsv[:, :, 0:cn]),
                     reads=[k.t_ZS], writes=[t_zf[bi]], dma=True)
            for ot in range(8):
                b = n % 2
                n += 1
                for kc in range(8):
                    P.op("pe", lambda e, b=b, bi=bi, kc=kc, ot=ot, cn=cn: e.matmul(
                        pp[b][:, 0:cn], gw[:, kc, ot * 128:(ot + 1) * 128], zb[bi][:, kc, 0:cn],
                        start=(kc == 0), stop=(kc == 7)), reads=[t_gw, t_zb[bi]], writes=[t_pp[b]])
                P.op("act", lambda e, b=b, ot=ot, cn=cn: e.activation(sg[b][:, 0:cn], pp[b][:, 0:cn], AF.Sigmoid,
                                                                      bias=glub[:, ot:ot + 1], scale=1.0),
                     reads=[t_pp[b], t_gb], writes=[t_sg[b]])
                P.op("dve", lambda e, b=b, bi=bi, ot=ot, cn=cn: e.tensor_tensor(yb[b][:, 0:cn], sg[b][:, 0:cn],
                                                                                zf[bi][:, ot, 0:cn], ALU.mult),
                     reads=[t_sg[b], t_zf[bi]], writes=[t_yb[b]])
                P.op("sp", lambda e, b=b, ot=ot, c0=c0, cn=cn: e.dma_start(
                    out=k.YT[ot * 128:(ot + 1) * 128, c0:c0 + cn], in_=yb[b][:, 0:cn]),
                    reads=[t_yb[b]], writes=[k.t_YT], dma=True, semkey=t_yb[b], nowaw=True)


def phase_gdn(k):
    nc, P = k.nc, k.P
    TP = TPC
    NCH = TP // 128 + 1
    with contextlib.ExitStack() as es:
        sb = lambda name, shape, dt=F32: es.enter_context(nc.sbuf_tensor("g_" + name, list(shape), dt))
        ps = lambda name, shape, dt=F32: es.enter_context(nc.psum_tensor("g_" + name, list(shape), dt))
        T = lambda name: Tok(name)
        t_c = T("gconst")
        convw = sb("convw", [128, 96]); alog = sb("alog", [128, NH]); dtb = sb("dtb", [128, NH]); normw = sb("normw", [128, 1])
        for dst, src in ((convw, k.c_convw), (alog, k.c_alog), (dtb, k.c_dtb), (normw, k.c_normw)):
            P.op("sp", lambda e, dst=dst, src=src: e.dma_start(out=dst[:], in_=src), writes=[t_c], dma=True, nowaw=True)
        nea = sb("nea", [128, NH])
        P.op("act", lambda e: e.activation(nea[:], alog[:], AF.Exp), reads=[t_c], writes=[t_c])
        P.op("dve", lambda e: e.tensor_scalar_mul(nea[:], nea[:], -1.0), reads=[t_c], writes=[t_c])
        t_m = T("masks")
        ones = sb("ones", [128, 128]); zeros = sb("zeros", [128, 128])
        TriU = sb("TriU", [128, 128]); MAs = sb("MAs", [128, 128]); MB = sb("MB", [128, 128]); MBs = sb("MBs", [128, 128])
        TriUS = sb("TriUS", [64, 64]); MAsS = sb("MAsS", [64, 64]); MBS = sb("MBS", [64, 64]); MBsS = sb("MBsS", [64, 64])
        Emat = sb("Emat", [16, 64]); SSm = sb("SSm", [64, 64]); tmpm = sb("tmpm", [64, 64])
        Msel = sb("Msel", [128, NSQ, TS]); Msel2 = sb("Msel2", [TS, NSQ, 128])
        mo = lambda fn, r=(): P.op("pool", fn, reads=[t_m] + list(r), writes=[t_m])
        mo(lambda e: e.memset(ones[:], 1.0)); mo(lambda e: e.memset(zeros[:], 0.0))
        mo(lambda e: e.affine_select(out=TriU[:], in_=ones[:], pattern=[[1, 128]], compare_op=ALU.is_ge, fill=0.0,
                                     base=0, channel_multiplier=-1))
        mo(lambda e: e.affine_select(out=MAs[:], in_=zeros[:], pattern=[[-1, 128]], compare_op=ALU.is_gt, fill=BIG,
                                     base=0, channel_multiplier=1))
        mo(lambda e: e.affine_select(out=MB[:], in_=zeros[:], pattern=[[1, 128]], compare_op=ALU.is_ge, fill=-BIG,
                                     base=0, channel_multiplier=-1))
        mo(lambda e: e.affine_select(out=MBs[:], in_=zeros[:], pattern=[[1, 128]], compare_op=ALU.is_gt, fill=-BIG,
                                     base=0, channel_multiplier=-1))
        mo(lambda e: e.affine_select(out=Emat[:], in_=ones[0:16, 0:64], pattern=[[1, 64]], compare_op=ALU.is_ge, fill=0.0,
                                     base=0, channel_multiplier=-4))
        mo(lambda e: e.affine_select(out=Emat[:], in_=Emat[:], pattern=[[-1, 64]], compare_op=ALU.is_ge, fill=0.0,
                                     base=3, channel_multiplier=4))
        mo(lambda e: e.memset(Msel[:], 1.0)); mo(lambda e: e.memset(Msel2[:], 1.0))
        mo(lambda e: e.affine_select(out=Msel[:], in_=Msel[:], pattern=[[-4, NSQ], [1, TS]], compare_op=ALU.is_ge,
                                     fill=0.0, base=0, channel_multiplier=0))
        mo(lambda e: e.affine_select(out=Msel[:], in_=Msel[:], pattern=[[4, NSQ], [-1, TS]], compare_op=ALU.is_ge,
                                     fill=0.0, base=3, channel_multiplier=0))
        mo(lambda e: e.affine_select(out=Msel2[:], in_=Msel2[:], pattern=[[-4, NSQ], [0, 128]], compare_op=ALU.is_ge,
                                     fill=0.0, base=0, channel_multiplier=1))
        mo(lambda e: e.affine_select(out=Msel2[:], in_=Msel2[:], pattern=[[4, NSQ], [0, 128]], compare_op=ALU.is_ge,
                                     fill=0.0, base=3, channel_multiplier=-1))
        pS = [ps("pS%d" % i, [128, 512]) for i in range(4)]
        t_pS = [[Tok("pS%d" % i, excl=True)] * 4 for i in range(4)]
        pQ = [ps("pQ%d" % i, [128, 512]) for i in range(2)]
        t_pQ = [[Tok("pQ%d" % i, excl=True)] * 4 for i in range(2)]
        pM = ps("pM", [128, 512]); t_pM = Tok("pM", excl=True)
        pN = ps("pN", [128, 512]); t_pN = Tok("pN", excl=True)
        P.op("pe", lambda e: e.matmul(pM[0:64, 0:64], Emat[:], Emat[:], start=True, stop=True), reads=[t_m], writes=[t_pM])
        P.op("dve", lambda e: e.tensor_copy(SSm[:], pM[0:64, 0:64]), reads=[t_pM], writes=[t_m])
        P.op("dve", lambda e: e.tensor_scalar(tmpm[:], SSm[:], -BIG, BIG, ALU.mult, ALU.add), reads=[t_m], writes=[t_m])
        P.op("dve", lambda e: e.tensor_tensor(MAsS[:], MAs[0:64, 0:64], tmpm[:], ALU.max), reads=[t_m], writes=[t_m])
        P.op("dve", lambda e: e.tensor_scalar_mul(tmpm[:], tmpm[:], -1.0), reads=[t_m], writes=[t_m])
        P.op("dve", lambda e: e.tensor_tensor(MBS[:], MB[0:64, 0:64], tmpm[:], ALU.min), reads=[t_m], writes=[t_m])
        P.op("dve", lambda e: e.tensor_tensor(MBsS[:], MBs[0:64, 0:64], tmpm[:], ALU.min), reads=[t_m], writes=[t_m])
        P.op("dve", lambda e: e.tensor_tensor(TriUS[:], TriU[0:64, 0:64], SSm[:], ALU.mult), reads=[t_m], writes=[t_m])

        NTL = 17
        t_g = T("gates")
        tmpa = sb("tmpa", [128, NTL, NH]); GK = sb("GK", [128, NTL, NH]); BETA = sb("BETA", [128, NTL, NH])
        GC = sb("GC", [128, NTL, NH]); EG = sb("EG", [128, NTL, NH]); BEG = sb("BEG", [128, NTL, NH]); NB = sb("NB", [128, NTL, NH])
        bdk_all = list(k.t_BDK)
        P.op("sp", lambda e: e.dma_start(out=k.BDK[:, 0:16, :], in_=k.BDG.rearrange("(t p) c -> p t c", p=128)),
             reads=[k.t_PROJT], writes=bdk_all, dma=True, semkey=bdk_all[0])
        P.op("sp", lambda e: e.dma_start(out=k.BDK[0:TS, 16, :], in_=k.BDS), reads=[k.t_BDS], writes=bdk_all,
             dma=True, semkey=bdk_all[0], nowaw=True)
        dtb_b = dtb[:].unsqueeze(1).to_broadcast([128, NTL, NH])
        nea_b = nea[:].unsqueeze(1).to_broadcast([128, NTL, NH])
        P.op("dve", lambda e: e.tensor_tensor(tmpa[:], k.BDK[:, :, 8:16], dtb_b, ALU.add), reads=bdk_all + [t_c], writes=[t_g])
        P.op("act", lambda e: e.activation(tmpa[:], tmpa[:], AF.Exp), reads=[t_g], writes=[t_g])
        P.op("act", lambda e: e.activation(tmpa[:], tmpa[:], AF.Ln, bias=1.0, scale=1.0), reads=[t_g], writes=[t_g])
        P.op("dve", lambda e: e.tensor_tensor(GK[:], tmpa[:], nea_b, ALU.mult), reads=[t_g, t_c], writes=[t_g])
        P.op("act", lambda e: e.activation(BETA[:], k.BDK[:, :, 0:8], AF.Sigmoid), reads=bdk_all + [t_g], writes=[t_g])
        for tl in range(16):
            P.op("pe", lambda e, tl=tl: e.matmul(pM[:, tl * 8:(tl + 1) * 8], TriU[:], GK[:, tl, :], start=True, stop=True),
                 reads=[t_m, t_g], writes=[t_pM])
        P.op("pe", lambda e: e.matmul(pM[0:64, 128:136], TriUS[:], GK[0:64, 16, :], start=True, stop=True),
             reads=[t_m, t_g], writes=[t_pM])
        P.op("pe", lambda e: e.matmul(pM[0:64, 136:144], SSm[:], GK[0:64, 16, :], start=True, stop=True),
             reads=[t_m, t_g], writes=[t_pM])
        GLS = sb("GLS", [64, NH])
        P.op("dve", lambda e: e.tensor_copy(GLS[:], pM[0:64, 136:144]), reads=[t_pM, t_g], writes=[t_g])
        P.op("pool", lambda e: e.memset(GC[:], 0.0), reads=[t_g], writes=[t_g])
        P.op("dve", lambda e: e.tensor_copy(GC[:, 0:16, :].rearrange("p a b -> p (a b)"), pM[:, 0:128]), reads=[t_pM, t_g], writes=[t_g])
        P.op("dve", lambda e: e.tensor_copy(GC[0:64, 16, :], pM[0:64, 128:136]), reads=[t_pM, t_g], writes=[t_g])
        NGC = sb("NGC", [128, NTL, NH])
        P.op("dve", lambda e: e.tensor_scalar_mul(NGC[:], GC[:], -1.0), reads=[t_g], writes=[t_g])
        P.op("act", lambda e: e.activation(EG[:], GC[:], AF.Exp), reads=[t_g], writes=[t_g])
        P.op("dve", lambda e: e.tensor_tensor(BEG[:], BETA[:], EG[:], ALU.mult), reads=[t_g], writes=[t_g])
        P.op("dve", lambda e: e.tensor_scalar_mul(NB[:], BETA[:], -1.0), reads=[t_g], writes=[t_g])

        import os
        GSTOP = int(os.environ.get("GDN_STOP", "99"))
        if GSTOP <= 1:
            return
        HM = NH // 2
        gt_all = {"GK": GK, "BETA": BETA, "GC": GC, "NGC": NGC, "NB": NB, "BEG": BEG}
        gt_m = {}
        for nm_, tl_ in gt_all.items():
            tm_ = sb("m_" + nm_, [128, NTL, HM])
            gt_m[nm_] = tm_
            P.op("dve", lambda e, tm_=tm_, tl_=tl_: e.tensor_scalar_mul(tm_[:], tl_[:, :, 0:HM], k.sel_sb[:, 0:1]),
                 reads=[t_g, k.t_sel], writes=[t_g])
            P.op("dve", lambda e, tm_=tm_, tl_=tl_: e.scalar_tensor_tensor(out=tm_[:], in0=tl_[:, :, HM:NH], scalar=k.sel_sb[:, 1:2],
                                                                           in1=tm_[:], op0=ALU.mult, op1=ALU.add),
                 reads=[t_g, k.t_sel], writes=[t_g])
        convwm = sb("convwm", [128, 3 * HM * 4])
        cv4 = convw[:].rearrange("p (x h j) -> p x h j", x=3, h=NH)
        cm4 = convwm[:].rearrange("p (x h j) -> p x h j", x=3, h=HM)
        for x in range(3):
            P.op("dve", lambda e, x=x: e.tensor_scalar_mul(cm4[:, x], cv4[:, x, 0:HM, :], k.sel_sb[:, 0:1]),
                 reads=[t_c, k.t_sel], writes=[t_c])
            P.op("dve", lambda e, x=x: e.scalar_tensor_tensor(out=cm4[:, x], in0=cv4[:, x, HM:NH, :], scalar=k.sel_sb[:, 1:2],
                                                              in1=cm4[:, x], op0=ALU.mult, op1=ALU.add),
                 reads=[t_c, k.t_sel], writes=[t_c])
        XP = sb("XP", [128, 3 + NT]); t_XP = T("XP")
        XS = sb("XS", [128, NSQ, 7]); t_XS = T("XS")
        CV = [sb("CV%d" % i, [128, NT]) for i in range(3)]; t_CV = [T("CV%d" % i) for i in range(3)]
        SG = sb("SG", [128, NT]); t_SG = T("SG")
        RN = [sb("RN%d" % i, [128, 512]) for i in range(2)]; t_RN = [T("RN%d" % i) for i in range(2)]
        YG = sb("YG", [128, NT], BF16); t_YG = T("YG")
        P.op("pool", lambda e: e.memset(XP[:, 0:3], 0.0), writes=[t_XP])
        WKT = sb("WKT", [128, NCH, 128]); QGT = sb("QGT", [128, NCH, 128]); ATT = sb("ATT", [128, NCH, 128])
        KD = sb("KD", [128, NCH, 128]); UU = sb("UU", [128, NCH, 128]); GLA = sb("GLA", [128, NCH])
        t_co = [T("co%d" % c) for c in range(NCH)]
        NSL = 4
        wk = [{n: sb("w%d_%s" % (sl, n), [128, 128]) for n in
               ("gmat", "bmat", "gd", "tA", "tB", "tC", "Ds", "DsT", "DT", "W", "Na", "Nb", "NTa", "NTb", "TTa", "TTb", "kbg", "vb", "EGR")}
              for sl in range(NSL)]
        wc = [sb("wc%d" % sl, [128, 8]) for sl in range(NSL)]
        t_wk = [{n: T("w%d_%s" % (sl, n)) for n in list(wk[0].keys()) + ["wc"]} for sl in range(NSL)]
        Sst = sb("Sst", [128, 128]); t_S = T("S")
        VN = sb("VN", [128, 128]); t_VN = T("VN")
        ON = sb("ON", [128, 128]); t_ON = T("ON")
        sq = sb("sqs", [128, 8]); t_sq = T("sq"); junk = sb("junk", [128, 128]); t_junk = T("junk")
        Sall = sb("Sall", [128, NSQ, 128]); t_Sall = T("Sall")
        Snew = sb("Snew", [128, NSQ, 128]); t_Snew = T("Snew")
        WKTm = sb("WKTm", [128, NSQ, TS]); QGTm = sb("QGTm", [128, NSQ, TS]); KDm = sb("KDm", [TS, NSQ, 128]); t_mk = T("mk")
        cntq = {"q": 0, "rn": 0}

        def chunk_pre(gt, hi, c, sl):
            n = 128 if c < 16 else TS
            t0 = c * 128
            w = wk[sl]; tw = t_wk[sl]; pq = pS[sl]; tq = t_pS[sl]
            QH, KH, VS = CV[0], CV[1], CV[2]
            qc, kc, vc = QH[:, t0:t0 + n], KH[:, t0:t0 + n], VS[:, t0:t0 + n]
            if c < 16:
                tri, mas, mb, mbs = TriU[:], MAs[:], MB[:], MBs[:]
            else:
                tri, mas, mb, mbs = TriUS[:], MAsS[:], MBS[:], MBsS[:]
            gcol = gt["GC"][0:n, c, hi:hi + 1]
            cols = wc[sl]
            steps = []
            def s0():
                P.op("dve", lambda e: e.tensor_scalar_mul(w["gmat"][0:n, :], ones[0:n, :], gt["GK"][0:n, c, hi:hi + 1]),
                     reads=[t_g, t_m], writes=[tw["gmat"]])
                P.op("dve", lambda e: e.tensor_scalar_mul(w["bmat"][0:n, 0:n], ones[0:n, 0:n], gt["BETA"][0:n, c, hi:hi + 1]),
                     reads=[t_g, t_m], writes=[tw["bmat"]])
                P.op("pe", lambda e: e.matmul(pq[:, 0:n], w["gmat"][0:n, :], tri, start=True, stop=True),
                     reads=[tw["gmat"], t_m], writes=[tq[0]])
                P.op("pe", lambda e: e.matmul(pq[0:n, 128:128 + n], w["bmat"][0:n, 0:n], k.ident[0:n, 0:n], start=True, stop=True),
                     reads=[tw["bmat"], k.t_ident], writes=[tq[1]])
                P.op("pe", lambda e: e.matmul(pq[0:n, 256:256 + n], kc, kc, start=True, stop=True),
                     reads=[t_CV[1]], writes=[tq[2]])
                P.op("pe", lambda e: e.matmul(pq[0:n, 384:384 + n], kc, qc, start=True, stop=True),
                     reads=[t_CV[1], t_CV[0]], writes=[tq[3]])
            steps.append(s0)
            def s1():
                _k = [0]; _lim = int(os.environ.get('S1_OPS', '99'))
                GB = pq[0:n, 0:n]
                ngcol = gt["NGC"][0:n, c, hi:hi + 1]
                _k[0] += 1
                if _k[0] > _lim: return
                P.op("dve", lambda e: e.tensor_scalar_add(w["gd"][0:n, 0:n], GB, ngcol),
                     reads=[tq[0], t_g], writes=[tw["gd"]])
                P.op("dve", lambda e: e.tensor_tensor(w["tA"][0:n, 0:n], w["gd"][0:n, 0:n], mas, ALU.max),
                     reads=[tw["gd"], t_m], writes=[tw["tA"]])
                P.op("act", lambda e: e.activation(w["Ds"][0:n, 0:n], w["tA"][0:n, 0:n], AF.Exp, scale=-1.0),
                     reads=[tw["tA"]], writes=[tw["Ds"]])
                _k[0] += 1
                if _k[0] > _lim: return
                P.op("dve", lambda e: e.tensor_tensor(w["tB"][0:n, 0:n], w["gd"][0:n, 0:n], mbs, ALU.min),
                     reads=[tw["gd"], t_m], writes=[tw["tB"]])
                P.op("act", lambda e: e.activation(w["DsT"][0:n, 0:n], w["tB"][0:n, 0:n], AF.Exp),
                     reads=[tw["tB"]], writes=[tw["DsT"]])
                P.op("dve", lambda e: e.tensor_tensor(w["tC"][0:n, 0:n], w["gd"][0:n, 0:n], mb, ALU.min),
                     reads=[tw["gd"], t_m], writes=[tw["tC"]])
                P.op("act", lambda e: e.activation(w["DT"][0:n, 0:n], w["tC"][0:n, 0:n], AF.Exp),
                     reads=[tw["tC"]], writes=[tw["DT"]])
                _k[0] += 1
                if _k[0] > _lim: return
                P.op("act", lambda e: e.activation(w["EGR"][:, 0:n], pq[:, 0:n], AF.Exp), reads=[tq[0]], writes=[tw["EGR"]])
                if c < 16:
                    P.op("dve", lambda e: e.tensor_copy(cols[:, 0:1], pq[:, n - 1:n]), reads=[tq[0]], writes=[tw["wc"]])
                else:
                    P.op("dve", lambda e: e.tensor_copy(cols[0:n, 0:1], GLS[:, hi:hi + 1]), reads=[t_g], writes=[tw["wc"]])
                _k[0] += 1
                if _k[0] > _lim: return
                P.op("act", lambda e: e.activation(cols[0:n, 1:2], gcol, AF.Exp, bias=cols[0:n, 0:1], scale=-1.0),
                     reads=[tw["wc"], t_g], writes=[tw["wc"]])
                _k[0] += 1
                if _k[0] > _lim: return
                P.op("dve", lambda e: e.tensor_copy(GLA[:, c:c + 1], w["EGR"][:, n - 1:n]), reads=[tw["EGR"]], writes=[t_co[c]])
                _k[0] += 1
                if _k[0] > _lim: return
                P.op("dve", lambda e: e.scalar_tensor_tensor(out=w["Na"][0:n, 0:n], in0=pq[0:n, 256:256 + n], scalar=gt["NB"][0:n, c, hi:hi + 1],
                                                             in1=w["Ds"][0:n, 0:n], op0=ALU.mult, op1=ALU.mult),
                     reads=[tq[2], t_g, tw["Ds"]], writes=[tw["Na"]])
                _k[0] += 1
                if _k[0] > _lim: return
                P.op("dve", lambda e: e.scalar_tensor_tensor(out=w["W"][0:n, 0:n], in0=pq[0:n, 128:128 + n], scalar=-1.0,
                                                             in1=w["DsT"][0:n, 0:n], op0=ALU.mult, op1=ALU.mult),
                     reads=[tq[1], tw["DsT"]], writes=[tw["W"]])
                _k[0] += 1
                if _k[0] > _lim: return
                P.op("dve", lambda e: e.tensor_tensor(w["NTa"][0:n, 0:n], pq[0:n, 256:256 + n], w["W"][0:n, 0:n], ALU.mult),
                     reads=[tq[2], tw["W"]], writes=[tw["NTa"]])
                _k[0] += 1
                if _k[0] > _lim: return
                P.op("dve", lambda e: e.tensor_tensor(ATT[0:n, c, 0:n], pq[0:n, 384:384 + n], w["DT"][0:n, 0:n], ALU.mult),
                     reads=[tq[3], tw["DT"]], writes=[t_co[c]])
                _k[0] += 1
                if _k[0] > _lim: return
                P.op("pool", lambda e: e.tensor_tensor(w["TTa"][0:n, 0:n], w["NTa"][0:n, 0:n], k.ident[0:n, 0:n], ALU.add),
                     reads=[tw["NTa"], k.t_ident], writes=[tw["TTa"]])
                _k[0] += 1
                if _k[0] > _lim: return
                P.op("pool", lambda e: e.tensor_tensor(QGT[:, c, 0:n], qc, w["EGR"][:, 0:n], ALU.mult),
                     reads=[t_CV[0], tw["EGR"]], writes=[t_co[c]])
            steps.append(s1)
            def s2():
                P.op("pe", lambda e: e.matmul(pq[0:n, 0:128], kc, k.ident[:], start=True, stop=True),
                     reads=[t_CV[1], k.t_ident], writes=[tq[0]])
                P.op("pe", lambda e: e.matmul(pq[0:n, 128:256], vc, k.ident[:], start=True, stop=True),
                     reads=[t_CV[2], k.t_ident], writes=[tq[1]])
                P.op("dve", lambda e: e.tensor_scalar_mul(w["kbg"][0:n, :], pq[0:n, 0:128], gt["BEG"][0:n, c, hi:hi + 1]),
                     reads=[tq[0], t_g], writes=[tw["kbg"]])
                P.op("dve", lambda e: e.tensor_scalar_mul(KD[0:n, c, :], pq[0:n, 0:128], cols[0:n, 1:2]),
                     reads=[tq[0], tw["wc"]], writes=[t_co[c]])
                P.op("dve", lambda e: e.tensor_scalar_mul(w["vb"][0:n, :], pq[0:n, 128:256], gt["BETA"][0:n, c, hi:hi + 1]),
                     reads=[tq[1], t_g], writes=[tw["vb"]])
            steps.append(s2)
            L = 6 if c < 16 else 1
            names = [("Na", "NTa", "TTa"), ("Nb", "NTb", "TTb")]
            for lv in range(1, L + 1):
                def sl_a(lv=lv):
                    pn, pnt, ptt = names[(lv - 1) % 2]
                    cn_, cnt_, ctt = names[lv % 2]
                    P.op("pe", lambda e: e.matmul(pq[0:n, 256:256 + n], w[pnt][0:n, 0:n], w[pn][0:n, 0:n], start=True, stop=True),
                         reads=[tw[pn], tw[pnt]], writes=[tq[2]])
                    if lv < L:
                        P.op("pe", lambda e: e.matmul(pq[0:n, 384:384 + n], w[pn][0:n, 0:n], w[pnt][0:n, 0:n], start=True, stop=True),
                             reads=[tw[pn], tw[pnt]], writes=[tq[3]])
                    P.op("act", lambda e: e.copy(w[cn_][0:n, 0:n], pq[0:n, 256:256 + n]), reads=[tq[2]], writes=[tw[cn_]])
                    if lv < L:
                        P.op("dve", lambda e: e.tensor_copy(w[cnt_][0:n, 0:n], pq[0:n, 384:384 + n]), reads=[tq[3]], writes=[tw[cnt_]])
                def sl_b(lv=lv):
                    pn, pnt, ptt = names[(lv - 1) % 2]
                    cn_, cnt_, ctt = names[lv % 2]
                    P.op("pe", lambda e: e.matmul(pq[0:n, 0:n], w[cn_][0:n, 0:n], w[ptt][0:n, 0:n], start=True, stop=True),
                         reads=[tw[cn_], tw[ptt]], writes=[tq[0]])
                    P.op("dve", lambda e: e.tensor_tensor(w[ctt][0:n, 0:n], w[ptt][0:n, 0:n], pq[0:n, 0:n], ALU.add),
                         reads=[tq[0], tw[ptt]], writes=[tw[ctt]])
                steps.append(sl_a)
                steps.append(sl_b)
            def sf():
                ftt = names[L % 2][2]
                P.op("pe", lambda e: e.matmul(pq[0:n, 128:256], w[ftt][0:n, 0:n], w["vb"][0:n, :], start=True, stop=True),
                     reads=[tw[ftt], tw["vb"]], writes=[tq[1]])
                P.op("pe", lambda e: e.matmul(pq[:, 256:256 + n], w["kbg"][0:n, :], w[ftt][0:n, 0:n], start=True, stop=True),
                     reads=[tw[ftt], tw["kbg"]], writes=[tq[2]])
                P.op("act", lambda e: e.copy(UU[0:n, c, :], pq[0:n, 128:256]), reads=[tq[1]], writes=[t_co[c]])
                P.op("dve", lambda e: e.tensor_copy(WKT[:, c, 0:n], pq[:, 256:256 + n]), reads=[tq[2]], writes=[t_co[c]])
            steps.append(sf)
            return steps

        def out_norm(h, c, n, opsum, t_op):
            t0 = c * 128
            P.op("act", lambda e: e.activation(junk[0:n, :], opsum, AF.Square, accum_out=sq[0:n, 0:1]),
                 reads=[t_op], writes=[t_junk, t_sq])
            P.op("act", lambda e: e.activation(sq[0:n, 1:2], sq[0:n, 0:1], AF.Sqrt, bias=k.eps_nm[0:n, :], scale=1.0 / 128),
                 reads=[t_sq, k.t_eps], writes=[t_sq])
            P.op("dve", lambda e: e.reciprocal(sq[0:n, 2:3], sq[0:n, 1:2]), reads=[t_sq], writes=[t_sq])
            P.op("dve", lambda e: e.tensor_scalar_mul(ON[0:n, :], opsum, sq[0:n, 2:3]), reads=[t_op, t_sq], writes=[t_ON])
            P.op("pe", lambda e: e.matmul(pN[:, 0:n], ON[0:n, :], k.ident[0:n, 0:n], start=True, stop=True),
                 reads=[t_ON, k.t_ident], writes=[t_pN])
            P.op("dve", lambda e: e.scalar_tensor_tensor(out=YG[:, t0:t0 + n], in0=pN[:, 0:n], scalar=normw[:, 0:1],
                                                         in1=SG[:, t0:t0 + n], op0=ALU.mult, op1=ALU.mult),
                 reads=[t_pN, t_c, t_SG], writes=[t_YG])

        def l2norm_qk(cks):
            for x, scale in ((0, 128.0 ** -0.5), (1, 1.0)):
                SQ = XP[:, 3:3 + NT]
                c_lo, c_hi = cks[0][0], cks[-1][0] + cks[-1][1]
                P.op("act", lambda e, x=x, SQ=SQ, c_lo=c_lo, c_hi=c_hi: e.activation(SQ[:, c_lo:c_hi], CV[x][:, c_lo:c_hi], AF.Square),
                     reads=[t_CV[x]], writes=[t_XP])
                for (c0, cn) in cks:
                    ri = cntq["rn"] % 2
                    cntq["rn"] += 1
                    P.op("pe", lambda e, SQ=SQ, c0=c0, cn=cn: e.matmul(pN[:, 0:cn], ones[:], SQ[:, c0:c0 + cn], start=True, stop=True),
                         reads=[t_XP, t_m], writes=[t_pN])
                    P.op("act", lambda e, ri=ri, cn=cn: e.activation(RN[ri][:, 0:cn], pN[:, 0:cn], AF.Sqrt, bias=k.eps_nm[:, :], scale=1.0),
                         reads=[t_pN, k.t_eps], writes=[t_RN[ri]])
                    P.op("dve", lambda e, ri=ri, cn=cn: e.reciprocal(RN[ri][:, 0:cn], RN[ri][:, 0:cn]), reads=[t_RN[ri]], writes=[t_RN[ri]])
                    P.op("dve", lambda e, x=x, ri=ri, c0=c0, cn=cn, scale=scale: e.scalar_tensor_tensor(
                        out=CV[x][:, c0:c0 + cn], in0=CV[x][:, c0:c0 + cn], scalar=scale, in1=RN[ri][:, 0:cn],
                        op0=ALU.mult, op1=ALU.mult), reads=[t_CV[x], t_RN[ri]], writes=[t_CV[x]])

        def run_chunks(gt, hi, cs, g0):
            plans = [chunk_pre(gt, hi, c, c - g0) for c in cs]
            for si in range(max(len(p) for p in plans)):
                for p in plans:
                    if si < len(p):
                        p[si]()

        s0c, s1c = k.sel_sb[:, 0:1], k.sel_sb[:, 1:2]
        for m in range(HM):
            for x in range(3):
                rowA = 1024 + x * 1024 + m * 128
                rowB = rowA + HM * 128
                P.op("sp", lambda e, rowA=rowA: e.dma_start(out=XP[:, 3:3 + TP].rearrange("p (h t) -> p h t", h=2),
                                                            in_=pg_rows(k, rowA, 128)),
                     reads=[k.t_PROJT], writes=[t_XP], dma=True)
                P.op("sp", lambda e, rowB=rowB, x=x: e.dma_start(out=CV[x][:, 0:TP].rearrange("p (h t) -> p h t", h=2),
                                                                 in_=pg_rows(k, rowB, 128)),
                     reads=[k.t_PROJT], writes=[t_CV[x]], dma=True)
                P.op("dve", lambda e: e.tensor_scalar_mul(XP[:, 3:3 + TP], XP[:, 3:3 + TP], s0c), reads=[k.t_sel], writes=[t_XP])
                P.op("dve", lambda e, x=x: e.scalar_tensor_tensor(out=XP[:, 3:3 + TP], in0=CV[x][:, 0:TP], scalar=s1c, in1=XP[:, 3:3 + TP],
                                                                  op0=ALU.mult, op1=ALU.add),
                     reads=[k.t_sel, t_CV[x]], writes=[t_XP])
                cw = lambda j, x=x, m=m: convwm[:, (x * HM + m) * 4 + j:(x * HM + m) * 4 + j + 1]
                cvp = CV[x][:, 0:TP]
                P.op("dve", lambda e, cvp=cvp, cw=cw: e.tensor_scalar_mul(cvp, XP[:, 0:TP], cw(0)),
                     reads=[t_XP, t_c], writes=[t_CV[x]])
                for j in range(1, 4):
                    P.op("dve", lambda e, cvp=cvp, cw=cw, j=j: e.scalar_tensor_tensor(
                        out=cvp, in0=XP[:, j:j + TP], scalar=cw(j), in1=cvp, op0=ALU.mult, op1=ALU.add),
                        reads=[t_XP, t_c, t_CV[x]], writes=[t_CV[x]])
                P.op("act", lambda e, x=x: e.activation(CV[x][:, 0:TP], CV[x][:, 0:TP], AF.Silu), reads=[t_CV[x]], writes=[t_CV[x]])
            gA = 4096 + m * 128
            P.op("sp", lambda e, gA=gA: e.dma_start(out=SG[:, 0:TP].rearrange("p (h t) -> p h t", h=2), in_=pg_rows(k, gA, 128)),
                 reads=[k.t_PROJT], writes=[t_SG], dma=True)
            P.op("sp", lambda e, gA=gA: e.dma_start(out=XP[:, 3:3 + TP].rearrange("p (h t) -> p h t", h=2),
                                                    in_=pg_rows(k, gA + HM * 128, 128)),
                 reads=[k.t_PROJT], writes=[t_XP], dma=True)
            P.op("dve", lambda e: e.tensor_scalar_mul(SG[:, 0:TP], SG[:, 0:TP], s0c), reads=[k.t_sel], writes=[t_SG])
            P.op("dve", lambda e: e.scalar_tensor_tensor(out=SG[:, 0:TP], in0=XP[:, 3:3 + TP], scalar=s1c, in1=SG[:, 0:TP],
                                                         op0=ALU.mult, op1=ALU.add),
                 reads=[k.t_sel, t_XP], writes=[t_SG])
            l2norm_qk([(i * 512, 512) for i in range(TP // 512)])
            P.op("pool", lambda e: e.memset(Sst[:], 0.0), writes=[t_S])
            def pre_steps(g0):
                plans = [chunk_pre(gt_m, m, c, c - g0) for c in range(g0, g0 + NSL)]
                out = []
                for si in range(max(len(p) for p in plans)):
                    for p in plans:
                        if si < len(p):
                            out.append(p[si])
                return out

            def seq_chunk(c):
                def f():
                    qi = cntq["q"] % 2
                    cntq["q"] += 1
                    pq, tq = pQ[qi], t_pQ[qi]
                    P.op("pe", lambda e, pq=pq, c=c: e.matmul(pq[:, 0:128], WKT[:, c, :], Sst[:], start=True, stop=True),
                         reads=[t_co[c], t_S], writes=[tq[0]])
                    P.op("dve", lambda e, pq=pq, c=c: e.tensor_tensor(VN[:], UU[:, c, :], pq[:, 0:128], ALU.subtract),
                         reads=[t_co[c], tq[0]], writes=[t_VN])
                    P.op("pe", lambda e, pq=pq, c=c: e.matmul(pq[:, 128:256], QGT[:, c, :], Sst[:], start=True, stop=False),
                         reads=[t_co[c], t_S], writes=[tq[1]])
                    P.op("pe", lambda e, pq=pq, c=c: e.matmul(pq[:, 128:256], ATT[:, c, :], VN[:], start=False, stop=True),
                         reads=[t_co[c], t_VN], writes=[tq[1]])
                    P.op("pe", lambda e, pq=pq, c=c: e.matmul(pq[:, 256:384], KD[:, c, :], VN[:], start=True, stop=True),
                         reads=[t_co[c], t_VN], writes=[tq[2]])
                    P.op("dve", lambda e, pq=pq, c=c: e.scalar_tensor_tensor(
                        out=Sst[:], in0=Sst[:], scalar=GLA[:, c:c + 1], in1=pq[:, 256:384], op0=ALU.mult, op1=ALU.add),
                        reads=[t_S, t_co[c], tq[2]], writes=[t_S])
                    out_norm(m, c, 128, pq[:, 128:256], tq[1])
                return f

            for st_ in pre_steps(0):
                st_()
            for g0 in range(0, 16, NSL):
                nxt = pre_steps(g0 + NSL) if g0 + NSL < 16 else []
                seqs = [seq_chunk(c) for c in range(g0, g0 + NSL)]
                if nxt:
                    per = -(-len(nxt) // len(seqs))
                    for qi_, sq_ in enumerate(seqs):
                        for st_ in nxt[qi_ * per:(qi_ + 1) * per]:
                            st_()
                        sq_()
                else:
                    for sq_ in seqs:
                        sq_()
            o = P.op("sp", lambda e, m=m: e.dma_start(out=k.p_gdn[m * 128:(m + 1) * 128, :], in_=Sst[:]),
                     reads=[t_S], dma=True, semkey=t_S)
            P.final.append(o)
            P.op("sp", lambda e, m=m: e.dma_start(out=k.YTM[m * 128:(m + 1) * 128, :], in_=YG[:, 0:TP]),
                 reads=[t_YG], writes=[k.t_YTM], dma=True, semkey=t_YG, nowaw=True)
        rg = [[2 * i, 2 * i + 1] for i in range(N_CORES // 2)]
        for j in range(2):
            P.op("pool", lambda e, j=j: e.collective_compute(
                "AllGather", ALU.bypass, replica_groups=rg,
                ins=[k.YTM[j * 256:(j + 1) * 256, :]], outs=[k.YTG[j * 512:(j + 1) * 512, :]]),
                reads=[k.t_YTM], writes=[k.t_YTG], nowaw=True, cc=True, semkey=k.t_YTG)

        SC = 16
        for h in range(NH):
            for x in range(3):
                row0 = 1024 + x * 1024 + h * 128
                ch0 = x * 1024 + h * 128
                P.op("sp", lambda e, row0=row0: e.dma_start(
                    out=XS[:, :, 3:7], in_=k.PROJS[row0:row0 + 128, :].rearrange("p (b t) -> p b t", t=4)),
                    reads=[k.t_PROJS], writes=[t_XS], dma=True)
                P.op("sp", lambda e, ch0=ch0: e.dma_start(
                    out=XS[:, :, 0:3], in_=k.conv_in[ch0:ch0 + 128, :, :]), writes=[t_XS], dma=True)
                cw = lambda j, x=x, h=h: convw[:, (x * 8 + h) * 4 + j:(x * 8 + h) * 4 + j + 1]
                cvs = CV[x][:, TP:NT].rearrange("p (b t) -> p b t", t=4)
                P.op("dve", lambda e, cvs=cvs, cw=cw: e.tensor_scalar_mul(cvs, XS[:, :, 0:4], cw(0)),
                     reads=[t_XS, t_c], writes=[t_CV[x]])
                for j in range(1, 4):
                    P.op("dve", lambda e, cvs=cvs, cw=cw, j=j: e.scalar_tensor_tensor(
                        out=cvs, in0=XS[:, :, j:j + 4], scalar=cw(j), in1=cvs, op0=ALU.mult, op1=ALU.add),
                        reads=[t_XS, t_c, t_CV[x]], writes=[t_CV[x]])
                P.op("act", lambda e, x=x: e.activation(CV[x][:, TP:NT], CV[x][:, TP:NT], AF.Silu), reads=[t_CV[x]], writes=[t_CV[x]])
            P.op("sp", lambda e, h=h: e.dma_start(out=SG[:, TP:NT], in_=k.PROJS[4096 + h * 128:4096 + (h + 1) * 128, :]),
                 reads=[k.t_PROJS], writes=[t_SG], dma=True)
            l2norm_qk([(TP, TS)])
            P.op("sp", lambda e, h=h: e.dma_start(
                out=Sall[:], in_=k.gdn_in.rearrange("(b hh d) v -> hh d b v", hh=NH, d=128)[h]),
                writes=[t_Sall], dma=True)
            run_chunks(gt_all, h, [SC], SC)
            c = SC
            n = TS
            qi = cntq["q"] % 2
            cntq["q"] += 1
            pq, tq = pQ[qi], t_pQ[qi]
            wkb = WKT[:, c, 0:n].unsqueeze(1).to_broadcast([128, NSQ, n])
            qgb = QGT[:, c, 0:n].unsqueeze(1).to_broadcast([128, NSQ, n])
            kdb = KD[0:n, c, :].unsqueeze(1).to_broadcast([n, NSQ, 128])
            P.op("dve", lambda e, wkb=wkb: e.tensor_tensor(WKTm[:], Msel[:], wkb, ALU.mult), reads=[t_co[c], t_m], writes=[t_mk])
            P.op("pool", lambda e, qgb=qgb: e.tensor_tensor(QGTm[:], Msel[:], qgb, ALU.mult), reads=[t_co[c], t_m], writes=[t_mk], nowaw=True)
            P.op("pool", lambda e, kdb=kdb: e.tensor_tensor(KDm[:], Msel2[:], kdb, ALU.mult), reads=[t_co[c], t_m], writes=[t_mk], nowaw=True)
            for b in range(NSQ):
                P.op("pe", lambda e, pq=pq, b=b: e.matmul(pq[0:TS, 0:128], WKTm[:, b, :], Sall[:, b, :],
                                                          start=(b == 0), stop=(b == NSQ - 1)),
                     reads=[t_mk, t_Sall], writes=[tq[0]])
            P.op("dve", lambda e, pq=pq: e.tensor_tensor(VN[0:TS, :], UU[0:TS, SC, :], pq[0:TS, 0:128], ALU.subtract),
                 reads=[t_co[c], tq[0]], writes=[t_VN])
            for b in range(NSQ):
                P.op("pe", lambda e, pq=pq, b=b: e.matmul(pq[0:TS, 128:256], QGTm[:, b, :], Sall[:, b, :],
                                                          start=(b == 0), stop=False),
                     reads=[t_mk, t_Sall], writes=[tq[1]])
            P.op("pe", lambda e, pq=pq: e.matmul(pq[0:TS, 128:256], ATT[0:TS, SC, 0:TS], VN[0:TS, :], start=False, stop=True),
                 reads=[t_co[c], t_VN], writes=[tq[1]])
            out_norm(h, c, TS, pq[0:TS, 128:256], tq[1])
            egl = wk[0]["EGR"]
            for b in range(NSQ):
                sl4, off = divmod(b * 128, 512)
                P.op("pe", lambda e, b=b, sl4=sl4, off=off: e.matmul(pS[sl4][:, off:off + 128], KDm[:, b, :], VN[0:TS, :],
                                                                     start=True, stop=True),
                     reads=[t_mk, t_VN], writes=[t_pS[sl4][off // 128]])
            for b in range(NSQ):
                sl4, off = divmod(b * 128, 512)
                P.op("dve", lambda e, b=b, sl4=sl4, off=off, egl=egl: e.scalar_tensor_tensor(
                    out=Snew[:, b, :], in0=Sall[:, b, :], scalar=egl[:, 4 * b + 3:4 * b + 4], in1=pS[sl4][:, off:off + 128],
                    op0=ALU.mult, op1=ALU.add),
                    reads=[t_Sall, t_wk[0]["EGR"], t_pS[sl4][off // 128]], writes=[t_Snew])
            o = P.op("sp", lambda e, h=h: e.dma_start(
                out=k.s_gdn.rearrange("(b hh d) v -> hh d b v", hh=NH, d=128)[h], in_=Snew[:]),
                reads=[t_Snew], dma=True, semkey=t_Snew)
            P.final.append(o)
            P.op("sp", lambda e, h=h: e.dma_start(out=k.YT[1024 + h * 128:1024 + (h + 1) * 128, TP:NT], in_=YG[:, TP:NT]),
                 reads=[t_YG], writes=[k.t_YT], dma=True, semkey=t_YG, nowaw=True)


def make_in_maps(inp):
    f = lambda a: np.ascontiguousarray(np.asarray(a, dtype=np.float32))
    L = 0
    lam_re = f(inp["s5_lambda_re"])[L]
    lam_im = f(inp["s5_lambda_im"])[L]
    log_dt = f(inp["s5_log_dt"])[L]
    b_re = f(inp["s5_b_re"])[L]
    b_im = f(inp["s5_b_im"])[L]
    c_re = f(inp["s5_c_re"])[L]
    c_im = f(inp["s5_c_im"])[L]

    def pairlay(a):
        return np.ascontiguousarray(a.reshape(32, 2, 64).transpose(1, 2, 0).reshape(128, 32))

    c_lamre = pairlay(lam_re)
    c_lamim = pairlay(lam_im)
    c_logdt = pairlay(np.broadcast_to(log_dt[:, None], (64, 64)))

    def blay(b):
        out = np.zeros((2, 16, 32, 2, 64), np.float32)
        bb = b.reshape(32, 2, 64, 16)
        for g2 in range(2):
            out[g2, :, :, g2, :] = bb[:, g2].transpose(2, 0, 1)
        return np.ascontiguousarray(out.reshape(32, 32 * 128))

    def clay(c):
        out = np.zeros((2, 64, 32, 2, 16), np.float32)
        cc = c.reshape(32, 2, 16, 64)
        for g2 in range(2):
            out[g2, :, :, g2, :] = cc[:, g2].transpose(2, 0, 1)
        return np.ascontiguousarray(out.reshape(128, 32 * 32))

    s5_d = f(inp["s5_d"])[L]
    c_d = np.ascontiguousarray(s5_d.reshape(32, 32).T)
    c_glub = np.ascontiguousarray(f(inp["s5_glu_b"])[L].reshape(8, 128).T)
    convw = f(inp["gdn_conv_w"])[L]
    c_convw = np.ascontiguousarray(convw.reshape(4, 24, 128).transpose(2, 1, 0).reshape(128, 96))
    c_alog = np.ascontiguousarray(np.broadcast_to(f(inp["gdn_a_log"])[L][None, :], (128, NH)))
    c_dtb = np.ascontiguousarray(np.broadcast_to(f(inp["gdn_dt_bias"])[L][None, :], (128, NH)))
    c_normw = np.ascontiguousarray(f(inp["gdn_norm_w"])[L].reshape(128, 1))
    shared = {
        "c_lamre": c_lamre, "c_lamim": c_lamim, "c_logdt": c_logdt,
        "c_bre": blay(b_re), "c_bim": blay(b_im), "c_cre": clay(c_re), "c_cim": clay(c_im),
        "c_d": c_d, "c_glub": c_glub, "c_convw": c_convw, "c_alog": c_alog, "c_dtb": c_dtb,
        "c_normw": c_normw,
        "w_mix_in": f(inp["w_mix_in"])[L], "w_mix_out": f(inp["w_mix_out"])[L],
        "s5_glu_w": f(inp["s5_glu_w"])[L],
    }
    for name in ("ln1_g", "ln1_b", "ln2_g", "ln2_b", "ln3_g", "ln3_b",
                 "ffn1_w_in", "ffn1_w_out", "ffn2_w_in", "ffn2_w_out"):
        shared[name] = f(inp[name])[L]
    xp = f(inp["x_prompt"])
    xsm = f(inp["x_sample"])
    s5re = f(inp["state_s5_re"])[L]
    s5im = f(inp["state_s5_im"])[L]
    sg = f(inp["state_gdn"])[L]
    sc = f(inp["state_conv"])[L]
    maps = []
    for c in range(N_CORES):
        m = dict(shared)
        sq_, rk = c // 2, c % 2
        xc = np.empty((NL, D), np.float32)
        xc[:TL] = xp[sq_, rk * TL:(rk + 1) * TL]
        xc[TL:] = xsm[c * NSQ:(c + 1) * NSQ].reshape(TS, D)
        m["x"] = xc
        sel = np.zeros((128, 2), np.float32)
        sel[:, rk] = 1.0
        m["sel"] = sel
        m["s5re_in"] = np.ascontiguousarray(s5re[c * NSQ:(c + 1) * NSQ].reshape(NSQ, 4096))
        m["s5im_in"] = np.ascontiguousarray(s5im[c * NSQ:(c + 1) * NSQ].reshape(NSQ, 4096))
        m["gdn_in"] = np.ascontiguousarray(sg[c * NSQ:(c + 1) * NSQ].reshape(NSQ * NH * 128, 128))
        m["conv_in"] = np.ascontiguousarray(sc[c * NSQ:(c + 1) * NSQ].transpose(2, 0, 1))
        maps.append(m)
    return maps


def unpairlay(a):
    n = a.shape[1]
    return np.ascontiguousarray(a.reshape(2, 64, n).transpose(2, 0, 1).reshape(2 * n, 64))


_CACHE = {}


def kernel(**inputs):
    maps = make_in_maps(inputs)
    if "nc" not in _CACHE:
        _CACHE["nc"] = build_program()[0]
    nc = _CACHE["nc"]
    res = run_bass_kernel_spmd(nc, maps, core_ids=list(range(N_CORES))).results
    nb = 4
    yp = np.stack([np.concatenate([res[2 * s_]["y"][:TL], res[2 * s_ + 1]["y"][:TL]]) for s_ in range(nb)])
    ys = np.concatenate([res[c]["y"][TL:].reshape(NSQ, 4, D) for c in range(N_CORES)])
    p_re = np.stack([unpairlay(np.concatenate([res[2 * c]["p_s5re"], res[2 * c + 1]["p_s5re"]], axis=1)) for c in range(nb)])[None]
    p_im = np.stack([unpairlay(np.concatenate([res[2 * c]["p_s5im"], res[2 * c + 1]["p_s5im"]], axis=1)) for c in range(nb)])[None]
    p_gdn = np.stack([np.concatenate([res[2 * c]["p_gdn"], res[2 * c + 1]["p_gdn"]]).reshape(NH, 128, 128) for c in range(nb)])[None]
    p_conv = np.stack([res[2 * c + 1]["p_conv"] for c in range(nb)])[None]
    s_re = np.concatenate([res[c]["s_s5re"].reshape(NSQ, 64, 64) for c in range(N_CORES)])[None]
    s_im = np.concatenate([res[c]["s_s5im"].reshape(NSQ, 64, 64) for c in range(N_CORES)])[None]
    s_gdn = np.concatenate([res[c]["s_gdn"].reshape(NSQ, NH, 128, 128) for c in range(N_CORES)])[None]
    s_conv = np.concatenate([res[c]["s_conv"] for c in range(N_CORES)])[None]
    outs = (yp, ys, p_re, p_im, p_gdn, p_conv, s_re, s_im, s_gdn, s_conv)
    return tuple(np.ascontiguousarray(o, dtype=np.float32) for o in outs)
```
